# Optimizing a Trainium2 kernel written in Bass

```python
import math
import jax, jax.numpy as jnp
from jax import lax
import numpy as np

D_MODEL = 1024
BATCH = 32
SEQ = 2048
DEPTH = 1

CHUNK = 64
MIX_WIDTH = D_MODEL
CONV_WIDTH = MIX_WIDTH // 2
CONV_KSIZE = 31
RW_WIDTH = MIX_WIDTH - CONV_WIDTH
RW_HEAD_DIM = 64
RW_HEADS = RW_WIDTH // RW_HEAD_DIM
LORA_W = 64
LORA_A = 64
LORA_G = 128
RW_PROJ = 3 * RW_WIDTH + LORA_W + LORA_A + LORA_G
IN_PROJ = 2 * CONV_WIDTH + RW_PROJ
PEER_HEADS = 8
PEER_NKEYS = 128
PEER_TOPK = 16
PEER_QDIM = 256
PEER_N_EXPERTS = PEER_NKEYS * PEER_NKEYS
PEER_TOKEN_BLOCK = 128
ALPHA = (2.0 * DEPTH) ** 0.25
BETA = (8.0 * DEPTH) ** -0.25
LN_EPS = 1e-5
GN_EPS = 64e-5

kernel_name = "hybrid_conv_rwkv7_peer_deepnorm_adaln"


def layer_norm(x, g=None, b=None, eps=LN_EPS):
    xf = x.astype(jnp.float32)
    mu = jnp.mean(xf, axis=-1, keepdims=True)
    var = jnp.mean(jnp.square(xf - mu), axis=-1, keepdims=True)
    y = (xf - mu) * lax.rsqrt(var + eps)
    if g is not None:
        y = y * g + b
    return y


def adaln(x, shift, scale):
    return layer_norm(x) * (1.0 + scale[:, None, :]) + shift[:, None, :]


def rwkv7_scan(r, decay, k, v, kk, a):
    bsz, _, nh, nd = r.shape
    xs = tuple(jnp.moveaxis(t.astype(jnp.float32), 1, 0) for t in (r, decay, k, v, kk, a))

    def step(state, inp):
        r_t, w_t, k_t, v_t, kk_t, a_t = inp
        sa = jnp.einsum('bhvk,bhk->bhv', state, -kk_t)
        state = (state * w_t[:, :, None, :]
                 + sa[..., None] * (kk_t * a_t)[:, :, None, :]
                 + v_t[..., None] * k_t[:, :, None, :])
        y_t = jnp.einsum('bhvk,bhk->bhv', state, r_t)
        return state, y_t

    state0 = jnp.zeros((bsz, nh, nd, nd), jnp.float32)
    _, ys = lax.scan(step, state0, xs)
    return jnp.moveaxis(ys, 0, 1)


def setup_inputs(seed: int = 0) -> dict:
    key = jax.random.key(seed)
    ks = jax.random.split(key, 40)
    f32 = jnp.float32

    def nrm(k, shape, scale):
        return jax.random.normal(k, shape, f32) * scale

    s_in = D_MODEL ** -0.5
    col_scale = s_in * jnp.concatenate([
        jnp.full((CONV_WIDTH,), BETA, f32),
        jnp.ones((CONV_WIDTH,), f32),
        jnp.ones((2 * RW_WIDTH,), f32),
        jnp.full((RW_WIDTH,), BETA, f32),
        jnp.ones((LORA_W + LORA_A + LORA_G,), f32),
    ])
    return {
        "x": nrm(ks[0], (BATCH, SEQ, D_MODEL), 1.0),
        "c": nrm(ks[1], (BATCH, D_MODEL), 1.0),
        "cond_w": nrm(ks[2], (D_MODEL, 6 * D_MODEL), 0.5 * D_MODEL ** -0.5),
        "cond_b": nrm(ks[3], (6 * D_MODEL,), 0.02),
        "w_in": jax.random.normal(ks[4], (D_MODEL, IN_PROJ), f32) * col_scale,
        "mu_shift": jax.random.uniform(ks[5], (RW_PROJ,), f32),
        "conv_w": nrm(ks[6], (CONV_KSIZE, CONV_WIDTH), CONV_KSIZE ** -0.5),
        "conv_b": nrm(ks[7], (CONV_WIDTH,), 0.02),
        "conv_ln_g": 1.0 + nrm(ks[8], (CONV_WIDTH,), 0.05),
        "conv_ln_b": nrm(ks[9], (CONV_WIDTH,), 0.02),
        "rw_w0": jnp.linspace(-7.0, -2.0, RW_WIDTH, dtype=f32) + nrm(ks[10], (RW_WIDTH,), 0.1),
        "rw_w2": nrm(ks[11], (LORA_W, RW_WIDTH), 0.1 * LORA_W ** -0.5),
        "rw_a0": nrm(ks[12], (RW_WIDTH,), 0.1),
        "rw_a2": nrm(ks[13], (LORA_A, RW_WIDTH), LORA_A ** -0.5),
        "rw_g2": nrm(ks[14], (LORA_G, RW_WIDTH), LORA_G ** -0.5),
        "rw_kk": 0.85 + nrm(ks[15], (RW_WIDTH,), 0.05),
        "rw_ka": 1.0 + nrm(ks[16], (RW_WIDTH,), 0.05),
        "rw_rk": nrm(ks[17], (RW_HEADS, RW_HEAD_DIM), 0.1),
        "rw_lnx_g": 1.0 + nrm(ks[18], (RW_WIDTH,), 0.05),
        "rw_lnx_b": nrm(ks[19], (RW_WIDTH,), 0.02),
        "w_out": nrm(ks[20], (MIX_WIDTH, D_MODEL), BETA * MIX_WIDTH ** -0.5),
        "ln1_g": 1.0 + nrm(ks[21], (D_MODEL,), 0.05),
        "ln1_b": nrm(ks[22], (D_MODEL,), 0.02),
        "peer_wq": nrm(ks[23], (D_MODEL, PEER_HEADS * PEER_QDIM), s_in),
        "peer_k1": nrm(ks[24], (PEER_HEADS, PEER_NKEYS, PEER_QDIM // 2), (PEER_QDIM // 2) ** -0.5),
        "peer_k2": nrm(ks[25], (PEER_HEADS, PEER_NKEYS, PEER_QDIM // 2), (PEER_QDIM // 2) ** -0.5),
        "peer_u": nrm(ks[26], (PEER_N_EXPERTS, D_MODEL), BETA * s_in),
        "peer_v": nrm(ks[27], (PEER_N_EXPERTS, D_MODEL), BETA),
        "ln2_g": 1.0 + nrm(ks[28], (D_MODEL,), 0.05),
        "ln2_b": nrm(ks[29], (D_MODEL,), 0.02),
    }


def reference(x, c, cond_w, cond_b, w_in, mu_shift, conv_w, conv_b, conv_ln_g, conv_ln_b,
              rw_w0, rw_w2, rw_a0, rw_a2, rw_g2, rw_kk, rw_ka, rw_rk, rw_lnx_g, rw_lnx_b,
              w_out, ln1_g, ln1_b, peer_wq, peer_k1, peer_k2, peer_u, peer_v, ln2_g, ln2_b):
    bsz, seq, dm = x.shape

    mod = jax.nn.silu(c) @ cond_w + cond_b
    shift1, scale1, gate1, shift2, scale2, gate2 = jnp.split(mod, 6, axis=-1)

    for _ in range(DEPTH):
        h = adaln(x, shift1, scale1)
        p = h @ w_in
        p_conv = p[..., :2 * CONV_WIDTH]
        p_rw = p[..., 2 * CONV_WIDTH:]

        u = p_conv[..., :CONV_WIDTH] * jax.nn.sigmoid(p_conv[..., CONV_WIDTH:])
        u = lax.conv_general_dilated(
            u, conv_w.astype(u.dtype)[:, None, :], window_strides=(1,),
            padding=[(CONV_KSIZE - 1, 0)], dimension_numbers=('NWC', 'WIO', 'NWC'),
            feature_group_count=CONV_WIDTH) + conv_b
        y_conv = jax.nn.silu(layer_norm(u, conv_ln_g, conv_ln_b))

        p_prev = jnp.pad(p_rw[:, :-1], ((0, 0), (1, 0), (0, 0)))
        xm = p_rw + (p_prev - p_rw) * mu_shift
        o = 0
        r = xm[..., o:o + RW_WIDTH]; o += RW_WIDTH
        k = xm[..., o:o + RW_WIDTH]; o += RW_WIDTH
        v = xm[..., o:o + RW_WIDTH]; o += RW_WIDTH
        wd = xm[..., o:o + LORA_W]; o += LORA_W
        ad = xm[..., o:o + LORA_A]; o += LORA_A
        gd = xm[..., o:o + LORA_G]
        w_log = -jax.nn.softplus(-(rw_w0 + jnp.tanh(wd) @ rw_w2)) - 0.5
        decay = jnp.exp(-jnp.exp(w_log.astype(jnp.float32)))
        a = jax.nn.sigmoid(rw_a0 + ad @ rw_a2)
        g = jax.nn.sigmoid(gd) @ rw_g2
        heads = lambda t: t.reshape(bsz, seq, RW_HEADS, RW_HEAD_DIM)
        kk = heads(k * rw_kk).astype(jnp.float32)
        kk = kk / jnp.maximum(jnp.linalg.norm(kk, axis=-1, keepdims=True), 1e-12)
        k = k * (1.0 + (a - 1.0) * rw_ka)
        rh, kh, vh, ah = heads(r), heads(k), heads(v), heads(a)
        y_rw = rwkv7_scan(rh, heads(decay), kh, vh, kk, ah)
        y_rw = layer_norm(y_rw, eps=GN_EPS).reshape(bsz, seq, RW_WIDTH) * rw_lnx_g + rw_lnx_b
        bonus = jnp.sum(rh * kh * rw_rk, axis=-1, keepdims=True) * vh
        y_rw = (y_rw + bonus.reshape(bsz, seq, RW_WIDTH)) * g

        y1 = jnp.concatenate([y_conv, y_rw], axis=-1) @ w_out
        x = layer_norm(ALPHA * x + gate1[:, None, :] * y1, ln1_g, ln1_b)

        h2 = adaln(x, shift2, scale2)
        q = (h2 @ peer_wq).reshape(bsz, seq, PEER_HEADS, PEER_QDIM).astype(jnp.float32)
        half = PEER_QDIM // 2
        s1 = jnp.einsum('bshd,hnd->bshn', q[..., :half], peer_k1.astype(jnp.float32))
        s2 = jnp.einsum('bshd,hnd->bshn', q[..., half:], peer_k2.astype(jnp.float32))
        v1, i1 = lax.top_k(s1, PEER_TOPK)
        v2, i2 = lax.top_k(s2, PEER_TOPK)
        cand = (v1[..., :, None] + v2[..., None, :]).reshape(bsz, seq, PEER_HEADS, PEER_TOPK * PEER_TOPK)
        sc, ci = lax.top_k(cand, PEER_TOPK)
        experts = (jnp.take_along_axis(i1, ci // PEER_TOPK, axis=-1) * PEER_NKEYS
                   + jnp.take_along_axis(i2, ci % PEER_TOPK, axis=-1))
        gates = jax.nn.softmax(sc, axis=-1)

        n_tok = bsz * seq
        nblk = n_tok // PEER_TOKEN_BLOCK
        hk = PEER_HEADS * PEER_TOPK
        h_blk = h2.reshape(nblk, PEER_TOKEN_BLOCK, dm)
        e_blk = experts.reshape(nblk, PEER_TOKEN_BLOCK, hk)
        g_blk = gates.reshape(nblk, PEER_TOKEN_BLOCK, hk)

        def peer_block(args):
            hb, eb, gb = args
            u_sel = jnp.take(peer_u, eb, axis=0)
            z = jnp.einsum('td,tkd->tk', hb, u_sel)
            act = jax.nn.gelu(z, approximate=False) * gb
            return jnp.einsum('tk,tkd->td', act, jnp.take(peer_v, eb, axis=0))

        y2 = lax.map(peer_block, (h_blk, e_blk, g_blk)).reshape(bsz, seq, dm)
        x = layer_norm(ALPHA * x + gate2[:, None, :] * y2, ln2_g, ln2_b)

    return x
```

```python
import contextlib
import numpy as np
import concourse.bass as bass
import concourse.mybir as mybir

F32 = mybir.dt.float32
BF16 = mybir.dt.bfloat16
U32 = mybir.dt.uint32
I32 = mybir.dt.int32
AF = mybir.ActivationFunctionType
ALU = mybir.AluOpType
AX = mybir.AxisListType


class Op:
    __slots__ = ("eng", "dma", "lane", "lane_val", "sig", "sigval")

    def __init__(self, eng, dma):
        self.eng = eng
        self.dma = dma
        self.lane = None
        self.lane_val = 0
        self.sig = False
        self.sigval = 0


class Prog:
    ENGS = ("pe", "act", "dve", "pool", "sp")

    def __init__(self, nc, flags=None):
        self.nc = nc
        self.dry = flags is None
        self.flags = flags
        self.stack = [contextlib.ExitStack()]
        self.lastw = {}
        self.readers = {}
        self.lanes = {}
        self.alias = {}
        self.allops = []
        self.cnt = {e: 0 for e in self.ENGS}
        self.waited = {e: {} for e in self.ENGS}
        self.engobj = {"pe": nc.tensor, "act": nc.scalar, "dve": nc.vector, "pool": nc.gpsimd, "sp": nc.sync}
        if not self.dry:
            self.K = 12
            self.esem = {e: [self.sem("E%s%d" % (e, i)) for i in range(self.K)] for e in self.ENGS if e != "sp"}

    def push(self):
        self.stack.append(contextlib.ExitStack())

    def pop(self):
        self.stack.pop().close()

    def sb(self, name, shape, dt):
        return self.stack[-1].enter_context(self.nc.sbuf_tensor(name, list(shape), dt))

    def ps(self, name, shape, dt=F32):
        return self.stack[-1].enter_context(self.nc.psum_tensor(name, list(shape), dt))

    def sem(self, name):
        return self.stack[0].enter_context(self.nc.semaphore(name))

    def op(self, eng, fn, r=(), w=(), dma=False, lane=None):
        o = Op(eng, dma)
        al = self.alias
        r = [al.get(u, u) for u in r]
        w = [al.get(u, u) for u in w]
        deps = set()
        for u in r:
            lw = self.lastw.get(u)
            if lw is not None:
                deps.add(lw)
        for u in w:
            lw = self.lastw.get(u)
            if lw is not None:
                deps.add(lw)
            for rd in self.readers.get(u, {}).values():
                deps.add(rd)
        for u in w:
            self.lastw[u] = o
            self.readers[u] = {}
        for u in r:
            if u not in w:
                self.readers.setdefault(u, {})[(eng, len(self.allops)) if dma else eng] = o
        idx = len(self.allops)
        self.allops.append(o)
        if self.dry:
            for d in deps:
                if not (d.eng == eng and eng == "pe" and not d.dma):
                    d.sig = True
            if dma:
                self.lanes.setdefault(lane, 0)
            return o
        e = self.engobj[eng]
        wd = self.waited[eng]
        for d in deps:
            if d.dma:
                s, v = d.lane, d.lane_val
            else:
                if d.eng == eng and eng == "pe":
                    continue
                s, v = d.sigval
            k = id(s)
            if wd.get(k, 0) >= v:
                continue
            wd[k] = v
            e.wait_ge(s, v)
        ins = fn(e)
        if dma:
            if lane not in self.lanes:
                self.lanes[lane] = [self.sem("L%d" % len(self.lanes)), 0]
            L = self.lanes[lane]
            L[1] += 16
            o.lane, o.lane_val = L[0], L[1]
            ins.then_inc(L[0], 16)
        elif self.flags[idx]:
            n = self.cnt[eng]
            self.cnt[eng] += 1
            sm = self.esem[eng][n % self.K]
            o.sigval = (sm, n // self.K + 1)
            ins.then_inc(sm, 1)
        return o

    def pe(self, fn, r=(), w=()):
        return self.op("pe", fn, r, w)

    def act(self, fn, r=(), w=()):
        return self.op("act", fn, r, w)

    def dve(self, fn, r=(), w=()):
        return self.op("dve", fn, r, w)

    def pool(self, fn, r=(), w=()):
        return self.op("pool", fn, r, w)

    def dma(self, q, fn, r=(), w=(), lane=None):
        return self.op(q, fn, r, w, dma=True, lane=lane)

    def finish(self, final_ops=()):
        if not self.dry:
            for o in final_ops:
                self.nc.sync.wait_ge(o.lane, o.lane_val)
        while self.stack:
            self.stack.pop().close()

D = 1024
SEQ = 2048
NB = 4
TB = 128
NBLK = SEQ // TB
CH = 64
ALPHA_ = (2.0) ** 0.25
LN_EPS_ = 1e-5
GN_EPS_ = 64e-5
LWC = 0.6065306597126334


def build(nb_run=NB, nblk_run=NBLK, stage="AB"):
    nc1 = bass.Bass("TRN2", target_bir_lowering=False)
    P1 = Prog(nc1)
    body(nc1, P1, nb_run, nblk_run, stage)
    flags = [o.sig for o in P1.allops]
    nc = bass.Bass("TRN2", target_bir_lowering=False)
    P = Prog(nc, flags)
    body(nc, P, nb_run, nblk_run, stage)
    return nc, P


def body(nc, P, nb_run, nblk_run, stage):
    dt = nc.dram_tensor
    x_d = dt("x", [NB, SEQ, D], F32, kind="ExternalInput").ap()
    c_d = dt("c", [NB, D], F32, kind="ExternalInput").ap()
    cond_w_d = dt("cond_w", [D, 6 * D], F32, kind="ExternalInput").ap()
    cond_b_d = dt("cond_b", [6 * D], F32, kind="ExternalInput").ap()
    w_in_d = dt("w_in", [D, 2816], F32, kind="ExternalInput").ap()
    mu_d = dt("mu_shift", [1792], F32, kind="ExternalInput").ap()
    conv_w_d = dt("conv_w", [31, 512], F32, kind="ExternalInput").ap()
    conv_b_d = dt("conv_b", [512], F32, kind="ExternalInput").ap()
    cg_d = dt("conv_ln_g", [512], F32, kind="ExternalInput").ap()
    cb_d = dt("conv_ln_b", [512], F32, kind="ExternalInput").ap()
    w0_d = dt("rw_w0", [512], F32, kind="ExternalInput").ap()
    w2_d = dt("rw_w2", [64, 512], F32, kind="ExternalInput").ap()
    a0_d = dt("rw_a0", [512], F32, kind="ExternalInput").ap()
    a2_d = dt("rw_a2", [64, 512], F32, kind="ExternalInput").ap()
    g2_d = dt("rw_g2", [128, 512], F32, kind="ExternalInput").ap()
    kk_d = dt("rw_kk", [512], F32, kind="ExternalInput").ap()
    ka_d = dt("rw_ka", [512], F32, kind="ExternalInput").ap()
    rk_d = dt("rw_rk", [8, 64], F32, kind="ExternalInput").ap()
    lg_d = dt("rw_lnx_g", [512], F32, kind="ExternalInput").ap()
    lb_d = dt("rw_lnx_b", [512], F32, kind="ExternalInput").ap()
    w_out_d = dt("w_out", [D, D], F32, kind="ExternalInput").ap()
    ln1g_d = dt("ln1_g", [D], F32, kind="ExternalInput").ap()
    ln1b_d = dt("ln1_b", [D], F32, kind="ExternalInput").ap()
    wq_d = dt("peer_wq", [D, 2048], F32, kind="ExternalInput").ap()
    k1_d = dt("peer_k1", [8, 128, 128], F32, kind="ExternalInput").ap()
    k2_d = dt("peer_k2", [8, 128, 128], F32, kind="ExternalInput").ap()
    pu_d = dt("peer_u", [16384, D], F32, kind="ExternalInput").ap()
    pv_d = dt("peer_v", [16384, D], F32, kind="ExternalInput").ap()
    ln2g_d = dt("ln2_g", [D], F32, kind="ExternalInput").ap()
    ln2b_d = dt("ln2_b", [D], F32, kind="ExternalInput").ap()
    out_d = dt("out", [NB, SEQ, D], F32, kind="ExternalOutput").ap()

    banks = [P.ps("bank%d" % i, [128, 512], F32) for i in range(8)]
    bctr = [0]

    nrot = [8]

    def bank():
        i = bctr[0] % nrot[0]
        bctr[0] += 1
        return banks[i], ("B", i)

    def rsqrt(dst, src, eps, r, w, floor=None):
        P.act(lambda e: e.activation(dst, src, AF.Sqrt, bias=epsb[0:dst.shape[0], ekey[eps]:ekey[eps] + 1]), r=list(r) + ["epsb"], w=[w])
        if floor is not None:
            P.dve(lambda e: e.tensor_scalar_max(dst, dst, floor), r=[w], w=[w])
        P.dve(lambda e: e.reciprocal(dst, dst), r=[w], w=[w])

    epsb = P.sb("epsb", [128, 4], F32)
    ekey = {LN_EPS_: 0, GN_EPS_: 1, 0.0: 2}
    P.pool(lambda e: e.memset(epsb[:, 0:1], LN_EPS_), w=["epsb"])
    P.pool(lambda e: e.memset(epsb[:, 1:2], GN_EPS_), w=["epsb"])
    P.pool(lambda e: e.memset(epsb[:, 2:3], 0.0), w=["epsb"])
    ident = P.sb("ident", [128, 128], F32)
    identb = P.sb("identb", [128, 128], BF16)
    iota_p = P.sb("iota_p", [128, 1], F32)
    iota_f = P.sb("iota_f", [128, 128], F32)
    P.pool(lambda e: e.iota(iota_p[:], [[0, 1]], base=0, channel_multiplier=1,
                            allow_small_or_imprecise_dtypes=True), w=["iota_p"])
    P.pool(lambda e: e.iota(iota_f[:], [[1, 128]], base=0, channel_multiplier=0,
                            allow_small_or_imprecise_dtypes=True), w=["iota_f"])
    P.dve(lambda e: e.tensor_scalar(ident[:], iota_f[:], iota_p[:, 0:1], None, ALU.is_equal),
          r=["iota_p", "iota_f"], w=["ident"])
    P.dve(lambda e: e.tensor_copy(identb[:], ident[:]), r=["ident"], w=["identb"])
    m_st = P.sb("m_st", [64, 64], F32)
    m_in = P.sb("m_in", [64, 64], F32)
    m_lo = P.sb("m_lo", [64, 64], F32)
    P.dve(lambda e: e.tensor_scalar(m_st[:], iota_f[0:64, 0:64], iota_p[0:64, 0:1], None, ALU.is_gt),
          r=["iota_p", "iota_f"], w=["m_st"])
    P.dve(lambda e: e.tensor_scalar(m_in[:], iota_f[0:64, 0:64], iota_p[0:64, 0:1], None, ALU.is_ge),
          r=["iota_p", "iota_f"], w=["m_in"])
    P.dve(lambda e: e.tensor_scalar(m_lo[:], iota_f[0:64, 0:64], iota_p[0:64, 0:1], None, ALU.is_lt),
          r=["iota_p", "iota_f"], w=["m_lo"])
    ones_c = P.sb("ones_c", [128, 128], F32)
    ones_h = P.sb("ones_h", [64, 64], F32)
    ones_g = P.sb("ones_g", [64, 64], F32)
    P.pool(lambda e: e.memset(ones_c[:], 1.0 / 512.0), w=["ones_c"])
    P.pool(lambda e: e.memset(ones_h[:], 1.0), w=["ones_h"])
    P.pool(lambda e: e.memset(ones_g[:], 1.0 / 64.0), w=["ones_g"])
    rmask = P.sb("rmask", [64, 2, 64], F32)
    P.pool(lambda e: e.memset(rmask[:], 1.0), w=["rmask"])
    P.pool(lambda e: e.memset(rmask[:, :, 0:1], 0.0), w=["rmask"])

    big = P.sb("big", [128, 2048], F32)
    stA = P.sb("stA", [64, 128], F32)
    stB = P.sb("stB", [88, 64], F32)
    PA = P.sb("PA", [128, 64], F32)
    PB = P.sb("PB", [64, 88], F32)
    P.pool(lambda e: e.memset(stA[:], 0.0), w=["stA"])
    P.pool(lambda e: e.memset(stB[:], 0.0), w=["stB"])
    rowsA = [(cond_b_d, 0, 48), (conv_b_d, 48, 4), (cg_d, 52, 4), (cb_d, 56, 4)]
    for (src, r0, n) in rowsA:
        P.dma("sp", (lambda e, src=src, r0=r0, n=n: e.dma_start(
            out=stA[r0:r0 + n, :], in_=src.rearrange("(c p) -> c p", p=128))), w=["stA"], lane="stA")
    P.dma("sp", lambda e: e.dma_start(out=stA[60:61, :], in_=mu_d[1664:1792].rearrange("(c p) -> c p", p=128)),
          w=["stA"], lane="stA")
    rowsB = [(mu_d[0:1536], 0, 24), (w0_d, 24, 8), (a0_d, 32, 8), (kk_d, 40, 8), (ka_d, 48, 8),
             (lg_d, 56, 8), (lb_d, 64, 8), (mu_d[1536:1664], 80, 2)]
    for (src, r0, n) in rowsB:
        P.dma("sp", (lambda e, src=src, r0=r0, n=n: e.dma_start(
            out=stB[r0:r0 + n, :], in_=src.rearrange("(c p) -> c p", p=64))), w=["stB"], lane="stB")
    P.dma("sp", lambda e: e.dma_start(out=stB[72:80, :], in_=rk_d), w=["stB"], lane="stB")
    bk, bu = bank()
    P.pe(lambda e: e.transpose(bk[:, 0:64], stA[:, :], ident[0:64, 0:64]), r=["stA", "ident"], w=[bu])
    P.dve(lambda e: e.tensor_copy(PA[:], bk[:, 0:64]), r=[bu], w=["PA"])
    bk2, bu2 = bank()
    P.pe(lambda e: e.transpose(bk2[0:64, 0:88], stB[:, :], ident[0:88, 0:88]), r=["stB", "ident"], w=[bu2])
    P.dve(lambda e: e.tensor_copy(PB[:], bk2[0:64, 0:88]), r=[bu2], w=["PB"])
    OMB = P.sb("OMB", [64, 88], F32)
    P.dve(lambda e: e.tensor_scalar(OMB[:], PB[:], -1.0, 1.0, ALU.mult, ALU.add), r=["PB"], w=["OMB"])
    OMA = P.sb("OMA", [128, 64], F32)
    P.dve(lambda e: e.tensor_scalar(OMA[:], PA[:], -1.0, 1.0, ALU.mult, ALU.add), r=["PA"], w=["OMA"])
    CW = P.sb("CW", [128, 4, 31], F32)
    P.dma("sp", lambda e: e.dma_start(out=big[0:31, 0:512], in_=conv_w_d), w=["big"], lane="big")
    for c in range(4):
        bk, bu = bank()
        P.pe(lambda e, bk=bk, c=c: e.transpose(bk[:, 0:31], big[0:31, c * 128:(c + 1) * 128], ident[0:31, 0:31]),
             r=["big", "ident"], w=[bu])
        P.dve(lambda e, bk=bk, c=c: e.tensor_copy(CW[:, c, :], bk[:, 0:31]), r=[bu], w=["CW"])

    siluT = P.sb("siluT", [128, 8, 4], F32)
    MOD = P.sb("MOD", [128, 48, 4], F32)
    P.dma("sp", lambda e: e.dma_start(out=big[0:4, 0:D], in_=c_d), w=["big"], lane="big")
    bk, bu = bank()
    for kc in range(8):
        P.pe(lambda e, bk=bk, kc=kc: e.transpose(bk[:, kc * 4:(kc + 1) * 4], big[0:4, kc * 128:(kc + 1) * 128],
                                                 ident[0:4, 0:4]), r=["big", "ident"], w=[bu])
    P.act(lambda e, bk=bk: e.activation(siluT[:].rearrange("p a b -> p (a b)"), bk[:, 0:32], AF.Silu),
          r=[bu], w=["siluT"])
    bkm, bum = bank()
    for kc in range(8):
        for q in range(3):
            P.dma("sp", (lambda e, kc=kc, q=q: e.dma_start(
                out=big[:, :], in_=cond_w_d[kc * 128:(kc + 1) * 128, q * 2048:(q + 1) * 2048])), w=["big"], lane="big")
            for mm_ in range(16):
                m = q * 16 + mm_
                P.pe(lambda e, kc=kc, m=m, mm_=mm_: e.matmul(bkm[:, m * 4:(m + 1) * 4], big[:, mm_ * 128:(mm_ + 1) * 128],
                                                    siluT[:, kc, :], start=(kc == 0 and m == 0),
                                                    stop=(kc == 7 and m == 47), skip_group_check=True),
                     r=["big", "siluT"], w=[bum])
    P.dve(lambda e: e.tensor_tensor(MOD[:], bkm[:, 0:192].rearrange("p (m b) -> p m b", b=4),
                                    PA[:, 0:48].unsqueeze(2).broadcast_to([128, 48, 4]), ALU.add),
          r=[bum, "PA"], w=["MOD"])
    for lo in (8, 32):
        P.dve(lambda e, lo=lo: e.tensor_scalar_add(MOD[:, lo:lo + 8, :], MOD[:, lo:lo + 8, :], 1.0),
              r=["MOD"], w=["MOD"])
    GROW = P.sb("GROW", [4, 1, D], F32)
    for gi, lo in enumerate((16,)):
        for half in range(2):
            bk, bu = bank()
            for c in range(4):
                P.pe(lambda e, bk=bk, c=c, lo=lo, half=half: e.transpose(
                    bk[0:4, c * 128:(c + 1) * 128], MOD[:, lo + half * 4 + c, :], ident[:, :]),
                    r=["MOD", "ident"], w=[bu])
            P.dve(lambda e, bk=bk, gi=gi, half=half: e.tensor_copy(GROW[:, gi, half * 512:(half + 1) * 512],
                                                                   bk[0:4, :]), r=[bu], w=["GROW"])
    SEL = P.sb("SEL", [4, 4, 128], F32)
    P.dve(lambda e: e.tensor_copy(SEL[:], ident[0:4, 0:4].unsqueeze(2).broadcast_to([4, 4, 128])),
          r=["ident"], w=["SEL"])

    LG1 = P.sb("LG1", [128, D], F32)
    LB1 = P.sb("LB1", [128, D], F32)
    P.dma("sp", lambda e: e.dma_start(out=LG1[:], in_=ln1g_d.partition_broadcast(128)), w=["LG1"], lane="LG1")
    P.dma("sp", lambda e: e.dma_start(out=LB1[:], in_=ln1b_d.partition_broadcast(128)), w=["LB1"], lane="LB1")
    st6 = P.sb("st6", [128, 2, 6], F32)
    mv = P.sb("mv", [128, 2], F32)
    rstd = P.sb("rstd", [128, 1], F32)
    P.push()
    w_in_bf = P.sb("w_in_bf", [128, 8, 2816], BF16)
    for kc in range(8):
        for q in range(2):
            P.dma("sp", lambda e, kc=kc, q=q: e.dma_start(out=big[:, 0:1408], in_=w_in_d[kc * 128:(kc + 1) * 128, q * 1408:(q + 1) * 1408]),
                  w=["big"], lane="big")
            P.act(lambda e, kc=kc, q=q: e.activation(w_in_bf[:, kc, q * 1408:(q + 1) * 1408], big[:, 0:1408], AF.Copy), r=["big"], w=["w_in_bf"])
    w_oc_bf = P.sb("w_oc_bf", [128, 4, D], BF16)
    w_or_bf = P.sb("w_or_bf", [64, 8, D], BF16)
    for c in range(4):
        P.dma("sp", lambda e, c=c: e.dma_start(out=big[:, 0:D], in_=w_out_d[c * 128:(c + 1) * 128, :]),
              w=["big"], lane="big")
        P.dve(lambda e, c=c: e.tensor_copy(w_oc_bf[:, c, :], big[:, 0:D]), r=["big"], w=["w_oc_bf"])
    for h in range(8):
        P.dma("sp", lambda e, h=h: e.dma_start(out=big[0:64, 0:D], in_=w_out_d[512 + h * 64:512 + (h + 1) * 64, :]),
              w=["big"], lane="big")
        P.dve(lambda e, h=h: e.tensor_copy(w_or_bf[:, h, :], big[0:64, 0:D]), r=["big"], w=["w_or_bf"])
    w2_bf = P.sb("w2_bf", [64, 512], BF16)
    a2_bf = P.sb("a2_bf", [64, 512], BF16)
    g2_bf = P.sb("g2_bf", [128, 512], BF16)
    for (src, dst, np_, nm) in ((w2_d, w2_bf, 64, "w2_bf"), (a2_d, a2_bf, 64, "a2_bf"), (g2_d, g2_bf, 128, "g2_bf")):
        P.dma("sp", lambda e, src=src, np_=np_: e.dma_start(out=big[0:np_, 0:512], in_=src), w=["big"], lane="big")
        P.dve(lambda e, dst=dst, np_=np_: e.tensor_copy(dst[:], big[0:np_, 0:512]), r=["big"], w=[nm])

    x1_d = out_d

    xb = P.sb("xb", [128, D], F32)
    xn = P.sb("xn", [128, D], F32)
    hT = [P.sb("hT%d" % i, [128, 8, TB + 1], BF16) for i in range(2)]
    ub = [P.sb("ub%d" % i, [128, 4, 30 + TB], F32) for i in range(2)]
    sig = P.sb("sig", [128, TB], F32)
    acc = P.sb("acc", [128, 4, TB], F32)
    sq = P.sb("sq", [128, 4, TB], F32)
    cm = P.sb("cm", [128, TB], F32)
    cr = P.sb("cr", [128, TB], F32)
    ct = P.sb("ct", [128, TB], F32)
    ycT = P.sb("ycT", [128, 4, TB], BF16)
    tmpA = P.sb("tmpA", [128, TB], F32)
    F = [P.sb("F%d" % i, [64, 8, TB], F32) for i in range(11)]

    def role(name, idx):
        P.alias[name] = "F%d" % idx
        return F[idx]
    VBb = P.sb("VBb", [64, 8, TB], BF16)
    TW = P.sb("TW", [64, TB], BF16)
    ADb = P.sb("ADb", [64, TB], BF16)
    SGD = P.sb("SGD", [128, TB], BF16)
    RT = P.sb("RT", [64, 8, TB], BF16)
    KT = P.sb("KT", [64, 8, TB], BF16)
    BT = P.sb("BT", [64, 8, TB], BF16)
    ATl = P.sb("ATl", [64, 8, TB], BF16)
    YRW = P.sb("YRW", [64, 8, TB], BF16)
    Vt = P.sb("Vt", [64, 8, 64], BF16)
    Ktm = P.sb("Ktm", [64, 8, 64], BF16)
    Btm = P.sb("Btm", [64, 8, 64], BF16)
    AabT = P.sb("AabT", [64, 8, 64], BF16)
    ArbT = P.sb("ArbT", [64, 8, 64], BF16)
    AakT = P.sb("AakT", [64, 8, 64], BF16)
    ArkT = P.sb("ArkT", [64, 8, 64], BF16)
    Npw = [P.sb("Npw%d" % i, [64, 8, 64], BF16) for i in range(2)]
    NTpw = [P.sb("NTpw%d" % i, [64, 8, 64], BF16) for i in range(2)]
    PT = [P.sb("PT%d" % i, [64, 8, 64], BF16) for i in range(2)]
    Xb = P.sb("Xb", [64, 8, 64], BF16)
    Ub = P.sb("Ub", [64, 8, 64], BF16)
    S0 = P.sb("S0", [64, 8, 64], F32)
    S0b = P.sb("S0b", [64, 8, 64], BF16)
    Stmp = P.sb("Stmp", [64, 8, 64], F32)
    G1 = P.sb("G1", [128, D], F32)
    identb8 = P.sb("identb8", [64, 8, 64], BF16)
    P.dve(lambda e: e.tensor_copy(identb8[:], ident[0:64, 0:64].unsqueeze(1).broadcast_to([64, 8, 64])),
          r=["ident"], w=["identb8"])

    def pbc(col):
        return PB[:, col:col + 8].unsqueeze(2).broadcast_to([64, 8, TB])

    def ombc(col):
        return OMB[:, col:col + 8].unsqueeze(2).broadcast_to([64, 8, TB])

    def m8(m):
        return m[:, :].unsqueeze(1).broadcast_to([64, 8, 64])

    out_ops = []
    for b in range(nb_run):
        for half in range(2):
            bk, bu = bank()
            P.pe(lambda e, bk=bk, b=b, half=half: e.matmul(bk[:, :], SEL[:, b, :], GROW[:, 0, half * 512:(half + 1) * 512],
                                                           start=True, stop=True), r=["SEL", "GROW"], w=[bu])
            P.act(lambda e, bk=bk, half=half: e.activation(G1[:, half * 512:(half + 1) * 512], bk[:, :], AF.Copy),
                  r=[bu], w=["G1"])
        P.pool(lambda e: e.memset(hT[1][:, :, 0:1], 0.0), w=["hT1"])
        P.pool(lambda e: e.memset(ub[1][:, :, 0:30], 0.0), w=["ub1"])
        P.pool(lambda e: e.memset(S0[:], 0.0), w=["S0"])
        P.pool(lambda e: e.memset(S0b[:], 0.0), w=["S0b"])
        for blk in range(nblk_run):
            par = blk % 2
            hTc, hTn = hT[par], hT[1 - par]
            hu, hun = "hT%d" % par, "hT%d" % (1 - par)
            ubc, ubn = ub[par], ub[1 - par]
            uu, uun = "ub%d" % par, "ub%d" % (1 - par)
            if blk == 0:
                P.pool(lambda e: e.memset(hTc[:, :, 0:1], 0.0), w=[hu])
                P.pool(lambda e: e.memset(ubc[:, :, 0:30], 0.0), w=[uu])
            t0 = blk * TB
            P.dma("sp", lambda e, b=b, t0=t0: e.dma_start(out=xb[:], in_=x_d[b, t0:t0 + TB, :]), w=["xb"], lane="xb")
            for hh in range(2):
                P.dve(lambda e, hh=hh: e.bn_stats(st6[:, hh, :], xb[:, hh * 512:(hh + 1) * 512]), r=["xb"], w=["st6"])
            P.dve(lambda e: e.bn_aggr(mv[:], st6[:].rearrange("p a b -> p (a b)")), r=["st6"], w=["mv"])
            rsqrt(rstd[:], mv[:, 1:2], LN_EPS_, ["mv"], "rstd")
            P.dve(lambda e: e.tensor_scalar(xn[:], xb[:], mv[:, 0:1], rstd[:, 0:1], ALU.subtract, ALU.mult),
                  r=["xb", "mv", "rstd"], w=["xn"])
            for half in range(2):
                bk, bu = bank()
                for j in range(4):
                    fc = half * 4 + j
                    P.pe(lambda e, bk=bk, j=j, fc=fc: e.transpose(bk[:, j * 128:(j + 1) * 128],
                                                                  xn[:, fc * 128:(fc + 1) * 128], ident[:, :]),
                         r=["xn", "ident"], w=[bu])
                for j in range(4):
                    fc = half * 4 + j
                    P.act(lambda e, bk=bk, j=j, fc=fc, b=b, hTc=hTc: e.activation(
                        hTc[:, fc, 1:TB + 1], bk[:, j * 128:(j + 1) * 128], AF.Identity,
                        bias=MOD[:, fc, b:b + 1], scale=MOD[:, 8 + fc, b:b + 1]), r=[bu, "MOD"], w=[hu])
            P.pool(lambda e, hTc=hTc, hTn=hTn: e.tensor_copy(hTn[:, :, 0:1], hTc[:, :, TB:TB + 1]), r=[hu], w=[hun])

            RB = role("RB", 0); KB = role("KB", 1); VB = role("VB", 2); GB = role("GB", 3)
            SGW = role("SGW", 4); AB = role("AB", 5); T1 = role("T1", 9)
            def inproj(col0, M, N0):
                bk, bu = bank()
                for kc in range(8):
                    P.pe(lambda e, bk=bk, kc=kc, col0=col0, M=M, N0=N0, hTc=hTc: e.matmul(
                        bk[0:M, 0:TB + 1 - N0], w_in_bf[:, kc, col0:col0 + M], hTc[:, kc, N0:TB + 1],
                        start=(kc == 0), stop=(kc == 7)), r=["w_in_bf", hu], w=[bu])
                return bk, bu

            for c in range(4):
                bkg, bug = inproj(512 + c * 128, 128, 1)
                P.act(lambda e, bkg=bkg: e.activation(sig[:], bkg[:, 0:TB], AF.Sigmoid), r=[bug], w=["sig"])
                bkv, buv = inproj(c * 128, 128, 1)
                P.dve(lambda e, bkv=bkv, c=c, ubc=ubc: e.tensor_tensor(ubc[:, c, 30:30 + TB], bkv[:, 0:TB], sig[:], ALU.mult),
                      r=[buv, "sig"], w=[uu])
            P.pool(lambda e, ubc=ubc, ubn=ubn: e.tensor_copy(ubn[:, :, 0:30], ubc[:, :, TB:TB + 30]), r=[uu], w=[uun])

            def shift_evac(bk, bu, Mp, dst, dname, mucol, omcol):
                P.act(lambda e: e.activation(tmpA[0:Mp, :], bk[0:Mp, 1:TB + 1], AF.Identity, scale=omcol),
                      r=[bu, "OMB", "OMA"], w=["tmpA"])
                P.dve(lambda e: e.scalar_tensor_tensor(dst, bk[0:Mp, 0:TB], mucol, tmpA[0:Mp, :], ALU.mult, ALU.add),
                      r=[bu, "tmpA", "PB", "PA"], w=[dname])

            for wi, (dstT, dn) in enumerate(((RB, "RB"), (KB, "KB"), (VB, "VB"))):
                for h in range(8):
                    bk, bu = inproj(1024 + wi * 512 + h * 64, 64, 0)
                    shift_evac(bk, bu, 64, dstT[:, h, :], dn, PB[:, wi * 8 + h:wi * 8 + h + 1],
                               OMB[:, wi * 8 + h:wi * 8 + h + 1])
            bk, bu = inproj(2560, 64, 0)
            shift_evac(bk, bu, 64, T1[:, 0, :], "T1", PB[:, 80:81], OMB[:, 80:81])
            P.act(lambda e: e.activation(TW[:], T1[:, 0, :], AF.Tanh), r=["T1"], w=["TW"])
            bk, bu = inproj(2624, 64, 0)
            shift_evac(bk, bu, 64, T1[:, 1, :], "T1", PB[:, 81:82], OMB[:, 81:82])
            P.act(lambda e: e.activation(ADb[:], T1[:, 1, :], AF.Copy), r=["T1"], w=["ADb"])
            bk, bu = inproj(2688, 128, 0)
            shift_evac(bk, bu, 128, sq[:, 0, :], "sq", PA[:, 60:61], OMA[:, 60:61])
            P.act(lambda e: e.activation(SGD[:], sq[:, 0, :], AF.Sigmoid), r=["sq"], w=["SGD"])
            for h in range(8):
                bk, bu = bank()
                P.pe(lambda e, bk=bk, h=h: e.matmul(bk[0:64, 0:TB], w2_bf[:, h * 64:(h + 1) * 64], TW[:], start=True, stop=True),
                     r=["w2_bf", "TW"], w=[bu])
                P.act(lambda e, bk=bk, h=h: e.activation(SGW[:, h, :], bk[0:64, 0:TB], AF.Sigmoid, bias=PB[:, 24 + h:25 + h]),
                      r=[bu, "PB"], w=["SGW"])
                bk, bu = bank()
                P.pe(lambda e, bk=bk, h=h: e.matmul(bk[0:64, 0:TB], a2_bf[:, h * 64:(h + 1) * 64], ADb[:], start=True, stop=True),
                     r=["a2_bf", "ADb"], w=[bu])
                P.act(lambda e, bk=bk, h=h: e.activation(AB[:, h, :], bk[0:64, 0:TB], AF.Sigmoid, bias=PB[:, 32 + h:33 + h]),
                      r=[bu, "PB"], w=["AB"])
                bk, bu = bank()
                P.pe(lambda e, bk=bk, h=h: e.matmul(bk[0:64, 0:TB], g2_bf[:, h * 64:(h + 1) * 64], SGD[:], start=True, stop=True),
                     r=["g2_bf", "SGD"], w=[bu])
                P.act(lambda e, bk=bk, h=h: e.activation(GB[:, h, :], bk[0:64, 0:TB], AF.Copy), r=[bu], w=["GB"])

            for c in range(4):
                eng = P.dve
                eng(lambda e, c=c, ubc=ubc: e.tensor_scalar(acc[:, c, :], ubc[:, c, 0:TB], CW[:, c, 0:1], PA[:, 48 + c:49 + c],
                                                            ALU.mult, ALU.add), r=[uu, "CW", "PA"], w=[("acc", c)])
                for j in range(1, 31):
                    eng(lambda e, c=c, j=j, ubc=ubc: e.scalar_tensor_tensor(acc[:, c, :], ubc[:, c, j:j + TB], CW[:, c, j:j + 1],
                                                                            acc[:, c, :], ALU.mult, ALU.add),
                        r=[uu, "CW", ("acc", c)], w=[("acc", c)])
            for c in range(4):
                P.act(lambda e, c=c: e.activation(sq[:, c, :], acc[:, c, :], AF.Square), r=[("acc", c)], w=["sq"])
            bkm_, bum_ = bank()
            for c in range(4):
                P.pe(lambda e, c=c, bkm_=bkm_: e.matmul(bkm_[:, 0:TB], ones_c[:], acc[:, c, :], start=(c == 0), stop=(c == 3)),
                     r=["ones_c", ("acc", c)], w=[bum_])
            bks_, bus_ = bank()
            for c in range(4):
                P.pe(lambda e, c=c, bks_=bks_: e.matmul(bks_[:, 0:TB], ones_c[:], sq[:, c, :], start=(c == 0), stop=(c == 3)),
                     r=["ones_c", "sq"], w=[bus_])
            P.dve(lambda e, bkm_=bkm_: e.tensor_copy(cm[:], bkm_[:, 0:TB]), r=[bum_], w=["cm"])
            P.dve(lambda e: e.tensor_tensor(ct[:], cm[:], cm[:], ALU.mult), r=["cm"], w=["ct"])
            P.dve(lambda e, bks_=bks_: e.tensor_tensor(cr[:], bks_[:, 0:TB], ct[:], ALU.subtract), r=[bus_, "ct"], w=["cr"])
            rsqrt(cr[:], cr[:], LN_EPS_, ["cr"], "cr")
            for c in range(4):
                P.pool(lambda e, c=c: e.tensor_tensor(sq[:, c, :], acc[:, c, :], cm[:], ALU.subtract),
                       r=[("acc", c), "cm", "sq"], w=["sq"])
                P.pool(lambda e, c=c: e.tensor_tensor(sq[:, c, :], sq[:, c, :], cr[:], ALU.mult), r=["sq", "cr"], w=["sq"])
                P.act(lambda e, c=c: e.activation(ycT[:, c, :], sq[:, c, :], AF.Silu, bias=PA[:, 56 + c:57 + c],
                                                  scale=PA[:, 52 + c:53 + c]), r=["sq", "PA"], w=["ycT"])

            CUM = role("CUM", 6); EG = role("EG", 7); IEG = role("IEG", 8); EGX = role("EGX", 10)
            fl = lambda t: t[:].rearrange("p h t -> p (h t)")
            for h in range(8):
                P.dve(lambda e, h=h: e.tensor_tensor_scan(CUM[:, h, :], rmask[:].rearrange("p c t -> p (c t)"), SGW[:, h, :], 0.0,
                                                          ALU.mult, ALU.add), r=["rmask", "SGW"], w=["CUM"])
            P.act(lambda e: e.activation(fl(EG), fl(CUM), AF.Exp, scale=-LWC), r=["CUM"], w=["EG"])
            P.act(lambda e: e.activation(fl(IEG), fl(CUM), AF.Exp, scale=LWC), r=["CUM"], w=["IEG"])
            P.pool(lambda e: e.tensor_tensor(fl(T1), fl(CUM), fl(SGW), ALU.subtract), r=["CUM", "SGW"], w=["T1"])
            P.act(lambda e: e.activation(fl(EGX), fl(T1), AF.Exp, scale=-LWC), r=["T1"], w=["EGX"])
            KKR = role("KKR", 4); SQ2 = role("SQ2", 6); KKN = role("KKN", 9)
            P.dve(lambda e: e.tensor_tensor(KKR[:], KB[:], pbc(40), ALU.mult), r=["KB", "PB"], w=["KKR"])
            P.pool(lambda e: e.tensor_tensor(SQ2[:], KKR[:], KKR[:], ALU.mult), r=["KKR"], w=["SQ2"])
            for half in range(2):
                bk, bu = bank()
                P.pe(lambda e, bk=bk, half=half: e.matmul(bk[0:64, :], ones_h[:], fl(SQ2)[:, half * 512:(half + 1) * 512],
                                                          start=True, stop=True), r=["ones_h", "SQ2"], w=[bu])
                rsqrt(fl(KKN)[:, half * 512:(half + 1) * 512], bk[0:64, :], 0.0, [bu], "KKN", floor=1e-12)
            P.dve(lambda e: e.tensor_tensor(KKN[:], KKN[:], KKR[:], ALU.mult), r=["KKN", "KKR"], w=["KKN"])
            T1 = role("T1", 4); KF = role("KF", 6)
            P.pool(lambda e: e.tensor_tensor(T1[:], AB[:], pbc(48), ALU.mult), r=["AB", "PB"], w=["T1"])
            P.pool(lambda e: e.tensor_tensor(T1[:], T1[:], ombc(48), ALU.add), r=["T1", "OMB"], w=["T1"])
            P.dve(lambda e: e.tensor_tensor(KF[:], KB[:], T1[:], ALU.mult), r=["KB", "T1"], w=["KF"])
            BBt = role("BBt", 1)
            P.pool(lambda e: e.tensor_tensor(BBt[:], KKN[:], AB[:], ALU.mult), r=["KKN", "AB"], w=["BBt"])
            P.dve(lambda e: e.tensor_tensor(RT[:], RB[:], EG[:], ALU.mult), r=["RB", "EG"], w=["RT"])
            P.pool(lambda e: e.tensor_tensor(KT[:], KF[:], IEG[:], ALU.mult), r=["KF", "IEG"], w=["KT"])
            P.dve(lambda e: e.tensor_tensor(BT[:], BBt[:], IEG[:], ALU.mult), r=["BBt", "IEG"], w=["BT"])
            P.dve(lambda e: e.scalar_tensor_tensor(ATl[:], KKN[:], -1.0, EGX[:], ALU.mult, ALU.mult),
                   r=["KKN", "EGX"], w=["ATl"])
            P.act(lambda e: e.activation(fl(VBb), fl(VB), AF.Copy), r=["VB"], w=["VBb"])
            SQ2 = role("SQ2", 8); BON = role("BON", 4); YB = role("YB", 5)
            P.dve(lambda e: e.tensor_tensor(SQ2[:], RB[:], KF[:], ALU.mult), r=["RB", "KF"], w=["SQ2"])
            P.dve(lambda e: e.tensor_tensor(SQ2[:], SQ2[:], pbc(72), ALU.mult), r=["SQ2", "PB"], w=["SQ2"])
            for half in range(2):
                bk, bu = bank()
                P.pe(lambda e, bk=bk, half=half: e.matmul(bk[0:64, :], ones_h[:], fl(SQ2)[:, half * 512:(half + 1) * 512],
                                                          start=True, stop=True), r=["ones_h", "SQ2"], w=[bu])
                P.dve(lambda e, bk=bk, half=half: e.tensor_tensor(fl(BON)[:, half * 512:(half + 1) * 512], bk[0:64, :],
                                                                  fl(VB)[:, half * 512:(half + 1) * 512], ALU.mult),
                      r=[bu, "VB"], w=["BON"])

            for cc in range(TB // CH):
                c0 = cc * CH
                cs = slice(c0, c0 + CH)

                def bfview(bk):
                    return bk[0:64, 0:256].bitcast(BF16).rearrange("p (h t) -> p h t", h=8)

                for (srcT, sn, dstT, dn) in ((VBb, "VBb", Vt, "Vt"), (KT, "KT", Ktm, "Ktm"), (BT, "BT", Btm, "Btm")):
                    bk, bu = bank()
                    for h in range(8):
                        P.pe(lambda e, bk=bk, h=h, srcT=srcT: e.transpose(bfview(bk)[:, h, :], srcT[:, h, cs], identb[0:64, 0:64]),
                             r=[sn, "identb"], w=[bu])
                    P.act(lambda e, bk=bk, dstT=dstT: e.activation(dstT[:], bfview(bk), AF.Copy), r=[bu], w=[dn])

                def amat(lhsT_, ln, rhs_, rn, mask, mn, dst, dn, eng):
                    bk, bu = bank()
                    for h in range(8):
                        P.pe(lambda e, bk=bk, h=h: e.matmul(bk[0:64, h * 64:(h + 1) * 64], lhsT_[:, h, cs], rhs_[:, h, cs],
                                                            start=True, stop=True), r=[ln, rn], w=[bu])
                    eng(lambda e, bk=bk: e.tensor_tensor(dst[:], bk[0:64, :].rearrange("p (h t) -> p h t", h=8), m8(mask), ALU.mult),
                        r=[bu, mn], w=[dn])

                amat(BT, "BT", ATl, "ATl", m_st, "m_st", NTpw[0], "NT0", P.dve)
                amat(ATl, "ATl", BT, "BT", m_lo, "m_lo", Npw[0], "N0", P.dve)
                amat(BT, "BT", RT, "RT", m_in, "m_in", ArbT, "ArbT", P.dve)
                amat(KT, "KT", ATl, "ATl", m_st, "m_st", AakT, "AakT", P.dve)
                amat(KT, "KT", RT, "RT", m_in, "m_in", ArkT, "ArkT", P.dve)

                def mm8(lhs, ln, rhs, rn, dst, dn, add=None, an=None):
                    bk, bu = bank()
                    for h in range(8):
                        P.pe(lambda e, bk=bk, h=h: e.matmul(bk[0:64, h * 64:(h + 1) * 64], lhs[:, h, :], rhs[:, h, :],
                                                            start=True, stop=True), r=[ln, rn], w=[bu])
                    v = bk[0:64, :].rearrange("p (h t) -> p h t", h=8)
                    if add is None:
                        P.act(lambda e: e.activation(dst[:], v, AF.Copy), r=[bu], w=[dn])
                    else:
                        P.dve(lambda e: e.tensor_tensor(dst[:], v, add[:], ALU.add), r=[bu, an], w=[dn])

                P.dve(lambda e: e.tensor_tensor(PT[0][:], NTpw[0][:], identb8[:], ALU.add), r=["NT0", "identb8"], w=["PT0"])
                cur = 0
                for i in range(5):
                    a_, b_ = i % 2, (i + 1) % 2
                    mm8(NTpw[a_], "NT%d" % a_, Npw[a_], "N%d" % a_, Npw[b_], "N%d" % b_)
                    if i < 4:
                        mm8(Npw[a_], "N%d" % a_, NTpw[a_], "NT%d" % a_, NTpw[b_], "NT%d" % b_)
                    mm8(Npw[b_], "N%d" % b_, PT[cur], "PT%d" % cur, PT[1 - cur], "PT%d" % (1 - cur),
                        add=PT[cur], an="PT%d" % cur)
                    cur = 1 - cur
                PTf, PTn = PT[cur], "PT%d" % cur
                bk, bu = bank()
                for h in range(8):
                    P.pe(lambda e, bk=bk, h=h: e.matmul(bk[0:64, h * 64:(h + 1) * 64], ATl[:, h, cs], S0b[:, h, :],
                                                        start=True, stop=False), r=["ATl", "S0b"], w=[bu])
                    P.pe(lambda e, bk=bk, h=h: e.matmul(bk[0:64, h * 64:(h + 1) * 64], AakT[:, h, :], Vt[:, h, :],
                                                        start=False, stop=True), r=["AakT", "Vt"], w=[bu])
                P.act(lambda e, bk=bk: e.activation(Xb[:], bk[0:64, :].rearrange("p (h t) -> p h t", h=8), AF.Copy),
                      r=[bu], w=["Xb"])
                mm8(PTf, PTn, Xb, "Xb", Ub, "Ub")
                bk, bu = bank()
                for h in range(8):
                    o_ = bk[0:64, h * 64:(h + 1) * 64]
                    P.pe(lambda e, o_=o_, h=h: e.matmul(o_, S0b[:, h, :], RT[:, h, cs], start=True, stop=False),
                         r=["S0b", "RT"], w=[bu])
                    P.pe(lambda e, o_=o_, h=h: e.matmul(o_, Ub[:, h, :], ArbT[:, h, :], start=False, stop=False),
                         r=["Ub", "ArbT"], w=[bu])
                    P.pe(lambda e, o_=o_, h=h: e.matmul(o_, Vt[:, h, :], ArkT[:, h, :], start=False, stop=True),
                         r=["Vt", "ArkT"], w=[bu])
                P.act(lambda e, bk=bk: e.activation(YB[:, :, cs], bk[0:64, :].rearrange("p (h t) -> p h t", h=8), AF.Copy),
                      r=[bu], w=["YB"])
                bk, bu = bank()
                for h in range(8):
                    o_ = bk[0:64, h * 64:(h + 1) * 64]
                    P.pe(lambda e, o_=o_, h=h: e.matmul(o_, Btm[:, h, :], Ub[:, h, :], start=True, stop=False),
                         r=["Btm", "Ub"], w=[bu])
                    P.pe(lambda e, o_=o_, h=h: e.matmul(o_, Ktm[:, h, :], Vt[:, h, :], start=False, stop=True),
                         r=["Ktm", "Vt"], w=[bu])
                P.dve(lambda e, bk=bk: e.tensor_tensor(Stmp[:], bk[0:64, :].rearrange("p (h t) -> p h t", h=8), S0[:], ALU.add),
                      r=[bu, "S0"], w=["Stmp"])
                P.dve(lambda e: e.tensor_tensor(S0[:], Stmp[:], EG[:, :, c0 + CH - 1:c0 + CH].broadcast_to([64, 8, 64]), ALU.mult),
                      r=["Stmp", "EG"], w=["S0"])
                P.act(lambda e: e.activation(S0b[:], S0[:], AF.Copy), r=["S0"], w=["S0b"])

            T1 = role("T1", 9); KF = role("KF", 10)
            P.pool(lambda e: e.tensor_tensor(SQ2[:], YB[:], YB[:], ALU.mult), r=["YB"], w=["SQ2"])
            for half in range(2):
                hs = slice(half * 512, (half + 1) * 512)
                bk1, bu1 = bank()
                P.pe(lambda e, bk1=bk1, hs=hs: e.matmul(bk1[0:64, :], ones_g[:], fl(YB)[:, hs], start=True, stop=True),
                     r=["ones_g", "YB"], w=[bu1])
                bk2_, bu2_ = bank()
                P.pe(lambda e, bk2_=bk2_, hs=hs: e.matmul(bk2_[0:64, :], ones_g[:], fl(SQ2)[:, hs], start=True, stop=True),
                     r=["ones_g", "SQ2"], w=[bu2_])
                P.dve(lambda e, bk1=bk1, hs=hs: e.tensor_copy(fl(T1)[:, hs], bk1[0:64, :]), r=[bu1], w=["T1"])
                P.dve(lambda e, hs=hs: e.tensor_tensor(fl(KF)[:, hs], fl(T1)[:, hs], fl(T1)[:, hs], ALU.mult), r=["T1"], w=["KF"])
                P.dve(lambda e, bk2_=bk2_, hs=hs: e.tensor_tensor(fl(KF)[:, hs], bk2_[0:64, :], fl(KF)[:, hs], ALU.subtract),
                      r=[bu2_, "KF"], w=["KF"])
            rsqrt(fl(KF), fl(KF), GN_EPS_, ["KF"], "KF")
            P.pool(lambda e: e.tensor_tensor(YB[:], YB[:], T1[:], ALU.subtract), r=["YB", "T1"], w=["YB"])
            P.pool(lambda e: e.tensor_tensor(YB[:], YB[:], KF[:], ALU.mult), r=["YB", "KF"], w=["YB"])
            P.pool(lambda e: e.tensor_tensor(YB[:], YB[:], pbc(56), ALU.mult), r=["YB", "PB"], w=["YB"])
            P.pool(lambda e: e.tensor_tensor(YB[:], YB[:], pbc(64), ALU.add), r=["YB", "PB"], w=["YB"])
            P.dve(lambda e: e.tensor_tensor(YB[:], YB[:], BON[:], ALU.add), r=["YB", "BON"], w=["YB"])
            P.dve(lambda e: e.tensor_tensor(YRW[:], YB[:], GB[:], ALU.mult), r=["YB", "GB"], w=["YRW"])

            for half in range(2):
                hs = slice(half * 512, (half + 1) * 512)
                bk, bu = bank()
                for c in range(4):
                    P.pe(lambda e, bk=bk, c=c, hs=hs: e.matmul(bk[:, :], ycT[:, c, :], w_oc_bf[:, c, hs], start=(c == 0), stop=False),
                         r=["ycT", "w_oc_bf"], w=[bu])
                for h in range(8):
                    P.pe(lambda e, bk=bk, h=h, hs=hs: e.matmul(bk[:, :], YRW[:, h, :], w_or_bf[:, h, hs], start=False, stop=(h == 7)),
                         r=["YRW", "w_or_bf"], w=[bu])
                P.dve(lambda e, bk=bk, hs=hs: e.tensor_tensor(xn[:, hs], bk[:, :], G1[:, hs], ALU.mult), r=[bu, "G1"], w=["xn"])
            P.dve(lambda e: e.scalar_tensor_tensor(xn[:], xb[:], ALPHA_, xn[:], ALU.mult, ALU.add), r=["xb", "xn"], w=["xn"])
            for hh in range(2):
                P.dve(lambda e, hh=hh: e.bn_stats(st6[:, hh, :], xn[:, hh * 512:(hh + 1) * 512]), r=["xn"], w=["st6"])
            P.dve(lambda e: e.bn_aggr(mv[:], st6[:].rearrange("p a b -> p (a b)")), r=["st6"], w=["mv"])
            rsqrt(rstd[:], mv[:, 1:2], LN_EPS_, ["mv"], "rstd")
            P.dve(lambda e: e.tensor_scalar(xn[:], xn[:], mv[:, 0:1], rstd[:, 0:1], ALU.subtract, ALU.mult),
                  r=["xn", "mv", "rstd"], w=["xn"])
            P.pool(lambda e: e.tensor_tensor(xn[:], xn[:], LG1[:], ALU.mult), r=["xn", "LG1"], w=["xn"])
            P.pool(lambda e: e.tensor_tensor(xn[:], xn[:], LB1[:], ALU.add), r=["xn", "LB1"], w=["xn"])
            o = P.dma("sp", lambda e, b=b, t0=t0: e.dma_start(out=x1_d[b, t0:t0 + TB, :], in_=xn[:]), r=["xn"], w=[("x1d", b, blk)], lane="xn_out")
            out_ops.append(o)

    if stage == "A":
        P.finish(final_ops=out_ops[-1:])
        return

    P.pop()
    P.push()
    NT = 256
    NST = SEQ // NT
    NBUF = 3
    ut_scr = nc.dram_tensor("ut_scr", [128, 128, 8, 128], BF16).ap()
    v_scr = nc.dram_tensor("v_scr", [128, 128, D], BF16).ap()

    ust = [P.sb("ust%d" % i, [128, D], F32) for i in range(2)]
    vst = [P.sb("vst%d" % i, [128, D], F32) for i in range(2)]
    utb = [P.sb("utb%d" % i, [128, 8, 128], BF16) for i in range(2)]
    vbb = [P.sb("vbb%d" % i, [128, D], BF16) for i in range(2)]
    for c in range(128):
        i = c % 2
        P.dma("sp", lambda e, c=c, i=i: e.dma_start(out=ust[i][:], in_=pu_d[c * 128:(c + 1) * 128, :]), w=["ust%d" % i], lane="ust%d" % i)
        P.dma("sp", lambda e, c=c, i=i: e.dma_start(out=vst[i][:], in_=pv_d[c * 128:(c + 1) * 128, :]), w=["vst%d" % i], lane="vst%d" % i)
        for half in range(2):
            bk, bu = bank()
            for q in range(4):
                dc = half * 4 + q
                P.pe(lambda e, bk=bk, q=q, dc=dc, i=i: e.transpose(bk[:, q * 128:(q + 1) * 128], ust[i][:, dc * 128:(dc + 1) * 128], ident[:, :]),
                     r=["ust%d" % i, "ident"], w=[bu])
            if half == 0:
                P.act(lambda e, bk=bk, i=i: e.activation(utb[i][:, 0:4, :], bk[:, :].rearrange("p (q e) -> p q e", q=4), AF.Copy),
                      r=[bu], w=["utb%d" % i])
            else:
                P.dve(lambda e, bk=bk, i=i: e.tensor_copy(utb[i][:, 4:8, :], bk[:, :].rearrange("p (q e) -> p q e", q=4)),
                      r=[bu], w=["utb%d" % i])
        P.pool(lambda e, i=i: e.tensor_copy(vbb[i][:], vst[i][:]), r=["vst%d" % i], w=["vbb%d" % i])
        P.dma("sp", lambda e, c=c, i=i: e.dma_start(out=ut_scr[c], in_=utb[i][:]), r=["utb%d" % i], w=[("UTd", c)], lane="utb%d" % i)
        P.dma("sp", lambda e, c=c, i=i: e.dma_start(out=v_scr[c], in_=vbb[i][:]), r=["vbb%d" % i], w=[("Vd", c)], lane="vbb%d" % i)
    P.pop()
    P.push()

    wq_bf = P.sb("wq_bf", [128, 8, 2048], BF16)
    for kc in range(8):
        P.dma("sp", lambda e, kc=kc: e.dma_start(out=big[:, :], in_=wq_d[kc * 128:(kc + 1) * 128, :]), w=["big"], lane="big")
        P.act(lambda e, kc=kc: e.activation(wq_bf[:, kc, :], big[:, :], AF.Copy), r=["big"], w=["wq_bf"])
    K12 = P.sb("K12", [128, 16, 128], BF16)
    for s_, kd in enumerate((k1_d, k2_d)):
        for h in range(8):
            P.dma("sp", lambda e, kd=kd, h=h: e.dma_start(out=big[:, 0:128], in_=kd[h]), w=["big"], lane="big")
            bk, bu = bank()
            P.pe(lambda e, bk=bk: e.transpose(bk[:, 0:128], big[:, 0:128], ident[:, :]), r=["big", "ident"], w=[bu])
            P.dve(lambda e, bk=bk, s_=s_, h=h: e.tensor_copy(K12[:, s_ * 8 + h, :], bk[:, 0:128]), r=[bu], w=["K12"])
    P.dma("sp", lambda e: e.dma_start(out=LG1[:], in_=ln2g_d.partition_broadcast(128)), w=["LG1"], lane="LG1")
    P.dma("sp", lambda e: e.dma_start(out=LB1[:], in_=ln2b_d.partition_broadcast(128)), w=["LB1"], lane="LB1")
    for half in range(2):
        bk, bu = bank()
        for c in range(4):
            P.pe(lambda e, bk=bk, c=c, half=half: e.transpose(bk[0:4, c * 128:(c + 1) * 128], MOD[:, 40 + half * 4 + c, :], ident[:, :]),
                 r=["MOD", "ident"], w=[bu])
        P.dve(lambda e, bk=bk, half=half: e.tensor_copy(GROW[:, 0, half * 512:(half + 1) * 512], bk[0:4, :]), r=[bu], w=["GROW"])

    G2 = P.sb("G2", [128, D], F32)
    xs = P.sb("xs", [128, 2, D], F32)
    xn2 = P.sb("xn2", [128, D], F32)
    h2T = P.sb("h2T", [128, 8, NT], BF16)
    qT = P.sb("qT", [128, 16, NT], BF16)
    SC = P.sb("SC", [128, 16, 128], F32)
    SCm = P.sb("SCm", [128, 256], F32)
    TV = P.sb("TV", [128, 16, 16], F32)
    TI = P.sb("TI", [128, 16, 16], U32)
    TIf = P.sb("TIf", [128, 16, 16], F32)
    CAND = SC[:].rearrange("p a b -> p (a b)").rearrange("p (h c) -> p h c", h=8)
    SV = P.sb("SV", [128, 8, 16], F32)
    CI = P.sb("CI", [128, 8, 16], U32)
    CIf = P.sb("CIf", [128, 8, 16], F32)
    JS = P.sb("JS", [128, 8, 16], F32)
    IS = P.sb("IS", [128, 8, 16], F32)
    EQ = P.sb("EQ", [128, 8, 16, 16], BF16)
    ASEL = P.sb("ASEL", [128, 8, 16], F32)
    BSEL = P.sb("BSEL", [128, 8, 16], F32)
    GATE = P.sb("GATE", [128, 8, 16], F32)
    ssum = P.sb("ssum", [128, 8], F32)
    ATt = P.sb("ATt", [128, 128], F32)
    BTt = P.sb("BTt", [128, 128], F32)
    GTt = P.sb("GTt", [128, 128], F32)
    iota16 = P.sb("iota16", [128, 16], F32)
    iota3 = P.sb("iota3", [128, 8, 128], BF16)
    thr16 = P.sb("thr16", [128, 16], F32)
    P.pool(lambda e: e.iota(thr16[:], [[16, 16]], base=16, channel_multiplier=0, allow_small_or_imprecise_dtypes=True), w=["thr16"])
    P.pool(lambda e: e.iota(iota16[:], [[1, 16]], base=0, channel_multiplier=0, allow_small_or_imprecise_dtypes=True), w=["iota16"])
    P.pool(lambda e: e.iota(iota3[:], [[0, 8], [1, 128]], base=0, channel_multiplier=0, allow_small_or_imprecise_dtypes=True), w=["iota3"])
    OA = [P.sb("OA%d" % i, [128, 8, 128], BF16) for i in range(2)]
    OB = [P.sb("OB%d" % i, [128, 8, 128], BF16) for i in range(2)]
    GG = P.sb("GG", [128, 128, NT], BF16)
    UTb = [P.sb("UTb%d" % i, [128, 8, 128], BF16) for i in range(NBUF)]
    Vb = [P.sb("Vb%d" % i, [128, D], BF16) for i in range(NBUF)]
    gz = [P.sb("gz%d" % i, [128, NT], F32) for i in range(2)]
    actT = [P.sb("actT%d" % i, [128, NT], BF16) for i in range(2)]

    nrot[0] = 4
    nst_run = nblk_run * TB // NT
    total_g = nb_run * nst_run * 128

    def issue_load(g):
        c = g % 128
        i = g % NBUF
        P.dma("sp", lambda e: e.dma_start(out=UTb[i][:], in_=ut_scr[c]), r=[("UTd", c)], w=["UTb%d" % i], lane="UTb%d" % i)
        P.dma("sp", lambda e: e.dma_start(out=Vb[i][:], in_=v_scr[c]), r=[("Vd", c)], w=["Vb%d" % i], lane="Vb%d" % i)

    gctr = 0
    for g in range(min(NBUF - 1, total_g)):
        issue_load(g)

    for b in range(nb_run):
        for half in range(2):
            bk, bu = bank()
            P.pe(lambda e, bk=bk, b=b, half=half: e.matmul(bk[:, :], SEL[:, b, :], GROW[:, 0, half * 512:(half + 1) * 512],
                                                           start=True, stop=True), r=["SEL", "GROW"], w=[bu])
            P.act(lambda e, bk=bk, half=half: e.activation(G2[:, half * 512:(half + 1) * 512], bk[:, :], AF.Copy), r=[bu], w=["G2"])
        for st in range(nst_run):
            t0 = st * NT
            for j in range(2):
                tl = t0 // TB + j
                P.dma("sp", lambda e, j=j, tl=tl, b=b: e.dma_start(out=xs[:, j, :], in_=x1_d[b, tl * TB:(tl + 1) * TB, :]),
                      r=[("x1d", b, tl)], w=[("xs", j)], lane=("xs", j))
                for hh in range(2):
                    P.dve(lambda e, hh=hh, j=j: e.bn_stats(st6[:, hh, :], xs[:, j, hh * 512:(hh + 1) * 512]), r=[("xs", j)], w=["st6"])
                P.dve(lambda e: e.bn_aggr(mv[:], st6[:].rearrange("p a b -> p (a b)")), r=["st6"], w=["mv"])
                rsqrt(rstd[:], mv[:, 1:2], LN_EPS_, ["mv"], "rstd")
                P.dve(lambda e, j=j: e.tensor_scalar(xn2[:], xs[:, j, :], mv[:, 0:1], rstd[:, 0:1], ALU.subtract, ALU.mult),
                      r=[("xs", j), "mv", "rstd"], w=["xn2"])
                for half in range(2):
                    bk, bu = bank()
                    for q in range(4):
                        fc = half * 4 + q
                        P.pe(lambda e, bk=bk, q=q, fc=fc: e.transpose(bk[:, q * 128:(q + 1) * 128], xn2[:, fc * 128:(fc + 1) * 128], ident[:, :]),
                             r=["xn2", "ident"], w=[bu])
                    for q in range(4):
                        fc = half * 4 + q
                        P.act(lambda e, bk=bk, q=q, fc=fc, b=b, j=j: e.activation(
                            h2T[:, fc, j * 128:(j + 1) * 128], bk[:, q * 128:(q + 1) * 128], AF.Identity,
                            bias=MOD[:, 24 + fc, b:b + 1], scale=MOD[:, 32 + fc, b:b + 1]), r=[bu, "MOD"], w=["h2T"])
            for m in range(16):
                bk, bu = bank()
                for kc in range(8):
                    P.pe(lambda e, bk=bk, kc=kc, m=m: e.matmul(bk[:, 0:NT], wq_bf[:, kc, m * 128:(m + 1) * 128], h2T[:, kc, :],
                                                               start=(kc == 0), stop=(kc == 7)), r=["wq_bf", "h2T"], w=[bu])
                if m % 2 == 0:
                    P.act(lambda e, bk=bk, m=m: e.activation(qT[:, m, :], bk[:, 0:NT], AF.Copy), r=[bu], w=["qT"])
                else:
                    P.dve(lambda e, bk=bk, m=m: e.tensor_copy(qT[:, m, :], bk[:, 0:NT]), r=[bu], w=["qT"])
            for j in range(2):
                js = slice(j * 128, (j + 1) * 128)
                for grp in range(4):
                    bk, bu = bank()
                    for q in range(4):
                        hs_ = grp * 4 + q
                        h, s_ = hs_ // 2, hs_ % 2
                        P.pe(lambda e, bk=bk, q=q, hs_=hs_, h=h, s_=s_: e.matmul(bk[:, q * 128:(q + 1) * 128], qT[:, hs_, js], K12[:, s_ * 8 + h, :],
                                                                               start=True, stop=True), r=["qT", "K12"], w=[bu])
                    P.act(lambda e, bk=bk, grp=grp: e.activation(SC[:, grp * 4:(grp + 1) * 4, :], bk[:, :].rearrange("p (q n) -> p q n", q=4), AF.Copy),
                          r=[bu], w=["SC"])
                for hs_ in range(16):
                    P.dve(lambda e, hs_=hs_: e.max(TV[:, hs_, 0:8], SC[:, hs_, :]), r=["SC"], w=["TV"])
                    P.dve(lambda e, hs_=hs_: e.max_index(TI[:, hs_, 0:8], TV[:, hs_, 0:8], SC[:, hs_, :]), r=["SC", "TV"], w=["TI"])
                    P.dve(lambda e, hs_=hs_: e.match_replace(SCm[:, 0:128], TV[:, hs_, 0:8], SC[:, hs_, :], -1e30), r=["SC", "TV"], w=["SCm"])
                    P.dve(lambda e, hs_=hs_: e.max(TV[:, hs_, 8:16], SCm[:, 0:128]), r=["SCm"], w=["TV"])
                    P.dve(lambda e, hs_=hs_: e.max_index(TI[:, hs_, 8:16], TV[:, hs_, 8:16], SCm[:, 0:128]), r=["SCm", "TV"], w=["TI"])
                P.dve(lambda e: e.tensor_copy(TIf[:], TI[:]), r=["TI"], w=["TIf"])
                TV4 = TV[:].rearrange("p (h s) k -> p h s k", s=2)
                TI4 = TIf[:].rearrange("p (h s) k -> p h s k", s=2)
                P.dve(lambda e: e.tensor_tensor(CAND.rearrange("p h (i j) -> p h i j", i=16),
                                                TV4[:, :, 0, :].unsqueeze(3).broadcast_to([128, 8, 16, 16]),
                                                TV4[:, :, 1, :].unsqueeze(2).broadcast_to([128, 8, 16, 16]), ALU.add), r=["TV"], w=["SC"])
                for h in range(8):
                    P.dve(lambda e, h=h: e.max(SV[:, h, 0:8], CAND[:, h, :]), r=["SC"], w=["SV"])
                    P.dve(lambda e, h=h: e.max_index(CI[:, h, 0:8], SV[:, h, 0:8], CAND[:, h, :]), r=["SC", "SV"], w=["CI"])
                    P.dve(lambda e, h=h: e.match_replace(SCm[:, :], SV[:, h, 0:8], CAND[:, h, :], -1e30), r=["SC", "SV"], w=["SCm"])
                    P.dve(lambda e, h=h: e.max(SV[:, h, 8:16], SCm[:, :]), r=["SCm"], w=["SV"])
                    P.dve(lambda e, h=h: e.max_index(CI[:, h, 8:16], SV[:, h, 8:16], SCm[:, :]), r=["SCm", "SV"], w=["CI"])
                P.dve(lambda e: e.tensor_tensor(GATE[:], SV[:], SV[:, :, 0:1].broadcast_to([128, 8, 16]), ALU.subtract), r=["SV"], w=["GATE"])
                P.act(lambda e: e.activation(GATE[:], GATE[:], AF.Exp), r=["GATE"], w=["GATE"])
                P.dve(lambda e: e.tensor_reduce(ssum[:], GATE[:], AX.X, ALU.add), r=["GATE"], w=["ssum"])
                P.dve(lambda e: e.reciprocal(ssum[:], ssum[:]), r=["ssum"], w=["ssum"])
                P.dve(lambda e: e.tensor_tensor(GATE[:], GATE[:], ssum[:].unsqueeze(2).broadcast_to([128, 8, 16]), ALU.mult), r=["GATE", "ssum"], w=["GATE"])
                P.dve(lambda e: e.tensor_copy(CIf[:], CI[:]), r=["CI"], w=["CIf"])
                P.dve(lambda e: e.tensor_tensor(EQ[:], CIf[:].unsqueeze(3).broadcast_to([128, 8, 16, 16]),
                                                thr16[:, :].unsqueeze(1).unsqueeze(1).broadcast_to([128, 8, 16, 16]), ALU.is_ge), r=["CIf", "thr16"], w=["EQ"])
                P.dve(lambda e: e.tensor_reduce(IS[:], EQ[:], AX.X, ALU.add), r=["EQ"], w=["IS"])
                P.dve(lambda e: e.scalar_tensor_tensor(JS[:], IS[:], -16.0, CIf[:], ALU.mult, ALU.add), r=["IS", "CIf"], w=["JS"])
                io4 = iota16[:, :].unsqueeze(1).unsqueeze(1).broadcast_to([128, 8, 16, 16])
                for (selT, sn, s_, dstT, dn) in ((IS, "IS", 0, ASEL, "ASEL"), (JS, "JS", 1, BSEL, "BSEL")):
                    P.dve(lambda e, selT=selT: e.tensor_tensor(EQ[:], selT[:].unsqueeze(3).broadcast_to([128, 8, 16, 16]), io4, ALU.is_equal),
                           r=[sn, "iota16"], w=["EQ"])
                    P.pool(lambda e, s_=s_: e.tensor_tensor(EQ[:], EQ[:], TI4[:, :, s_, :].unsqueeze(2).broadcast_to([128, 8, 16, 16]), ALU.mult),
                           r=["EQ", "TIf"], w=["EQ"])
                    P.dve(lambda e, dstT=dstT: e.tensor_reduce(dstT[:], EQ[:], AX.X, ALU.add), r=["EQ"], w=[dn])
                for (srcT, sn, dstT, dn) in ((ASEL, "ASEL", ATt, "ATt"), (BSEL, "BSEL", BTt, "BTt"), (GATE, "GATE", GTt, "GTt")):
                    bk, bu = bank()
                    P.pe(lambda e, bk=bk, srcT=srcT: e.transpose(bk[:, 0:128], srcT[:].rearrange("p h k -> p (h k)"), ident[:, :]),
                         r=[sn, "ident"], w=[bu])
                    P.act(lambda e, bk=bk, dstT=dstT: e.activation(dstT[:], bk[:, 0:128], AF.Copy), r=[bu], w=[dn])
                for tg in range(16):
                    i = tg % 2
                    ts_ = slice(tg * 8, tg * 8 + 8)
                    P.dve(lambda e, ts_=ts_, i=i: e.tensor_tensor(OA[i][:], iota3[:], ATt[:, ts_].unsqueeze(2).broadcast_to([128, 8, 128]), ALU.is_equal),
                           r=["iota3", "ATt"], w=["OA%d" % i])
                    P.pool(lambda e, ts_=ts_, i=i: e.tensor_tensor(OA[i][:], OA[i][:], GTt[:, ts_].unsqueeze(2).broadcast_to([128, 8, 128]), ALU.mult),
                           r=["OA%d" % i, "GTt"], w=["OA%d" % i])
                    P.dve(lambda e, ts_=ts_, i=i: e.tensor_tensor(OB[i][:], iota3[:], BTt[:, ts_].unsqueeze(2).broadcast_to([128, 8, 128]), ALU.is_equal),
                          r=["iota3", "BTt"], w=["OB%d" % i])
                    for q2 in range(2):
                        bk, bu = bank()
                        for q in range(4):
                            tt = q2 * 4 + q
                            P.pe(lambda e, bk=bk, q=q, tt=tt, i=i: e.matmul(bk[:, q * 128:(q + 1) * 128], OB[i][:, tt, :], OA[i][:, tt, :],
                                                                          start=True, stop=True), r=["OA%d" % i, "OB%d" % i], w=[bu])
                        tb_ = j * 128 + tg * 8 + q2 * 4
                        P.act(lambda e, bk=bk, tb_=tb_: e.activation(GG[:, :, tb_:tb_ + 4].rearrange("p i t -> p t i"),
                                                                     bk[:, :].rearrange("p (t i) -> p t i", t=4), AF.Copy), r=[bu], w=["GG"])
            ybanks = [(banks[4 + k], ("B", 4 + k)) for k in range(4)]
            for c in range(128):
                g = gctr
                gctr += 1
                if g + NBUF - 1 < total_g:
                    issue_load(g + NBUF - 1)
                i = g % NBUF
                pz = c % 2
                bk, bu = bank()
                for dc in range(8):
                    P.pe(lambda e, bk=bk, dc=dc, i=i: e.matmul(bk[:, 0:NT], UTb[i][:, dc, :], h2T[:, dc, :], start=(dc == 0), stop=(dc == 7)),
                         r=["UTb%d" % i, "h2T"], w=[bu])
                P.act(lambda e, bk=bk, pz=pz: e.activation(gz[pz][:], bk[:, 0:NT], AF.Gelu), r=[bu], w=["gz%d" % pz])
                eng = P.dve if c % 2 == 0 else P.pool
                eng(lambda e, pz=pz, c=c: e.tensor_tensor(actT[pz][:], gz[pz][:], GG[:, c, :], ALU.mult), r=["gz%d" % pz, "GG"], w=["actT%d" % pz])
                for j in range(2):
                    for half in range(2):
                        yb, yu = ybanks[j * 2 + half]
                        P.pe(lambda e, yb=yb, j=j, half=half, pz=pz, i=i, c=c: e.matmul(
                            yb[:, :], actT[pz][:, j * 128:(j + 1) * 128], Vb[i][:, half * 512:(half + 1) * 512],
                            start=(c == 0), stop=(c == 127)), r=["actT%d" % pz, "Vb%d" % i], w=[yu])
            for j in range(2):
                tl = t0 // TB + j
                for half in range(2):
                    yb, yu = ybanks[j * 2 + half]
                    hs = slice(half * 512, (half + 1) * 512)
                    P.dve(lambda e, yb=yb, hs=hs: e.tensor_tensor(xn2[:, hs], yb[:, :], G2[:, hs], ALU.mult), r=[yu, "G2"], w=["xn2"])
                P.dve(lambda e, j=j: e.scalar_tensor_tensor(xn2[:], xs[:, j, :], ALPHA_, xn2[:], ALU.mult, ALU.add), r=[("xs", j), "xn2"], w=["xn2"])
                for hh in range(2):
                    P.dve(lambda e, hh=hh: e.bn_stats(st6[:, hh, :], xn2[:, hh * 512:(hh + 1) * 512]), r=["xn2"], w=["st6"])
                P.dve(lambda e: e.bn_aggr(mv[:], st6[:].rearrange("p a b -> p (a b)")), r=["st6"], w=["mv"])
                rsqrt(rstd[:], mv[:, 1:2], LN_EPS_, ["mv"], "rstd")
                P.dve(lambda e: e.tensor_scalar(xn2[:], xn2[:], mv[:, 0:1], rstd[:, 0:1], ALU.subtract, ALU.mult), r=["xn2", "mv", "rstd"], w=["xn2"])
                P.pool(lambda e: e.tensor_tensor(xn2[:], xn2[:], LG1[:], ALU.mult), r=["xn2", "LG1"], w=["xn2"])
                P.pool(lambda e: e.tensor_tensor(xn2[:], xn2[:], LB1[:], ALU.add), r=["xn2", "LB1"], w=["xn2"])
                o = P.dma("sp", lambda e, b=b, tl=tl: e.dma_start(out=out_d[b, tl * TB:(tl + 1) * TB, :], in_=xn2[:]),
                          r=["xn2"], w=[("x1d", b, tl)], lane="xn2_out")
                out_ops.append(o)
    P.finish(final_ops=out_ops[-1:])
    return nc, P


_NAMES = ["x", "c", "cond_w", "cond_b", "w_in", "mu_shift", "conv_w", "conv_b", "conv_ln_g", "conv_ln_b",
          "rw_w0", "rw_w2", "rw_a0", "rw_a2", "rw_g2", "rw_kk", "rw_ka", "rw_rk", "rw_lnx_g", "rw_lnx_b",
          "w_out", "ln1_g", "ln1_b", "peer_wq", "peer_k1", "peer_k2", "peer_u", "peer_v", "ln2_g", "ln2_b"]


def kernel(**inputs):
    from concourse.bass_utils import run_bass_kernel_spmd
    nc, P = build()
    in_maps = []
    for i in range(8):
        m = {}
        for k in _NAMES:
            v = np.ascontiguousarray(np.asarray(inputs[k], dtype=np.float32))
            if k in ("x", "c"):
                v = np.ascontiguousarray(v[i * NB:(i + 1) * NB])
            m[k] = v
        in_maps.append(m)
    res = run_bass_kernel_spmd(nc, in_maps, core_ids=list(range(8)))
    return np.concatenate([np.asarray(r["out"]) for r in res.results], axis=0).astype(np.float32)
```

```python
import contextlib
import numpy as np
import concourse.bass as bass
import concourse.mybir as mybir

F32 = mybir.dt.float32
BF16 = mybir.dt.bfloat16
U32 = mybir.dt.uint32
I32 = mybir.dt.int32
AF = mybir.ActivationFunctionType
ALU = mybir.AluOpType
AX = mybir.AxisListType


class Op:
    __slots__ = ("eng", "dma", "lane", "lane_val", "sig", "sigval")

    def __init__(self, eng, dma):
        self.eng = eng
        self.dma = dma
        self.lane = None
        self.lane_val = 0
        self.sig = False
        self.sigval = 0


class Prog:
    ENGS = ("pe", "act", "dve", "pool", "sp")

    def __init__(self, nc, flags=None):
        self.nc = nc
        self.dry = flags is None
        self.flags = flags
        self.stack = [contextlib.ExitStack()]
        self.lastw = {}
        self.readers = {}
        self.lanes = {}
        self.alias = {}
        self.allops = []
        self.cnt = {e: 0 for e in self.ENGS}
        self.waited = {e: {} for e in self.ENGS}
        self.engobj = {"pe": nc.tensor, "act": nc.scalar, "dve": nc.vector, "pool": nc.gpsimd, "sp": nc.sync}
        if not self.dry:
            self.K = 12
            self.esem = {e: [self.sem("E%s%d" % (e, i)) for i in range(self.K)] for e in self.ENGS if e != "sp"}

    def push(self):
        self.stack.append(contextlib.ExitStack())

    def pop(self):
        self.stack.pop().close()

    def sb(self, name, shape, dt):
        return self.stack[-1].enter_context(self.nc.sbuf_tensor(name, list(shape), dt))

    def ps(self, name, shape, dt=F32):
        return self.stack[-1].enter_context(self.nc.psum_tensor(name, list(shape), dt))

    def sem(self, name):
        return self.stack[0].enter_context(self.nc.semaphore(name))

    def op(self, eng, fn, r=(), w=(), dma=False, lane=None):
        o = Op(eng, dma)
        al = self.alias
        r = [al.get(u, u) for u in r]
        w = [al.get(u, u) for u in w]
        deps = set()
        for u in r:
            lw = self.lastw.get(u)
            if lw is not None:
                deps.add(lw)
        for u in w:
            lw = self.lastw.get(u)
            if lw is not None:
                deps.add(lw)
            for rd in self.readers.get(u, {}).values():
                deps.add(rd)
        for u in w:
            self.lastw[u] = o
            self.readers[u] = {}
        for u in r:
            if u not in w:
                self.readers.setdefault(u, {})[(eng, len(self.allops)) if dma else eng] = o
        idx = len(self.allops)
        self.allops.append(o)
        if self.dry:
            for d in deps:
                if not (d.eng == eng and eng == "pe" and not d.dma):
                    d.sig = True
            if dma:
                self.lanes.setdefault(lane, 0)
            return o
        e = self.engobj[eng]
        wd = self.waited[eng]
        for d in deps:
            if d.dma:
                s, v = d.lane, d.lane_val
            else:
                if d.eng == eng and eng == "pe":
                    continue
                s, v = d.sigval
            k = id(s)
            if wd.get(k, 0) >= v:
                continue
            wd[k] = v
            e.wait_ge(s, v)
        ins = fn(e)
        if dma:
            if lane not in self.lanes:
                self.lanes[lane] = [self.sem("L%d" % len(self.lanes)), 0]
            L = self.lanes[lane]
            L[1] += 16
            o.lane, o.lane_val = L[0], L[1]
            ins.then_inc(L[0], 16)
        elif self.flags[idx]:
            n = self.cnt[eng]
            self.cnt[eng] += 1
            sm = self.esem[eng][n % self.K]
            o.sigval = (sm, n // self.K + 1)
            ins.then_inc(sm, 1)
        return o

    def pe(self, fn, r=(), w=()):
        return self.op("pe", fn, r, w)

    def act(self, fn, r=(), w=()):
        return self.op("act", fn, r, w)

    def dve(self, fn, r=(), w=()):
        return self.op("dve", fn, r, w)

    def pool(self, fn, r=(), w=()):
        return self.op("pool", fn, r, w)

    def dma(self, q, fn, r=(), w=(), lane=None):
        return self.op(q, fn, r, w, dma=True, lane=lane)

    def finish(self, final_ops=()):
        if not self.dry:
            for o in final_ops:
                self.nc.sync.wait_ge(o.lane, o.lane_val)
        while self.stack:
            self.stack.pop().close()

D = 1024
SEQ = 2048
NB = 4
TB = 128
NBLK = SEQ // TB
CH = 64
ALPHA_ = (2.0) ** 0.25
LN_EPS_ = 1e-5
GN_EPS_ = 64e-5
LWC = 0.6065306597126334


def build(nb_run=NB, nblk_run=NBLK, stage="AB"):
    nc1 = bass.Bass("TRN2", target_bir_lowering=False)
    P1 = Prog(nc1)
    body(nc1, P1, nb_run, nblk_run, stage)
    flags = [o.sig for o in P1.allops]
    nc = bass.Bass("TRN2", target_bir_lowering=False)
    P = Prog(nc, flags)
    body(nc, P, nb_run, nblk_run, stage)
    return nc, P


def body(nc, P, nb_run, nblk_run, stage):
    dt = nc.dram_tensor
    x_d = dt("x", [NB, SEQ, D], F32, kind="ExternalInput").ap()
    c_d = dt("c", [NB, D], F32, kind="ExternalInput").ap()
    cond_w_d = dt("cond_w", [D, 6 * D], F32, kind="ExternalInput").ap()
    cond_b_d = dt("cond_b", [6 * D], F32, kind="ExternalInput").ap()
    w_in_d = dt("w_in", [D, 2816], F32, kind="ExternalInput").ap()
    mu_d = dt("mu_shift", [1792], F32, kind="ExternalInput").ap()
    conv_w_d = dt("conv_w", [31, 512], F32, kind="ExternalInput").ap()
    conv_b_d = dt("conv_b", [512], F32, kind="ExternalInput").ap()
    cg_d = dt("conv_ln_g", [512], F32, kind="ExternalInput").ap()
    cb_d = dt("conv_ln_b", [512], F32, kind="ExternalInput").ap()
    w0_d = dt("rw_w0", [512], F32, kind="ExternalInput").ap()
    w2_d = dt("rw_w2", [64, 512], F32, kind="ExternalInput").ap()
    a0_d = dt("rw_a0", [512], F32, kind="ExternalInput").ap()
    a2_d = dt("rw_a2", [64, 512], F32, kind="ExternalInput").ap()
    g2_d = dt("rw_g2", [128, 512], F32, kind="ExternalInput").ap()
    kk_d = dt("rw_kk", [512], F32, kind="ExternalInput").ap()
    ka_d = dt("rw_ka", [512], F32, kind="ExternalInput").ap()
    rk_d = dt("rw_rk", [8, 64], F32, kind="ExternalInput").ap()
    lg_d = dt("rw_lnx_g", [512], F32, kind="ExternalInput").ap()
    lb_d = dt("rw_lnx_b", [512], F32, kind="ExternalInput").ap()
    w_out_d = dt("w_out", [D, D], F32, kind="ExternalInput").ap()
    ln1g_d = dt("ln1_g", [D], F32, kind="ExternalInput").ap()
    ln1b_d = dt("ln1_b", [D], F32, kind="ExternalInput").ap()
    wq_d = dt("peer_wq", [D, 2048], F32, kind="ExternalInput").ap()
    k1_d = dt("peer_k1", [8, 128, 128], F32, kind="ExternalInput").ap()
    k2_d = dt("peer_k2", [8, 128, 128], F32, kind="ExternalInput").ap()
    pu_d = dt("peer_u", [16384, D], F32, kind="ExternalInput").ap()
    pv_d = dt("peer_v", [16384, D], F32, kind="ExternalInput").ap()
    ln2g_d = dt("ln2_g", [D], F32, kind="ExternalInput").ap()
    ln2b_d = dt("ln2_b", [D], F32, kind="ExternalInput").ap()
    out_d = dt("out", [NB, SEQ, D], F32, kind="ExternalOutput").ap()

    banks = [P.ps("bank%d" % i, [128, 512], F32) for i in range(8)]
    bctr = [0]

    nrot = [8]

    def bank():
        i = bctr[0] % nrot[0]
        bctr[0] += 1
        return banks[i], ("B", i)

    def rsqrt(dst, src, eps, r, w, floor=None):
        P.act(lambda e: e.activation(dst, src, AF.Sqrt, bias=epsb[0:dst.shape[0], ekey[eps]:ekey[eps] + 1]), r=list(r) + ["epsb"], w=[w])
        if floor is not None:
            P.dve(lambda e: e.tensor_scalar_max(dst, dst, floor), r=[w], w=[w])
        P.dve(lambda e: e.reciprocal(dst, dst), r=[w], w=[w])

    epsb = P.sb("epsb", [128, 4], F32)
    ekey = {LN_EPS_: 0, GN_EPS_: 1, 0.0: 2}
    P.pool(lambda e: e.memset(epsb[:, 0:1], LN_EPS_), w=["epsb"])
    P.pool(lambda e: e.memset(epsb[:, 1:2], GN_EPS_), w=["epsb"])
    P.pool(lambda e: e.memset(epsb[:, 2:3], 0.0), w=["epsb"])
    ident = P.sb("ident", [128, 128], F32)
    identb = P.sb("identb", [128, 128], BF16)
    iota_p = P.sb("iota_p", [128, 1], F32)
    iota_f = P.sb("iota_f", [128, 128], F32)
    P.pool(lambda e: e.iota(iota_p[:], [[0, 1]], base=0, channel_multiplier=1,
                            allow_small_or_imprecise_dtypes=True), w=["iota_p"])
    P.pool(lambda e: e.iota(iota_f[:], [[1, 128]], base=0, channel_multiplier=0,
                            allow_small_or_imprecise_dtypes=True), w=["iota_f"])
    P.dve(lambda e: e.tensor_scalar(ident[:], iota_f[:], iota_p[:, 0:1], None, ALU.is_equal),
          r=["iota_p", "iota_f"], w=["ident"])
    P.dve(lambda e: e.tensor_copy(identb[:], ident[:]), r=["ident"], w=["identb"])
    m_st = P.sb("m_st", [64, 64], F32)
    m_in = P.sb("m_in", [64, 64], F32)
    m_lo = P.sb("m_lo", [64, 64], F32)
    P.dve(lambda e: e.tensor_scalar(m_st[:], iota_f[0:64, 0:64], iota_p[0:64, 0:1], None, ALU.is_gt),
          r=["iota_p", "iota_f"], w=["m_st"])
    P.dve(lambda e: e.tensor_scalar(m_in[:], iota_f[0:64, 0:64], iota_p[0:64, 0:1], None, ALU.is_ge),
          r=["iota_p", "iota_f"], w=["m_in"])
    P.dve(lambda e: e.tensor_scalar(m_lo[:], iota_f[0:64, 0:64], iota_p[0:64, 0:1], None, ALU.is_lt),
          r=["iota_p", "iota_f"], w=["m_lo"])
    ones_c = P.sb("ones_c", [128, 128], F32)
    ones_h = P.sb("ones_h", [64, 64], F32)
    ones_g = P.sb("ones_g", [64, 64], F32)
    P.pool(lambda e: e.memset(ones_c[:], 1.0 / 512.0), w=["ones_c"])
    P.pool(lambda e: e.memset(ones_h[:], 1.0), w=["ones_h"])
    P.pool(lambda e: e.memset(ones_g[:], 1.0 / 64.0), w=["ones_g"])
    rmask = P.sb("rmask", [64, 2, 64], F32)
    P.pool(lambda e: e.memset(rmask[:], 1.0), w=["rmask"])
    P.pool(lambda e: e.memset(rmask[:, :, 0:1], 0.0), w=["rmask"])

    big = P.sb("big", [128, 2048], F32)
    stA = P.sb("stA", [64, 128], F32)
    stB = P.sb("stB", [88, 64], F32)
    PA = P.sb("PA", [128, 64], F32)
    PB = P.sb("PB", [64, 88], F32)
    P.pool(lambda e: e.memset(stA[:], 0.0), w=["stA"])
    P.pool(lambda e: e.memset(stB[:], 0.0), w=["stB"])
    rowsA = [(cond_b_d, 0, 48), (conv_b_d, 48, 4), (cg_d, 52, 4), (cb_d, 56, 4)]
    for (src, r0, n) in rowsA:
        P.dma("sp", (lambda e, src=src, r0=r0, n=n: e.dma_start(
            out=stA[r0:r0 + n, :], in_=src.rearrange("(c p) -> c p", p=128))), w=["stA"], lane="stA")
    P.dma("sp", lambda e: e.dma_start(out=stA[60:61, :], in_=mu_d[1664:1792].rearrange("(c p) -> c p", p=128)),
          w=["stA"], lane="stA")
    rowsB = [(mu_d[0:1536], 0, 24), (w0_d, 24, 8), (a0_d, 32, 8), (kk_d, 40, 8), (ka_d, 48, 8),
             (lg_d, 56, 8), (lb_d, 64, 8), (mu_d[1536:1664], 80, 2)]
    for (src, r0, n) in rowsB:
        P.dma("sp", (lambda e, src=src, r0=r0, n=n: e.dma_start(
            out=stB[r0:r0 + n, :], in_=src.rearrange("(c p) -> c p", p=64))), w=["stB"], lane="stB")
    P.dma("sp", lambda e: e.dma_start(out=stB[72:80, :], in_=rk_d), w=["stB"], lane="stB")
    bk, bu = bank()
    P.pe(lambda e: e.transpose(bk[:, 0:64], stA[:, :], ident[0:64, 0:64]), r=["stA", "ident"], w=[bu])
    P.dve(lambda e: e.tensor_copy(PA[:], bk[:, 0:64]), r=[bu], w=["PA"])
    bk2, bu2 = bank()
    P.pe(lambda e: e.transpose(bk2[0:64, 0:88], stB[:, :], ident[0:88, 0:88]), r=["stB", "ident"], w=[bu2])
    P.dve(lambda e: e.tensor_copy(PB[:], bk2[0:64, 0:88]), r=[bu2], w=["PB"])
    OMB = P.sb("OMB", [64, 88], F32)
    P.dve(lambda e: e.tensor_scalar(OMB[:], PB[:], -1.0, 1.0, ALU.mult, ALU.add), r=["PB"], w=["OMB"])
    OMA = P.sb("OMA", [128, 64], F32)
    P.dve(lambda e: e.tensor_scalar(OMA[:], PA[:], -1.0, 1.0, ALU.mult, ALU.add), r=["PA"], w=["OMA"])
    CW = P.sb("CW", [128, 4, 31], F32)
    P.dma("sp", lambda e: e.dma_start(out=big[0:31, 0:512], in_=conv_w_d), w=["big"], lane="big")
    for c in range(4):
        bk, bu = bank()
        P.pe(lambda e, bk=bk, c=c: e.transpose(bk[:, 0:31], big[0:31, c * 128:(c + 1) * 128], ident[0:31, 0:31]),
             r=["big", "ident"], w=[bu])
        P.dve(lambda e, bk=bk, c=c: e.tensor_copy(CW[:, c, :], bk[:, 0:31]), r=[bu], w=["CW"])

    siluT = P.sb("siluT", [128, 8, 4], F32)
    MOD = P.sb("MOD", [128, 48, 4], F32)
    P.dma("sp", lambda e: e.dma_start(out=big[0:4, 0:D], in_=c_d), w=["big"], lane="big")
    bk, bu = bank()
    for kc in range(8):
        P.pe(lambda e, bk=bk, kc=kc: e.transpose(bk[:, kc * 4:(kc + 1) * 4], big[0:4, kc * 128:(kc + 1) * 128],
                                                 ident[0:4, 0:4]), r=["big", "ident"], w=[bu])
    P.act(lambda e, bk=bk: e.activation(siluT[:].rearrange("p a b -> p (a b)"), bk[:, 0:32], AF.Silu),
          r=[bu], w=["siluT"])
    bkm, bum = bank()
    for kc in range(8):
        for q in range(3):
            P.dma("sp", (lambda e, kc=kc, q=q: e.dma_start(
                out=big[:, :], in_=cond_w_d[kc * 128:(kc + 1) * 128, q * 2048:(q + 1) * 2048])), w=["big"], lane="big")
            for mm_ in range(16):
                m = q * 16 + mm_
                P.pe(lambda e, kc=kc, m=m, mm_=mm_: e.matmul(bkm[:, m * 4:(m + 1) * 4], big[:, mm_ * 128:(mm_ + 1) * 128],
                                                    siluT[:, kc, :], start=(kc == 0 and m == 0),
                                                    stop=(kc == 7 and m == 47), skip_group_check=True),
                     r=["big", "siluT"], w=[bum])
    P.dve(lambda e: e.tensor_tensor(MOD[:], bkm[:, 0:192].rearrange("p (m b) -> p m b", b=4),
                                    PA[:, 0:48].unsqueeze(2).broadcast_to([128, 48, 4]), ALU.add),
          r=[bum, "PA"], w=["MOD"])
    for lo in (8, 32):
        P.dve(lambda e, lo=lo: e.tensor_scalar_add(MOD[:, lo:lo + 8, :], MOD[:, lo:lo + 8, :], 1.0),
              r=["MOD"], w=["MOD"])
    GROW = P.sb("GROW", [4, 1, D], F32)
    for gi, lo in enumerate((16,)):
        for half in range(2):
            bk, bu = bank()
            for c in range(4):
                P.pe(lambda e, bk=bk, c=c, lo=lo, half=half: e.transpose(
                    bk[0:4, c * 128:(c + 1) * 128], MOD[:, lo + half * 4 + c, :], ident[:, :]),
                    r=["MOD", "ident"], w=[bu])
            P.dve(lambda e, bk=bk, gi=gi, half=half: e.tensor_copy(GROW[:, gi, half * 512:(half + 1) * 512],
                                                                   bk[0:4, :]), r=[bu], w=["GROW"])
    SEL = P.sb("SEL", [4, 4, 128], F32)
    P.dve(lambda e: e.tensor_copy(SEL[:], ident[0:4, 0:4].unsqueeze(2).broadcast_to([4, 4, 128])),
          r=["ident"], w=["SEL"])

    LG1 = P.sb("LG1", [128, D], F32)
    LB1 = P.sb("LB1", [128, D], F32)
    P.dma("sp", lambda e: e.dma_start(out=LG1[:], in_=ln1g_d.partition_broadcast(128)), w=["LG1"], lane="LG1")
    P.dma("sp", lambda e: e.dma_start(out=LB1[:], in_=ln1b_d.partition_broadcast(128)), w=["LB1"], lane="LB1")
    st6 = P.sb("st6", [128, 2, 6], F32)
    mv = P.sb("mv", [128, 2], F32)
    rstd = P.sb("rstd", [128, 1], F32)
    P.push()
    w_in_bf = P.sb("w_in_bf", [128, 8, 2816], BF16)
    for kc in range(8):
        for q in range(2):
            P.dma("sp", lambda e, kc=kc, q=q: e.dma_start(out=big[:, 0:1408], in_=w_in_d[kc * 128:(kc + 1) * 128, q * 1408:(q + 1) * 1408]),
                  w=["big"], lane="big")
            P.act(lambda e, kc=kc, q=q: e.activation(w_in_bf[:, kc, q * 1408:(q + 1) * 1408], big[:, 0:1408], AF.Copy), r=["big"], w=["w_in_bf"])
    w_oc_bf = P.sb("w_oc_bf", [128, 4, D], BF16)
    w_or_bf = P.sb("w_or_bf", [64, 8, D], BF16)
    for c in range(4):
        P.dma("sp", lambda e, c=c: e.dma_start(out=big[:, 0:D], in_=w_out_d[c * 128:(c + 1) * 128, :]),
              w=["big"], lane="big")
        P.dve(lambda e, c=c: e.tensor_copy(w_oc_bf[:, c, :], big[:, 0:D]), r=["big"], w=["w_oc_bf"])
    for h in range(8):
        P.dma("sp", lambda e, h=h: e.dma_start(out=big[0:64, 0:D], in_=w_out_d[512 + h * 64:512 + (h + 1) * 64, :]),
              w=["big"], lane="big")
        P.dve(lambda e, h=h: e.tensor_copy(w_or_bf[:, h, :], big[0:64, 0:D]), r=["big"], w=["w_or_bf"])
    w2_bf = P.sb("w2_bf", [64, 512], BF16)
    a2_bf = P.sb("a2_bf", [64, 512], BF16)
    g2_bf = P.sb("g2_bf", [128, 512], BF16)
    for (src, dst, np_, nm) in ((w2_d, w2_bf, 64, "w2_bf"), (a2_d, a2_bf, 64, "a2_bf"), (g2_d, g2_bf, 128, "g2_bf")):
        P.dma("sp", lambda e, src=src, np_=np_: e.dma_start(out=big[0:np_, 0:512], in_=src), w=["big"], lane="big")
        P.dve(lambda e, dst=dst, np_=np_: e.tensor_copy(dst[:], big[0:np_, 0:512]), r=["big"], w=[nm])

    x1_d = out_d

    xb = P.sb("xb", [128, D], F32)
    xn = P.sb("xn", [128, D], F32)
    hT = [P.sb("hT%d" % i, [128, 8, TB + 1], BF16) for i in range(2)]
    ub = [P.sb("ub%d" % i, [128, 4, 30 + TB], F32) for i in range(2)]
    sig = P.sb("sig", [128, TB], F32)
    acc = P.sb("acc", [128, 4, TB], F32)
    sq = P.sb("sq", [128, 4, TB], F32)
    cm = P.sb("cm", [128, TB], F32)
    cr = P.sb("cr", [128, TB], F32)
    ct = P.sb("ct", [128, TB], F32)
    ycT = P.sb("ycT", [128, 4, TB], BF16)
    tmpA = P.sb("tmpA", [128, TB], F32)
    F = [P.sb("F%d" % i, [64, 8, TB], F32) for i in range(11)]

    def role(name, idx):
        P.alias[name] = "F%d" % idx
        return F[idx]
    VBb = P.sb("VBb", [64, 8, TB], BF16)
    TW = P.sb("TW", [64, TB], BF16)
    ADb = P.sb("ADb", [64, TB], BF16)
    SGD = P.sb("SGD", [128, TB], BF16)
    RT = P.sb("RT", [64, 8, TB], BF16)
    KT = P.sb("KT", [64, 8, TB], BF16)
    BT = P.sb("BT", [64, 8, TB], BF16)
    ATl = P.sb("ATl", [64, 8, TB], BF16)
    YRW = P.sb("YRW", [64, 8, TB], BF16)
    Vt = P.sb("Vt", [64, 8, 64], BF16)
    Ktm = P.sb("Ktm", [64, 8, 64], BF16)
    Btm = P.sb("Btm", [64, 8, 64], BF16)
    AabT = P.sb("AabT", [64, 8, 64], BF16)
    ArbT = P.sb("ArbT", [64, 8, 64], BF16)
    AakT = P.sb("AakT", [64, 8, 64], BF16)
    ArkT = P.sb("ArkT", [64, 8, 64], BF16)
    Npw = [P.sb("Npw%d" % i, [64, 8, 64], BF16) for i in range(2)]
    NTpw = [P.sb("NTpw%d" % i, [64, 8, 64], BF16) for i in range(2)]
    PT = [P.sb("PT%d" % i, [64, 8, 64], BF16) for i in range(2)]
    Xb = P.sb("Xb", [64, 8, 64], BF16)
    Ub = P.sb("Ub", [64, 8, 64], BF16)
    S0 = P.sb("S0", [64, 8, 64], F32)
    S0b = P.sb("S0b", [64, 8, 64], BF16)
    Stmp = P.sb("Stmp", [64, 8, 64], F32)
    G1 = P.sb("G1", [128, D], F32)
    identb8 = P.sb("identb8", [64, 8, 64], BF16)
    P.dve(lambda e: e.tensor_copy(identb8[:], ident[0:64, 0:64].unsqueeze(1).broadcast_to([64, 8, 64])),
          r=["ident"], w=["identb8"])

    def pbc(col):
        return PB[:, col:col + 8].unsqueeze(2).broadcast_to([64, 8, TB])

    def ombc(col):
        return OMB[:, col:col + 8].unsqueeze(2).broadcast_to([64, 8, TB])

    def m8(m):
        return m[:, :].unsqueeze(1).broadcast_to([64, 8, 64])

    out_ops = []
    for b in range(nb_run):
        for half in range(2):
            bk, bu = bank()
            P.pe(lambda e, bk=bk, b=b, half=half: e.matmul(bk[:, :], SEL[:, b, :], GROW[:, 0, half * 512:(half + 1) * 512],
                                                           start=True, stop=True), r=["SEL", "GROW"], w=[bu])
            P.act(lambda e, bk=bk, half=half: e.activation(G1[:, half * 512:(half + 1) * 512], bk[:, :], AF.Copy),
                  r=[bu], w=["G1"])
        P.pool(lambda e: e.memset(hT[1][:, :, 0:1], 0.0), w=["hT1"])
        P.pool(lambda e: e.memset(ub[1][:, :, 0:30], 0.0), w=["ub1"])
        P.pool(lambda e: e.memset(S0[:], 0.0), w=["S0"])
        P.pool(lambda e: e.memset(S0b[:], 0.0), w=["S0b"])
        for blk in range(nblk_run):
            par = blk % 2
            hTc, hTn = hT[par], hT[1 - par]
            hu, hun = "hT%d" % par, "hT%d" % (1 - par)
            ubc, ubn = ub[par], ub[1 - par]
            uu, uun = "ub%d" % par, "ub%d" % (1 - par)
            if blk == 0:
                P.pool(lambda e: e.memset(hTc[:, :, 0:1], 0.0), w=[hu])
                P.pool(lambda e: e.memset(ubc[:, :, 0:30], 0.0), w=[uu])
            t0 = blk * TB
            P.dma("sp", lambda e, b=b, t0=t0: e.dma_start(out=xb[:], in_=x_d[b, t0:t0 + TB, :]), w=["xb"], lane="xb")
            for hh in range(2):
                P.dve(lambda e, hh=hh: e.bn_stats(st6[:, hh, :], xb[:, hh * 512:(hh + 1) * 512]), r=["xb"], w=["st6"])
            P.dve(lambda e: e.bn_aggr(mv[:], st6[:].rearrange("p a b -> p (a b)")), r=["st6"], w=["mv"])
            rsqrt(rstd[:], mv[:, 1:2], LN_EPS_, ["mv"], "rstd")
            P.dve(lambda e: e.tensor_scalar(xn[:], xb[:], mv[:, 0:1], rstd[:, 0:1], ALU.subtract, ALU.mult),
                  r=["xb", "mv", "rstd"], w=["xn"])
            for half in range(2):
                bk, bu = bank()
                for j in range(4):
                    fc = half * 4 + j
                    P.pe(lambda e, bk=bk, j=j, fc=fc: e.transpose(bk[:, j * 128:(j + 1) * 128],
                                                                  xn[:, fc * 128:(fc + 1) * 128], ident[:, :]),
                         r=["xn", "ident"], w=[bu])
                for j in range(4):
                    fc = half * 4 + j
                    P.act(lambda e, bk=bk, j=j, fc=fc, b=b, hTc=hTc: e.activation(
                        hTc[:, fc, 1:TB + 1], bk[:, j * 128:(j + 1) * 128], AF.Identity,
                        bias=MOD[:, fc, b:b + 1], scale=MOD[:, 8 + fc, b:b + 1]), r=[bu, "MOD"], w=[hu])
            P.pool(lambda e, hTc=hTc, hTn=hTn: e.tensor_copy(hTn[:, :, 0:1], hTc[:, :, TB:TB + 1]), r=[hu], w=[hun])

            RB = role("RB", 0); KB = role("KB", 1); VB = role("VB", 2); GB = role("GB", 3)
            SGW = role("SGW", 4); AB = role("AB", 5); T1 = role("T1", 9)
            def inproj(col0, M, N0):
                bk, bu = bank()
                for kc in range(8):
                    P.pe(lambda e, bk=bk, kc=kc, col0=col0, M=M, N0=N0, hTc=hTc: e.matmul(
                        bk[0:M, 0:TB + 1 - N0], w_in_bf[:, kc, col0:col0 + M], hTc[:, kc, N0:TB + 1],
                        start=(kc == 0), stop=(kc == 7)), r=["w_in_bf", hu], w=[bu])
                return bk, bu

            for c in range(4):
                bkg, bug = inproj(512 + c * 128, 128, 1)
                P.act(lambda e, bkg=bkg: e.activation(sig[:], bkg[:, 0:TB], AF.Sigmoid), r=[bug], w=["sig"])
                bkv, buv = inproj(c * 128, 128, 1)
                P.dve(lambda e, bkv=bkv, c=c, ubc=ubc: e.tensor_tensor(ubc[:, c, 30:30 + TB], bkv[:, 0:TB], sig[:], ALU.mult),
                      r=[buv, "sig"], w=[uu])
            P.pool(lambda e, ubc=ubc, ubn=ubn: e.tensor_copy(ubn[:, :, 0:30], ubc[:, :, TB:TB + 30]), r=[uu], w=[uun])

            def shift_evac(bk, bu, Mp, dst, dname, mucol, omcol):
                P.act(lambda e: e.activation(tmpA[0:Mp, :], bk[0:Mp, 1:TB + 1], AF.Identity, scale=omcol),
                      r=[bu, "OMB", "OMA"], w=["tmpA"])
                P.dve(lambda e: e.scalar_tensor_tensor(dst, bk[0:Mp, 0:TB], mucol, tmpA[0:Mp, :], ALU.mult, ALU.add),
                      r=[bu, "tmpA", "PB", "PA"], w=[dname])

            for wi, (dstT, dn) in enumerate(((RB, "RB"), (KB, "KB"), (VB, "VB"))):
                for h in range(8):
                    bk, bu = inproj(1024 + wi * 512 + h * 64, 64, 0)
                    shift_evac(bk, bu, 64, dstT[:, h, :], dn, PB[:, wi * 8 + h:wi * 8 + h + 1],
                               OMB[:, wi * 8 + h:wi * 8 + h + 1])
            bk, bu = inproj(2560, 64, 0)
            shift_evac(bk, bu, 64, T1[:, 0, :], "T1", PB[:, 80:81], OMB[:, 80:81])
            P.act(lambda e: e.activation(TW[:], T1[:, 0, :], AF.Tanh), r=["T1"], w=["TW"])
            bk, bu = inproj(2624, 64, 0)
            shift_evac(bk, bu, 64, T1[:, 1, :], "T1", PB[:, 81:82], OMB[:, 81:82])
            P.act(lambda e: e.activation(ADb[:], T1[:, 1, :], AF.Copy), r=["T1"], w=["ADb"])
            bk, bu = inproj(2688, 128, 0)
            shift_evac(bk, bu, 128, sq[:, 0, :], "sq", PA[:, 60:61], OMA[:, 60:61])
            P.act(lambda e: e.activation(SGD[:], sq[:, 0, :], AF.Sigmoid), r=["sq"], w=["SGD"])
            for h in range(8):
                bk, bu = bank()
                P.pe(lambda e, bk=bk, h=h: e.matmul(bk[0:64, 0:TB], w2_bf[:, h * 64:(h + 1) * 64], TW[:], start=True, stop=True),
                     r=["w2_bf", "TW"], w=[bu])
                P.act(lambda e, bk=bk, h=h: e.activation(SGW[:, h, :], bk[0:64, 0:TB], AF.Sigmoid, bias=PB[:, 24 + h:25 + h]),
                      r=[bu, "PB"], w=["SGW"])
                bk, bu = bank()
                P.pe(lambda e, bk=bk, h=h: e.matmul(bk[0:64, 0:TB], a2_bf[:, h * 64:(h + 1) * 64], ADb[:], start=True, stop=True),
                     r=["a2_bf", "ADb"], w=[bu])
                P.act(lambda e, bk=bk, h=h: e.activation(AB[:, h, :], bk[0:64, 0:TB], AF.Sigmoid, bias=PB[:, 32 + h:33 + h]),
                      r=[bu, "PB"], w=["AB"])
                bk, bu = bank()
                P.pe(lambda e, bk=bk, h=h: e.matmul(bk[0:64, 0:TB], g2_bf[:, h * 64:(h + 1) * 64], SGD[:], start=True, stop=True),
                     r=["g2_bf", "SGD"], w=[bu])
                P.act(lambda e, bk=bk, h=h: e.activation(GB[:, h, :], bk[0:64, 0:TB], AF.Copy), r=[bu], w=["GB"])

            for c in range(4):
                eng = P.dve
                eng(lambda e, c=c, ubc=ubc: e.tensor_scalar(acc[:, c, :], ubc[:, c, 0:TB], CW[:, c, 0:1], PA[:, 48 + c:49 + c],
                                                            ALU.mult, ALU.add), r=[uu, "CW", "PA"], w=[("acc", c)])
                for j in range(1, 31):
                    eng(lambda e, c=c, j=j, ubc=ubc: e.scalar_tensor_tensor(acc[:, c, :], ubc[:, c, j:j + TB], CW[:, c, j:j + 1],
                                                                            acc[:, c, :], ALU.mult, ALU.add),
                        r=[uu, "CW", ("acc", c)], w=[("acc", c)])
            for c in range(4):
                P.act(lambda e, c=c: e.activation(sq[:, c, :], acc[:, c, :], AF.Square), r=[("acc", c)], w=["sq"])
            bkm_, bum_ = bank()
            for c in range(4):
                P.pe(lambda e, c=c, bkm_=bkm_: e.matmul(bkm_[:, 0:TB], ones_c[:], acc[:, c, :], start=(c == 0), stop=(c == 3)),
                     r=["ones_c", ("acc", c)], w=[bum_])
            bks_, bus_ = bank()
            for c in range(4):
                P.pe(lambda e, c=c, bks_=bks_: e.matmul(bks_[:, 0:TB], ones_c[:], sq[:, c, :], start=(c == 0), stop=(c == 3)),
                     r=["ones_c", "sq"], w=[bus_])
            P.dve(lambda e, bkm_=bkm_: e.tensor_copy(cm[:], bkm_[:, 0:TB]), r=[bum_], w=["cm"])
            P.dve(lambda e: e.tensor_tensor(ct[:], cm[:], cm[:], ALU.mult), r=["cm"], w=["ct"])
            P.dve(lambda e, bks_=bks_: e.tensor_tensor(cr[:], bks_[:, 0:TB], ct[:], ALU.subtract), r=[bus_, "ct"], w=["cr"])
            rsqrt(cr[:], cr[:], LN_EPS_, ["cr"], "cr")
            for c in range(4):
                P.pool(lambda e, c=c: e.tensor_tensor(sq[:, c, :], acc[:, c, :], cm[:], ALU.subtract),
                       r=[("acc", c), "cm", "sq"], w=["sq"])
                P.pool(lambda e, c=c: e.tensor_tensor(sq[:, c, :], sq[:, c, :], cr[:], ALU.mult), r=["sq", "cr"], w=["sq"])
                P.act(lambda e, c=c: e.activation(ycT[:, c, :], sq[:, c, :], AF.Silu, bias=PA[:, 56 + c:57 + c],
                                                  scale=PA[:, 52 + c:53 + c]), r=["sq", "PA"], w=["ycT"])

            CUM = role("CUM", 6); EG = role("EG", 7); IEG = role("IEG", 8); EGX = role("EGX", 10)
            fl = lambda t: t[:].rearrange("p h t -> p (h t)")
            for h in range(8):
                P.dve(lambda e, h=h: e.tensor_tensor_scan(CUM[:, h, :], rmask[:].rearrange("p c t -> p (c t)"), SGW[:, h, :], 0.0,
                                                          ALU.mult, ALU.add), r=["rmask", "SGW"], w=["CUM"])
            P.act(lambda e: e.activation(fl(EG), fl(CUM), AF.Exp, scale=-LWC), r=["CUM"], w=["EG"])
            P.act(lambda e: e.activation(fl(IEG), fl(CUM), AF.Exp, scale=LWC), r=["CUM"], w=["IEG"])
            P.pool(lambda e: e.tensor_tensor(fl(T1), fl(CUM), fl(SGW), ALU.subtract), r=["CUM", "SGW"], w=["T1"])
            P.act(lambda e: e.activation(fl(EGX), fl(T1), AF.Exp, scale=-LWC), r=["T1"], w=["EGX"])
            KKR = role("KKR", 4); SQ2 = role("SQ2", 6); KKN = role("KKN", 9)
            P.dve(lambda e: e.tensor_tensor(KKR[:], KB[:], pbc(40), ALU.mult), r=["KB", "PB"], w=["KKR"])
            P.pool(lambda e: e.tensor_tensor(SQ2[:], KKR[:], KKR[:], ALU.mult), r=["KKR"], w=["SQ2"])
            for half in range(2):
                bk, bu = bank()
                P.pe(lambda e, bk=bk, half=half: e.matmul(bk[0:64, :], ones_h[:], fl(SQ2)[:, half * 512:(half + 1) * 512],
                                                          start=True, stop=True), r=["ones_h", "SQ2"], w=[bu])
                rsqrt(fl(KKN)[:, half * 512:(half + 1) * 512], bk[0:64, :], 0.0, [bu], "KKN", floor=1e-12)
            P.dve(lambda e: e.tensor_tensor(KKN[:], KKN[:], KKR[:], ALU.mult), r=["KKN", "KKR"], w=["KKN"])
            T1 = role("T1", 4); KF = role("KF", 6)
            P.pool(lambda e: e.tensor_tensor(T1[:], AB[:], pbc(48), ALU.mult), r=["AB", "PB"], w=["T1"])
            P.pool(lambda e: e.tensor_tensor(T1[:], T1[:], ombc(48), ALU.add), r=["T1", "OMB"], w=["T1"])
            P.dve(lambda e: e.tensor_tensor(KF[:], KB[:], T1[:], ALU.mult), r=["KB", "T1"], w=["KF"])
            BBt = role("BBt", 1)
            P.pool(lambda e: e.tensor_tensor(BBt[:], KKN[:], AB[:], ALU.mult), r=["KKN", "AB"], w=["BBt"])
            P.dve(lambda e: e.tensor_tensor(RT[:], RB[:], EG[:], ALU.mult), r=["RB", "EG"], w=["RT"])
            P.pool(lambda e: e.tensor_tensor(KT[:], KF[:], IEG[:], ALU.mult), r=["KF", "IEG"], w=["KT"])
            P.dve(lambda e: e.tensor_tensor(BT[:], BBt[:], IEG[:], ALU.mult), r=["BBt", "IEG"], w=["BT"])
            P.dve(lambda e: e.scalar_tensor_tensor(ATl[:], KKN[:], -1.0, EGX[:], ALU.mult, ALU.mult),
                   r=["KKN", "EGX"], w=["ATl"])
            P.act(lambda e: e.activation(fl(VBb), fl(VB), AF.Copy), r=["VB"], w=["VBb"])
            SQ2 = role("SQ2", 8); BON = role("BON", 4); YB = role("YB", 5)
            P.dve(lambda e: e.tensor_tensor(SQ2[:], RB[:], KF[:], ALU.mult), r=["RB", "KF"], w=["SQ2"])
            P.dve(lambda e: e.tensor_tensor(SQ2[:], SQ2[:], pbc(72), ALU.mult), r=["SQ2", "PB"], w=["SQ2"])
            for half in range(2):
                bk, bu = bank()
                P.pe(lambda e, bk=bk, half=half: e.matmul(bk[0:64, :], ones_h[:], fl(SQ2)[:, half * 512:(half + 1) * 512],
                                                          start=True, stop=True), r=["ones_h", "SQ2"], w=[bu])
                P.dve(lambda e, bk=bk, half=half: e.tensor_tensor(fl(BON)[:, half * 512:(half + 1) * 512], bk[0:64, :],
                                                                  fl(VB)[:, half * 512:(half + 1) * 512], ALU.mult),
                      r=[bu, "VB"], w=["BON"])

            for cc in range(TB // CH):
                c0 = cc * CH
                cs = slice(c0, c0 + CH)

                def bfview(bk):
                    return bk[0:64, 0:256].bitcast(BF16).rearrange("p (h t) -> p h t", h=8)

                for (srcT, sn, dstT, dn) in ((VBb, "VBb", Vt, "Vt"), (KT, "KT", Ktm, "Ktm"), (BT, "BT", Btm, "Btm")):
                    bk, bu = bank()
                    for h in range(8):
                        P.pe(lambda e, bk=bk, h=h, srcT=srcT: e.transpose(bfview(bk)[:, h, :], srcT[:, h, cs], identb[0:64, 0:64]),
                             r=[sn, "identb"], w=[bu])
                    P.act(lambda e, bk=bk, dstT=dstT: e.activation(dstT[:], bfview(bk), AF.Copy), r=[bu], w=[dn])

                def amat(lhsT_, ln, rhs_, rn, mask, mn, dst, dn, eng):
                    bk, bu = bank()
                    for h in range(8):
                        P.pe(lambda e, bk=bk, h=h: e.matmul(bk[0:64, h * 64:(h + 1) * 64], lhsT_[:, h, cs], rhs_[:, h, cs],
                                                            start=True, stop=True), r=[ln, rn], w=[bu])
                    eng(lambda e, bk=bk: e.tensor_tensor(dst[:], bk[0:64, :].rearrange("p (h t) -> p h t", h=8), m8(mask), ALU.mult),
                        r=[bu, mn], w=[dn])

                amat(BT, "BT", ATl, "ATl", m_st, "m_st", NTpw[0], "NT0", P.dve)
                amat(ATl, "ATl", BT, "BT", m_lo, "m_lo", Npw[0], "N0", P.dve)
                amat(BT, "BT", RT, "RT", m_in, "m_in", ArbT, "ArbT", P.dve)
                amat(KT, "KT", ATl, "ATl", m_st, "m_st", AakT, "AakT", P.dve)
                amat(KT, "KT", RT, "RT", m_in, "m_in", ArkT, "ArkT", P.dve)

                def mm8(lhs, ln, rhs, rn, dst, dn, add=None, an=None):
                    bk, bu = bank()
                    for h in range(8):
                        P.pe(lambda e, bk=bk, h=h: e.matmul(bk[0:64, h * 64:(h + 1) * 64], lhs[:, h, :], rhs[:, h, :],
                                                            start=True, stop=True), r=[ln, rn], w=[bu])
                    v = bk[0:64, :].rearrange("p (h t) -> p h t", h=8)
                    if add is None:
                        P.act(lambda e: e.activation(dst[:], v, AF.Copy), r=[bu], w=[dn])
                    else:
                        P.dve(lambda e: e.tensor_tensor(dst[:], v, add[:], ALU.add), r=[bu, an], w=[dn])

                P.dve(lambda e: e.tensor_tensor(PT[0][:], NTpw[0][:], identb8[:], ALU.add), r=["NT0", "identb8"], w=["PT0"])
                cur = 0
                for i in range(5):
                    a_, b_ = i % 2, (i + 1) % 2
                    mm8(NTpw[a_], "NT%d" % a_, Npw[a_], "N%d" % a_, Npw[b_], "N%d" % b_)
                    if i < 4:
                        mm8(Npw[a_], "N%d" % a_, NTpw[a_], "NT%d" % a_, NTpw[b_], "NT%d" % b_)
                    mm8(Npw[b_], "N%d" % b_, PT[cur], "PT%d" % cur, PT[1 - cur], "PT%d" % (1 - cur),
                        add=PT[cur], an="PT%d" % cur)
                    cur = 1 - cur
                PTf, PTn = PT[cur], "PT%d" % cur
                bk, bu = bank()
                for h in range(8):
                    P.pe(lambda e, bk=bk, h=h: e.matmul(bk[0:64, h * 64:(h + 1) * 64], ATl[:, h, cs], S0b[:, h, :],
                                                        start=True, stop=False), r=["ATl", "S0b"], w=[bu])
                    P.pe(lambda e, bk=bk, h=h: e.matmul(bk[0:64, h * 64:(h + 1) * 64], AakT[:, h, :], Vt[:, h, :],
                                                        start=False, stop=True), r=["AakT", "Vt"], w=[bu])
                P.act(lambda e, bk=bk: e.activation(Xb[:], bk[0:64, :].rearrange("p (h t) -> p h t", h=8), AF.Copy),
                      r=[bu], w=["Xb"])
                mm8(PTf, PTn, Xb, "Xb", Ub, "Ub")
                bk, bu = bank()
                for h in range(8):
                    o_ = bk[0:64, h * 64:(h + 1) * 64]
                    P.pe(lambda e, o_=o_, h=h: e.matmul(o_, S0b[:, h, :], RT[:, h, cs], start=True, stop=False),
                         r=["S0b", "RT"], w=[bu])
                    P.pe(lambda e, o_=o_, h=h: e.matmul(o_, Ub[:, h, :], ArbT[:, h, :], start=False, stop=False),
                         r=["Ub", "ArbT"], w=[bu])
                    P.pe(lambda e, o_=o_, h=h: e.matmul(o_, Vt[:, h, :], ArkT[:, h, :], start=False, stop=True),
                         r=["Vt", "ArkT"], w=[bu])
                P.act(lambda e, bk=bk: e.activation(YB[:, :, cs], bk[0:64, :].rearrange("p (h t) -> p h t", h=8), AF.Copy),
                      r=[bu], w=["YB"])
                bk, bu = bank()
                for h in range(8):
                    o_ = bk[0:64, h * 64:(h + 1) * 64]
                    P.pe(lambda e, o_=o_, h=h: e.matmul(o_, Btm[:, h, :], Ub[:, h, :], start=True, stop=False),
                         r=["Btm", "Ub"], w=[bu])
                    P.pe(lambda e, o_=o_, h=h: e.matmul(o_, Ktm[:, h, :], Vt[:, h, :], start=False, stop=True),
                         r=["Ktm", "Vt"], w=[bu])
                P.dve(lambda e, bk=bk: e.tensor_tensor(Stmp[:], bk[0:64, :].rearrange("p (h t) -> p h t", h=8), S0[:], ALU.add),
                      r=[bu, "S0"], w=["Stmp"])
                P.dve(lambda e: e.tensor_tensor(S0[:], Stmp[:], EG[:, :, c0 + CH - 1:c0 + CH].broadcast_to([64, 8, 64]), ALU.mult),
                      r=["Stmp", "EG"], w=["S0"])
                P.act(lambda e: e.activation(S0b[:], S0[:], AF.Copy), r=["S0"], w=["S0b"])

            T1 = role("T1", 9); KF = role("KF", 10)
            P.pool(lambda e: e.tensor_tensor(SQ2[:], YB[:], YB[:], ALU.mult), r=["YB"], w=["SQ2"])
            for half in range(2):
                hs = slice(half * 512, (half + 1) * 512)
                bk1, bu1 = bank()
                P.pe(lambda e, bk1=bk1, hs=hs: e.matmul(bk1[0:64, :], ones_g[:], fl(YB)[:, hs], start=True, stop=True),
                     r=["ones_g", "YB"], w=[bu1])
                bk2_, bu2_ = bank()
                P.pe(lambda e, bk2_=bk2_, hs=hs: e.matmul(bk2_[0:64, :], ones_g[:], fl(SQ2)[:, hs], start=True, stop=True),
                     r=["ones_g", "SQ2"], w=[bu2_])
                P.dve(lambda e, bk1=bk1, hs=hs: e.tensor_copy(fl(T1)[:, hs], bk1[0:64, :]), r=[bu1], w=["T1"])
                P.dve(lambda e, hs=hs: e.tensor_tensor(fl(KF)[:, hs], fl(T1)[:, hs], fl(T1)[:, hs], ALU.mult), r=["T1"], w=["KF"])
                P.dve(lambda e, bk2_=bk2_, hs=hs: e.tensor_tensor(fl(KF)[:, hs], bk2_[0:64, :], fl(KF)[:, hs], ALU.subtract),
                      r=[bu2_, "KF"], w=["KF"])
            rsqrt(fl(KF), fl(KF), GN_EPS_, ["KF"], "KF")
            P.pool(lambda e: e.tensor_tensor(YB[:], YB[:], T1[:], ALU.subtract), r=["YB", "T1"], w=["YB"])
            P.pool(lambda e: e.tensor_tensor(YB[:], YB[:], KF[:], ALU.mult), r=["YB", "KF"], w=["YB"])
            P.pool(lambda e: e.tensor_tensor(YB[:], YB[:], pbc(56), ALU.mult), r=["YB", "PB"], w=["YB"])
            P.pool(lambda e: e.tensor_tensor(YB[:], YB[:], pbc(64), ALU.add), r=["YB", "PB"], w=["YB"])
            P.dve(lambda e: e.tensor_tensor(YB[:], YB[:], BON[:], ALU.add), r=["YB", "BON"], w=["YB"])
            P.dve(lambda e: e.tensor_tensor(YRW[:], YB[:], GB[:], ALU.mult), r=["YB", "GB"], w=["YRW"])

            for half in range(2):
                hs = slice(half * 512, (half + 1) * 512)
                bk, bu = bank()
                for c in range(4):
                    P.pe(lambda e, bk=bk, c=c, hs=hs: e.matmul(bk[:, :], ycT[:, c, :], w_oc_bf[:, c, hs], start=(c == 0), stop=False),
                         r=["ycT", "w_oc_bf"], w=[bu])
                for h in range(8):
                    P.pe(lambda e, bk=bk, h=h, hs=hs: e.matmul(bk[:, :], YRW[:, h, :], w_or_bf[:, h, hs], start=False, stop=(h == 7)),
                         r=["YRW", "w_or_bf"], w=[bu])
                P.dve(lambda e, bk=bk, hs=hs: e.tensor_tensor(xn[:, hs], bk[:, :], G1[:, hs], ALU.mult), r=[bu, "G1"], w=["xn"])
            P.dve(lambda e: e.scalar_tensor_tensor(xn[:], xb[:], ALPHA_, xn[:], ALU.mult, ALU.add), r=["xb", "xn"], w=["xn"])
            for hh in range(2):
                P.dve(lambda e, hh=hh: e.bn_stats(st6[:, hh, :], xn[:, hh * 512:(hh + 1) * 512]), r=["xn"], w=["st6"])
            P.dve(lambda e: e.bn_aggr(mv[:], st6[:].rearrange("p a b -> p (a b)")), r=["st6"], w=["mv"])
            rsqrt(rstd[:], mv[:, 1:2], LN_EPS_, ["mv"], "rstd")
            P.dve(lambda e: e.tensor_scalar(xn[:], xn[:], mv[:, 0:1], rstd[:, 0:1], ALU.subtract, ALU.mult),
                  r=["xn", "mv", "rstd"], w=["xn"])
            P.pool(lambda e: e.tensor_tensor(xn[:], xn[:], LG1[:], ALU.mult), r=["xn", "LG1"], w=["xn"])
            P.pool(lambda e: e.tensor_tensor(xn[:], xn[:], LB1[:], ALU.add), r=["xn", "LB1"], w=["xn"])
            o = P.dma("sp", lambda e, b=b, t0=t0: e.dma_start(out=x1_d[b, t0:t0 + TB, :], in_=xn[:]), r=["xn"], w=[("x1d", b, blk)], lane="xn_out")
            out_ops.append(o)

    if stage == "A":
        P.finish(final_ops=out_ops[-1:])
        return

    import os
    SKIP = os.environ.get("KSKIP", "")
    P.pop()
    P.push()
    NT = 256
    NST = SEQ // NT
    NBUF = 3
    ut_scr = nc.dram_tensor("ut_scr", [128, 128, 8, 128], BF16).ap()
    v_scr = nc.dram_tensor("v_scr", [128, 128, D], BF16).ap()

    ust = [P.sb("ust%d" % i, [128, D], F32) for i in range(2)]
    vst = [P.sb("vst%d" % i, [128, D], F32) for i in range(2)]
    utb = [P.sb("utb%d" % i, [128, 8, 128], BF16) for i in range(2)]
    vbb = [P.sb("vbb%d" % i, [128, D], BF16) for i in range(2)]
    for c in range(128):
        i = c % 2
        P.dma("sp", lambda e, c=c, i=i: e.dma_start(out=ust[i][:], in_=pu_d[c * 128:(c + 1) * 128, :]), w=["ust%d" % i], lane="ust%d" % i)
        P.dma("sp", lambda e, c=c, i=i: e.dma_start(out=vst[i][:], in_=pv_d[c * 128:(c + 1) * 128, :]), w=["vst%d" % i], lane="vst%d" % i)
        for half in range(2):
            bk, bu = bank()
            for q in range(4):
                dc = half * 4 + q
                P.pe(lambda e, bk=bk, q=q, dc=dc, i=i: e.transpose(bk[:, q * 128:(q + 1) * 128], ust[i][:, dc * 128:(dc + 1) * 128], ident[:, :]),
                     r=["ust%d" % i, "ident"], w=[bu])
            if half == 0:
                P.act(lambda e, bk=bk, i=i: e.activation(utb[i][:, 0:4, :], bk[:, :].rearrange("p (q e) -> p q e", q=4), AF.Copy),
                      r=[bu], w=["utb%d" % i])
            else:
                P.dve(lambda e, bk=bk, i=i: e.tensor_copy(utb[i][:, 4:8, :], bk[:, :].rearrange("p (q e) -> p q e", q=4)),
                      r=[bu], w=["utb%d" % i])
        P.pool(lambda e, i=i: e.tensor_copy(vbb[i][:], vst[i][:]), r=["vst%d" % i], w=["vbb%d" % i])
        P.dma("sp", lambda e, c=c, i=i: e.dma_start(out=ut_scr[c], in_=utb[i][:]), r=["utb%d" % i], w=[("UTd", c)], lane="utb%d" % i)
        P.dma("sp", lambda e, c=c, i=i: e.dma_start(out=v_scr[c], in_=vbb[i][:]), r=["vbb%d" % i], w=[("Vd", c)], lane="vbb%d" % i)
    P.pop()
    P.push()

    wq_bf = P.sb("wq_bf", [128, 8, 2048], BF16)
    for kc in range(8):
        P.dma("sp", lambda e, kc=kc: e.dma_start(out=big[:, :], in_=wq_d[kc * 128:(kc + 1) * 128, :]), w=["big"], lane="big")
        P.act(lambda e, kc=kc: e.activation(wq_bf[:, kc, :], big[:, :], AF.Copy), r=["big"], w=["wq_bf"])
    K12 = P.sb("K12", [128, 16, 128], BF16)
    for s_, kd in enumerate((k1_d, k2_d)):
        for h in range(8):
            P.dma("sp", lambda e, kd=kd, h=h: e.dma_start(out=big[:, 0:128], in_=kd[h]), w=["big"], lane="big")
            bk, bu = bank()
            P.pe(lambda e, bk=bk: e.transpose(bk[:, 0:128], big[:, 0:128], ident[:, :]), r=["big", "ident"], w=[bu])
            P.dve(lambda e, bk=bk, s_=s_, h=h: e.tensor_copy(K12[:, s_ * 8 + h, :], bk[:, 0:128]), r=[bu], w=["K12"])
    P.dma("sp", lambda e: e.dma_start(out=LG1[:], in_=ln2g_d.partition_broadcast(128)), w=["LG1"], lane="LG1")
    P.dma("sp", lambda e: e.dma_start(out=LB1[:], in_=ln2b_d.partition_broadcast(128)), w=["LB1"], lane="LB1")
    for half in range(2):
        bk, bu = bank()
        for c in range(4):
            P.pe(lambda e, bk=bk, c=c, half=half: e.transpose(bk[0:4, c * 128:(c + 1) * 128], MOD[:, 40 + half * 4 + c, :], ident[:, :]),
                 r=["MOD", "ident"], w=[bu])
        P.dve(lambda e, bk=bk, half=half: e.tensor_copy(GROW[:, 0, half * 512:(half + 1) * 512], bk[0:4, :]), r=[bu], w=["GROW"])

    G2 = P.sb("G2", [128, D], F32)
    xs = P.sb("xs", [128, 2, D], F32)
    xn2 = P.sb("xn2", [128, D], F32)
    h2T = P.sb("h2T", [128, 8, NT], BF16)
    qT = P.sb("qT", [128, 16, NT], BF16)
    SC = P.sb("SC", [128, 16, 128], F32)
    SCm = P.sb("SCm", [128, 256], F32)
    TV = P.sb("TV", [128, 16, 16], F32)
    TI = P.sb("TI", [128, 16, 16], U32)
    TIf = P.sb("TIf", [128, 16, 16], F32)
    CAND = SC[:].rearrange("p a b -> p (a b)").rearrange("p (h c) -> p h c", h=8)
    SV = P.sb("SV", [128, 8, 16], F32)
    CI = P.sb("CI", [128, 8, 16], U32)
    CIf = P.sb("CIf", [128, 8, 16], F32)
    JS = P.sb("JS", [128, 8, 16], F32)
    IS = P.sb("IS", [128, 8, 16], F32)
    EQ = P.sb("EQ", [128, 8, 16, 16], BF16)
    ASEL = P.sb("ASEL", [128, 8, 16], F32)
    BSEL = P.sb("BSEL", [128, 8, 16], F32)
    GATE = P.sb("GATE", [128, 8, 16], F32)
    ssum = P.sb("ssum", [128, 8], F32)
    ATt = P.sb("ATt", [128, 128], F32)
    BTt = P.sb("BTt", [128, 128], F32)
    GTt = P.sb("GTt", [128, 128], F32)
    iota16 = P.sb("iota16", [128, 16], F32)
    iota3 = P.sb("iota3", [128, 8, 128], BF16)
    thr16 = P.sb("thr16", [128, 16], F32)
    P.pool(lambda e: e.iota(thr16[:], [[16, 16]], base=16, channel_multiplier=0, allow_small_or_imprecise_dtypes=True), w=["thr16"])
    P.pool(lambda e: e.iota(iota16[:], [[1, 16]], base=0, channel_multiplier=0, allow_small_or_imprecise_dtypes=True), w=["iota16"])
    P.pool(lambda e: e.iota(iota3[:], [[0, 8], [1, 128]], base=0, channel_multiplier=0, allow_small_or_imprecise_dtypes=True), w=["iota3"])
    OA = [P.sb("OA%d" % i, [128, 8, 128], BF16) for i in range(2)]
    OB = [P.sb("OB%d" % i, [128, 8, 128], BF16) for i in range(2)]
    GG = P.sb("GG", [128, 128, NT], BF16)
    UTb = [P.sb("UTb%d" % i, [128, 8, 128], BF16) for i in range(NBUF)]
    Vb = [P.sb("Vb%d" % i, [128, D], BF16) for i in range(NBUF)]
    gz = [P.sb("gz%d" % i, [128, NT], F32) for i in range(2)]
    actT = [P.sb("actT%d" % i, [128, NT], BF16) for i in range(2)]

    nrot[0] = 4
    nst_run = nblk_run * TB // NT
    total_g = nb_run * nst_run * 128

    def issue_load(g):
        c = g % 128
        i = g % NBUF
        P.dma("sp", lambda e: e.dma_start(out=UTb[i][:], in_=ut_scr[c]), r=[("UTd", c)], w=["UTb%d" % i], lane="UTb%d" % i)
        P.dma("act", lambda e: e.dma_start(out=Vb[i][:], in_=v_scr[c]), r=[("Vd", c)], w=["Vb%d" % i], lane="Vb%d" % i)

    gctr = 0
    for g in range(min(NBUF, total_g)):
        issue_load(g)

    for b in range(nb_run):
        for half in range(2):
            bk, bu = bank()
            P.pe(lambda e, bk=bk, b=b, half=half: e.matmul(bk[:, :], SEL[:, b, :], GROW[:, 0, half * 512:(half + 1) * 512],
                                                           start=True, stop=True), r=["SEL", "GROW"], w=[bu])
            P.act(lambda e, bk=bk, half=half: e.activation(G2[:, half * 512:(half + 1) * 512], bk[:, :], AF.Copy), r=[bu], w=["G2"])
        for st in range(nst_run):
            t0 = st * NT
            for j in range(2):
                tl = t0 // TB + j
                P.dma("sp", lambda e, j=j, tl=tl, b=b: e.dma_start(out=xs[:, j, :], in_=x1_d[b, tl * TB:(tl + 1) * TB, :]),
                      r=[("x1d", b, tl)], w=[("xs", j)], lane=("xs", j))
                for hh in range(2):
                    P.dve(lambda e, hh=hh, j=j: e.bn_stats(st6[:, hh, :], xs[:, j, hh * 512:(hh + 1) * 512]), r=[("xs", j)], w=["st6"])
                P.dve(lambda e: e.bn_aggr(mv[:], st6[:].rearrange("p a b -> p (a b)")), r=["st6"], w=["mv"])
                rsqrt(rstd[:], mv[:, 1:2], LN_EPS_, ["mv"], "rstd")
                P.dve(lambda e, j=j: e.tensor_scalar(xn2[:], xs[:, j, :], mv[:, 0:1], rstd[:, 0:1], ALU.subtract, ALU.mult),
                      r=[("xs", j), "mv", "rstd"], w=["xn2"])
                for half in range(2):
                    bk, bu = bank()
                    for q in range(4):
                        fc = half * 4 + q
                        P.pe(lambda e, bk=bk, q=q, fc=fc: e.transpose(bk[:, q * 128:(q + 1) * 128], xn2[:, fc * 128:(fc + 1) * 128], ident[:, :]),
                             r=["xn2", "ident"], w=[bu])
                    for q in range(4):
                        fc = half * 4 + q
                        P.act(lambda e, bk=bk, q=q, fc=fc, b=b, j=j: e.activation(
                            h2T[:, fc, j * 128:(j + 1) * 128], bk[:, q * 128:(q + 1) * 128], AF.Identity,
                            bias=MOD[:, 24 + fc, b:b + 1], scale=MOD[:, 32 + fc, b:b + 1]), r=[bu, "MOD"], w=["h2T"])
            for m in range(16):
                bk, bu = bank()
                for kc in range(8):
                    P.pe(lambda e, bk=bk, kc=kc, m=m: e.matmul(bk[:, 0:NT], wq_bf[:, kc, m * 128:(m + 1) * 128], h2T[:, kc, :],
                                                               start=(kc == 0), stop=(kc == 7)), r=["wq_bf", "h2T"], w=[bu])
                if m % 2 == 0:
                    P.act(lambda e, bk=bk, m=m: e.activation(qT[:, m, :], bk[:, 0:NT], AF.Copy), r=[bu], w=["qT"])
                else:
                    P.dve(lambda e, bk=bk, m=m: e.tensor_copy(qT[:, m, :], bk[:, 0:NT]), r=[bu], w=["qT"])
            for j in range(2):
                js = slice(j * 128, (j + 1) * 128)
                for grp in range(4):
                    bk, bu = bank()
                    for q in range(4):
                        hs_ = grp * 4 + q
                        h, s_ = hs_ // 2, hs_ % 2
                        P.pe(lambda e, bk=bk, q=q, hs_=hs_, h=h, s_=s_: e.matmul(bk[:, q * 128:(q + 1) * 128], qT[:, hs_, js], K12[:, s_ * 8 + h, :],
                                                                               start=True, stop=True), r=["qT", "K12"], w=[bu])
                    P.act(lambda e, bk=bk, grp=grp: e.activation(SC[:, grp * 4:(grp + 1) * 4, :], bk[:, :].rearrange("p (q n) -> p q n", q=4), AF.Copy),
                          r=[bu], w=["SC"])
                for hs_ in range(0 if "T" in SKIP else 16):
                    P.dve(lambda e, hs_=hs_: e.max(TV[:, hs_, 0:8], SC[:, hs_, :]), r=["SC"], w=["TV"])
                    P.dve(lambda e, hs_=hs_: e.max_index(TI[:, hs_, 0:8], TV[:, hs_, 0:8], SC[:, hs_, :]), r=["SC", "TV"], w=["TI"])
                    P.dve(lambda e, hs_=hs_: e.match_replace(SCm[:, 0:128], TV[:, hs_, 0:8], SC[:, hs_, :], -1e30), r=["SC", "TV"], w=["SCm"])
                    P.dve(lambda e, hs_=hs_: e.max(TV[:, hs_, 8:16], SCm[:, 0:128]), r=["SCm"], w=["TV"])
                    P.dve(lambda e, hs_=hs_: e.max_index(TI[:, hs_, 8:16], TV[:, hs_, 8:16], SCm[:, 0:128]), r=["SCm", "TV"], w=["TI"])
                P.dve(lambda e: e.tensor_copy(TIf[:], TI[:]), r=["TI"], w=["TIf"])
                TV4 = TV[:].rearrange("p (h s) k -> p h s k", s=2)
                TI4 = TIf[:].rearrange("p (h s) k -> p h s k", s=2)
                P.dve(lambda e: e.tensor_tensor(CAND.rearrange("p h (i j) -> p h i j", i=16),
                                                TV4[:, :, 0, :].unsqueeze(3).broadcast_to([128, 8, 16, 16]),
                                                TV4[:, :, 1, :].unsqueeze(2).broadcast_to([128, 8, 16, 16]), ALU.add), r=["TV"], w=["SC"])
                for h in range(8):
                    P.dve(lambda e, h=h: e.max(SV[:, h, 0:8], CAND[:, h, :]), r=["SC"], w=["SV"])
                    P.dve(lambda e, h=h: e.max_index(CI[:, h, 0:8], SV[:, h, 0:8], CAND[:, h, :]), r=["SC", "SV"], w=["CI"])
                    P.dve(lambda e, h=h: e.match_replace(SCm[:, :], SV[:, h, 0:8], CAND[:, h, :], -1e30), r=["SC", "SV"], w=["SCm"])
                    P.dve(lambda e, h=h: e.max(SV[:, h, 8:16], SCm[:, :]), r=["SCm"], w=["SV"])
                    P.dve(lambda e, h=h: e.max_index(CI[:, h, 8:16], SV[:, h, 8:16], SCm[:, :]), r=["SCm", "SV"], w=["CI"])
                P.dve(lambda e: e.tensor_tensor(GATE[:], SV[:], SV[:, :, 0:1].broadcast_to([128, 8, 16]), ALU.subtract), r=["SV"], w=["GATE"])
                P.act(lambda e: e.activation(GATE[:], GATE[:], AF.Exp), r=["GATE"], w=["GATE"])
                P.dve(lambda e: e.tensor_reduce(ssum[:], GATE[:], AX.X, ALU.add), r=["GATE"], w=["ssum"])
                P.dve(lambda e: e.reciprocal(ssum[:], ssum[:]), r=["ssum"], w=["ssum"])
                P.dve(lambda e: e.tensor_tensor(GATE[:], GATE[:], ssum[:].unsqueeze(2).broadcast_to([128, 8, 16]), ALU.mult), r=["GATE", "ssum"], w=["GATE"])
                P.dve(lambda e: e.tensor_copy(CIf[:], CI[:]), r=["CI"], w=["CIf"])
                P.dve(lambda e: e.tensor_tensor(EQ[:], CIf[:].unsqueeze(3).broadcast_to([128, 8, 16, 16]),
                                                thr16[:, :].unsqueeze(1).unsqueeze(1).broadcast_to([128, 8, 16, 16]), ALU.is_ge), r=["CIf", "thr16"], w=["EQ"])
                P.dve(lambda e: e.tensor_reduce(IS[:], EQ[:], AX.X, ALU.add), r=["EQ"], w=["IS"])
                P.dve(lambda e: e.scalar_tensor_tensor(JS[:], IS[:], -16.0, CIf[:], ALU.mult, ALU.add), r=["IS", "CIf"], w=["JS"])
                io4 = iota16[:, :].unsqueeze(1).unsqueeze(1).broadcast_to([128, 8, 16, 16])
                for (selT, sn, s_, dstT, dn) in ((IS, "IS", 0, ASEL, "ASEL"), (JS, "JS", 1, BSEL, "BSEL")):
                    P.dve(lambda e, selT=selT: e.tensor_tensor(EQ[:], selT[:].unsqueeze(3).broadcast_to([128, 8, 16, 16]), io4, ALU.is_equal),
                           r=[sn, "iota16"], w=["EQ"])
                    P.pool(lambda e, s_=s_: e.tensor_tensor(EQ[:], EQ[:], TI4[:, :, s_, :].unsqueeze(2).broadcast_to([128, 8, 16, 16]), ALU.mult),
                           r=["EQ", "TIf"], w=["EQ"])
                    P.dve(lambda e, dstT=dstT: e.tensor_reduce(dstT[:], EQ[:], AX.X, ALU.add), r=["EQ"], w=[dn])
                for (srcT, sn, dstT, dn) in ((ASEL, "ASEL", ATt, "ATt"), (BSEL, "BSEL", BTt, "BTt"), (GATE, "GATE", GTt, "GTt")):
                    bk, bu = bank()
                    P.pe(lambda e, bk=bk, srcT=srcT: e.transpose(bk[:, 0:128], srcT[:].rearrange("p h k -> p (h k)"), ident[:, :]),
                         r=[sn, "ident"], w=[bu])
                    P.act(lambda e, bk=bk, dstT=dstT: e.activation(dstT[:], bk[:, 0:128], AF.Copy), r=[bu], w=[dn])
                for tg in range(0 if "G" in SKIP else 16):
                    i = tg % 2
                    ts_ = slice(tg * 8, tg * 8 + 8)
                    P.dve(lambda e, ts_=ts_, i=i: e.tensor_tensor(OA[i][:], iota3[:], ATt[:, ts_].unsqueeze(2).broadcast_to([128, 8, 128]), ALU.is_equal),
                           r=["iota3", "ATt"], w=["OA%d" % i])
                    P.pool(lambda e, ts_=ts_, i=i: e.tensor_tensor(OA[i][:], OA[i][:], GTt[:, ts_].unsqueeze(2).broadcast_to([128, 8, 128]), ALU.mult),
                           r=["OA%d" % i, "GTt"], w=["OA%d" % i])
                    P.dve(lambda e, ts_=ts_, i=i: e.tensor_tensor(OB[i][:], iota3[:], BTt[:, ts_].unsqueeze(2).broadcast_to([128, 8, 128]), ALU.is_equal),
                          r=["iota3", "BTt"], w=["OB%d" % i])
                    for q2 in range(2):
                        bk, bu = bank()
                        for q in range(4):
                            tt = q2 * 4 + q
                            P.pe(lambda e, bk=bk, q=q, tt=tt, i=i: e.matmul(bk[:, q * 128:(q + 1) * 128], OB[i][:, tt, :], OA[i][:, tt, :],
                                                                          start=True, stop=True), r=["OA%d" % i, "OB%d" % i], w=[bu])
                        tb_ = j * 128 + tg * 8 + q2 * 4
                        P.act(lambda e, bk=bk, tb_=tb_: e.activation(GG[:, :, tb_:tb_ + 4].rearrange("p i t -> p t i"),
                                                                     bk[:, :].rearrange("p (t i) -> p t i", t=4), AF.Copy), r=[bu], w=["GG"])
            ybanks = [(banks[4 + k], ("B", 4 + k)) for k in range(4)]
            def zstage(c):
                nonlocal gctr
                g = gctr
                gctr += 1
                i = g % NBUF
                pz = c % 2
                bk, bu = bank()
                for dc in range(8):
                    P.pe(lambda e, bk=bk, dc=dc, i=i: e.matmul(bk[:, 0:NT], UTb[i][:, dc, :], h2T[:, dc, :], start=(dc == 0), stop=(dc == 7)),
                         r=["UTb%d" % i, "h2T"], w=[bu])
                P.act(lambda e, bk=bk, pz=pz: e.activation(gz[pz][:], bk[:, 0:NT], AF.Gelu), r=[bu], w=["gz%d" % pz])
                eng = P.dve if c % 2 == 0 else P.pool
                eng(lambda e, pz=pz, c=c: e.tensor_tensor(actT[pz][:], gz[pz][:], GG[:, c, :], ALU.mult), r=["gz%d" % pz, "GG"], w=["actT%d" % pz])
                return i, pz, g

            def ystage(c, i, pz, g):
                for j in range(2):
                    for half in range(2):
                        yb, yu = ybanks[j * 2 + half]
                        P.pe(lambda e, yb=yb, j=j, half=half, pz=pz, i=i, c=c: e.matmul(
                            yb[:, :], actT[pz][:, j * 128:(j + 1) * 128], Vb[i][:, half * 512:(half + 1) * 512],
                            start=(c == 0), stop=(c == 127)), r=["actT%d" % pz, "Vb%d" % i], w=[yu])
                if g + NBUF < total_g:
                    issue_load(g + NBUF)

            nch = 0 if "E" in SKIP else 128
            pend = None
            for c in range(nch):
                cur = zstage(c)
                if pend is not None:
                    ystage(c - 1, *pend)
                pend = cur
            if pend is not None:
                ystage(nch - 1, *pend)
            for j in range(2):
                tl = t0 // TB + j
                for half in range(2):
                    yb, yu = ybanks[j * 2 + half]
                    hs = slice(half * 512, (half + 1) * 512)
                    P.dve(lambda e, yb=yb, hs=hs: e.tensor_tensor(xn2[:, hs], yb[:, :], G2[:, hs], ALU.mult), r=[yu, "G2"], w=["xn2"])
                P.dve(lambda e, j=j: e.scalar_tensor_tensor(xn2[:], xs[:, j, :], ALPHA_, xn2[:], ALU.mult, ALU.add), r=[("xs", j), "xn2"], w=["xn2"])
                for hh in range(2):
                    P.dve(lambda e, hh=hh: e.bn_stats(st6[:, hh, :], xn2[:, hh * 512:(hh + 1) * 512]), r=["xn2"], w=["st6"])
                P.dve(lambda e: e.bn_aggr(mv[:], st6[:].rearrange("p a b -> p (a b)")), r=["st6"], w=["mv"])
                rsqrt(rstd[:], mv[:, 1:2], LN_EPS_, ["mv"], "rstd")
                P.dve(lambda e: e.tensor_scalar(xn2[:], xn2[:], mv[:, 0:1], rstd[:, 0:1], ALU.subtract, ALU.mult), r=["xn2", "mv", "rstd"], w=["xn2"])
                P.pool(lambda e: e.tensor_tensor(xn2[:], xn2[:], LG1[:], ALU.mult), r=["xn2", "LG1"], w=["xn2"])
                P.pool(lambda e: e.tensor_tensor(xn2[:], xn2[:], LB1[:], ALU.add), r=["xn2", "LB1"], w=["xn2"])
                o = P.dma("sp", lambda e, b=b, tl=tl: e.dma_start(out=out_d[b, tl * TB:(tl + 1) * TB, :], in_=xn2[:]),
                          r=["xn2"], w=[("x1d", b, tl)], lane="xn2_out")
                out_ops.append(o)
    P.finish(final_ops=out_ops[-1:])
    return nc, P


_NAMES = ["x", "c", "cond_w", "cond_b", "w_in", "mu_shift", "conv_w", "conv_b", "conv_ln_g", "conv_ln_b",
          "rw_w0", "rw_w2", "rw_a0", "rw_a2", "rw_g2", "rw_kk", "rw_ka", "rw_rk", "rw_lnx_g", "rw_lnx_b",
          "w_out", "ln1_g", "ln1_b", "peer_wq", "peer_k1", "peer_k2", "peer_u", "peer_v", "ln2_g", "ln2_b"]


def kernel(**inputs):
    from concourse.bass_utils import run_bass_kernel_spmd
    nc, P = build()
    in_maps = []
    for i in range(8):
        m = {}
        for k in _NAMES:
            v = np.ascontiguousarray(np.asarray(inputs[k], dtype=np.float32))
            if k in ("x", "c"):
                v = np.ascontiguousarray(v[i * NB:(i + 1) * NB])
            m[k] = v
        in_maps.append(m)
    res = run_bass_kernel_spmd(nc, in_maps, core_ids=list(range(8)))
    return np.concatenate([np.asarray(r["out"]) for r in res.results], axis=0).astype(np.float32)
```

```python
import contextlib
import numpy as np
import concourse.bass as bass
import concourse.mybir as mybir

F32 = mybir.dt.float32
BF16 = mybir.dt.bfloat16
U32 = mybir.dt.uint32
I32 = mybir.dt.int32
AF = mybir.ActivationFunctionType
ALU = mybir.AluOpType
AX = mybir.AxisListType


class Op:
    __slots__ = ("eng", "dma", "lane", "lane_val", "sig", "sigval")

    def __init__(self, eng, dma):
        self.eng = eng
        self.dma = dma
        self.lane = None
        self.lane_val = 0
        self.sig = False
        self.sigval = 0


class Prog:
    ENGS = ("pe", "act", "dve", "pool", "sp")

    def __init__(self, nc, flags=None):
        self.nc = nc
        self.dry = flags is None
        self.flags = flags
        self.stack = [contextlib.ExitStack()]
        self.lastw = {}
        self.readers = {}
        self.lanes = {}
        self.alias = {}
        self.allops = []
        self.cnt = {e: 0 for e in self.ENGS}
        self.waited = {e: {} for e in self.ENGS}
        self.engobj = {"pe": nc.tensor, "act": nc.scalar, "dve": nc.vector, "pool": nc.gpsimd, "sp": nc.sync}
        if not self.dry:
            self.K = 12
            self.esem = {e: [self.sem("E%s%d" % (e, i)) for i in range(self.K)] for e in self.ENGS if e != "sp"}

    def push(self):
        self.stack.append(contextlib.ExitStack())

    def pop(self):
        self.stack.pop().close()

    def sb(self, name, shape, dt):
        return self.stack[-1].enter_context(self.nc.sbuf_tensor(name, list(shape), dt))

    def ps(self, name, shape, dt=F32):
        return self.stack[-1].enter_context(self.nc.psum_tensor(name, list(shape), dt))

    def sem(self, name):
        return self.stack[0].enter_context(self.nc.semaphore(name))

    def op(self, eng, fn, r=(), w=(), dma=False, lane=None):
        o = Op(eng, dma)
        al = self.alias
        r = [al.get(u, u) for u in r]
        w = [al.get(u, u) for u in w]
        deps = set()
        for u in r:
            lw = self.lastw.get(u)
            if lw is not None:
                deps.add(lw)
        for u in w:
            lw = self.lastw.get(u)
            if lw is not None:
                deps.add(lw)
            for rd in self.readers.get(u, {}).values():
                deps.add(rd)
        for u in w:
            self.lastw[u] = o
            self.readers[u] = {}
        for u in r:
            if u not in w:
                self.readers.setdefault(u, {})[(eng, len(self.allops)) if dma else eng] = o
        idx = len(self.allops)
        self.allops.append(o)
        if self.dry:
            for d in deps:
                if not (d.eng == eng and eng == "pe" and not d.dma):
                    d.sig = True
            if dma:
                self.lanes.setdefault(lane, 0)
            return o
        e = self.engobj[eng]
        wd = self.waited[eng]
        for d in deps:
            if d.dma:
                s, v = d.lane, d.lane_val
            else:
                if d.eng == eng and eng == "pe":
                    continue
                s, v = d.sigval
            k = id(s)
            if wd.get(k, 0) >= v:
                continue
            wd[k] = v
            e.wait_ge(s, v)
        ins = fn(e)
        if dma:
            if lane not in self.lanes:
                self.lanes[lane] = [self.sem("L%d" % len(self.lanes)), 0]
            L = self.lanes[lane]
            L[1] += 16
            o.lane, o.lane_val = L[0], L[1]
            ins.then_inc(L[0], 16)
        elif self.flags[idx]:
            n = self.cnt[eng]
            self.cnt[eng] += 1
            sm = self.esem[eng][n % self.K]
            o.sigval = (sm, n // self.K + 1)
            ins.then_inc(sm, 1)
        return o

    def pe(self, fn, r=(), w=()):
        return self.op("pe", fn, r, w)

    def act(self, fn, r=(), w=()):
        return self.op("act", fn, r, w)

    def dve(self, fn, r=(), w=()):
        return self.op("dve", fn, r, w)

    def pool(self, fn, r=(), w=()):
        return self.op("pool", fn, r, w)

    def dma(self, q, fn, r=(), w=(), lane=None):
        return self.op(q, fn, r, w, dma=True, lane=lane)

    def finish(self, final_ops=()):
        if not self.dry:
            for o in final_ops:
                self.nc.sync.wait_ge(o.lane, o.lane_val)
        while self.stack:
            self.stack.pop().close()

D = 1024
SEQ = 2048
NB = 4
TB = 128
NBLK = SEQ // TB
CH = 64
ALPHA_ = (2.0) ** 0.25
LN_EPS_ = 1e-5
GN_EPS_ = 64e-5
LWC = 0.6065306597126334


def build(nb_run=NB, nblk_run=NBLK, stage="AB"):
    nc1 = bass.Bass("TRN2", target_bir_lowering=False)
    P1 = Prog(nc1)
    body(nc1, P1, nb_run, nblk_run, stage)
    flags = [o.sig for o in P1.allops]
    nc = bass.Bass("TRN2", target_bir_lowering=False)
    P = Prog(nc, flags)
    body(nc, P, nb_run, nblk_run, stage)
    return nc, P


def body(nc, P, nb_run, nblk_run, stage):
    import os
    SKIP = os.environ.get("KSKIP", "")
    dt = nc.dram_tensor
    x_d = dt("x", [NB, SEQ, D], F32, kind="ExternalInput").ap()
    c_d = dt("c", [NB, D], F32, kind="ExternalInput").ap()
    cond_w_d = dt("cond_w", [D, 6 * D], F32, kind="ExternalInput").ap()
    cond_b_d = dt("cond_b", [6 * D], F32, kind="ExternalInput").ap()
    w_in_d = dt("w_in", [D, 2816], F32, kind="ExternalInput").ap()
    mu_d = dt("mu_shift", [1792], F32, kind="ExternalInput").ap()
    conv_w_d = dt("conv_w", [31, 512], F32, kind="ExternalInput").ap()
    conv_b_d = dt("conv_b", [512], F32, kind="ExternalInput").ap()
    cg_d = dt("conv_ln_g", [512], F32, kind="ExternalInput").ap()
    cb_d = dt("conv_ln_b", [512], F32, kind="ExternalInput").ap()
    w0_d = dt("rw_w0", [512], F32, kind="ExternalInput").ap()
    w2_d = dt("rw_w2", [64, 512], F32, kind="ExternalInput").ap()
    a0_d = dt("rw_a0", [512], F32, kind="ExternalInput").ap()
    a2_d = dt("rw_a2", [64, 512], F32, kind="ExternalInput").ap()
    g2_d = dt("rw_g2", [128, 512], F32, kind="ExternalInput").ap()
    kk_d = dt("rw_kk", [512], F32, kind="ExternalInput").ap()
    ka_d = dt("rw_ka", [512], F32, kind="ExternalInput").ap()
    rk_d = dt("rw_rk", [8, 64], F32, kind="ExternalInput").ap()
    lg_d = dt("rw_lnx_g", [512], F32, kind="ExternalInput").ap()
    lb_d = dt("rw_lnx_b", [512], F32, kind="ExternalInput").ap()
    w_out_d = dt("w_out", [D, D], F32, kind="ExternalInput").ap()
    ln1g_d = dt("ln1_g", [D], F32, kind="ExternalInput").ap()
    ln1b_d = dt("ln1_b", [D], F32, kind="ExternalInput").ap()
    wq_d = dt("peer_wq", [D, 2048], F32, kind="ExternalInput").ap()
    k1_d = dt("peer_k1", [8, 128, 128], F32, kind="ExternalInput").ap()
    k2_d = dt("peer_k2", [8, 128, 128], F32, kind="ExternalInput").ap()
    pu_d = dt("peer_u", [16384, D], F32, kind="ExternalInput").ap()
    pv_d = dt("peer_v", [16384, D], F32, kind="ExternalInput").ap()
    ln2g_d = dt("ln2_g", [D], F32, kind="ExternalInput").ap()
    ln2b_d = dt("ln2_b", [D], F32, kind="ExternalInput").ap()
    out_d = dt("out", [NB, SEQ, D], F32, kind="ExternalOutput").ap()

    banks = [P.ps("bank%d" % i, [128, 512], F32) for i in range(8)]
    bctr = [0]

    nrot = [8]

    def bank():
        i = bctr[0] % nrot[0]
        bctr[0] += 1
        return banks[i], ("B", i)

    def rsqrt(dst, src, eps, r, w, floor=None):
        P.act(lambda e: e.activation(dst, src, AF.Sqrt, bias=epsb[0:dst.shape[0], ekey[eps]:ekey[eps] + 1]), r=list(r) + ["epsb"], w=[w])
        if floor is not None:
            P.dve(lambda e: e.tensor_scalar_max(dst, dst, floor), r=[w], w=[w])
        P.dve(lambda e: e.reciprocal(dst, dst), r=[w], w=[w])

    epsb = P.sb("epsb", [128, 4], F32)
    ekey = {LN_EPS_: 0, GN_EPS_: 1, 0.0: 2}
    P.pool(lambda e: e.memset(epsb[:, 0:1], LN_EPS_), w=["epsb"])
    P.pool(lambda e: e.memset(epsb[:, 1:2], GN_EPS_), w=["epsb"])
    P.pool(lambda e: e.memset(epsb[:, 2:3], 0.0), w=["epsb"])
    ident = P.sb("ident", [128, 128], F32)
    identb = P.sb("identb", [128, 128], BF16)
    iota_p = P.sb("iota_p", [128, 1], F32)
    iota_f = P.sb("iota_f", [128, 128], F32)
    P.pool(lambda e: e.iota(iota_p[:], [[0, 1]], base=0, channel_multiplier=1,
                            allow_small_or_imprecise_dtypes=True), w=["iota_p"])
    P.pool(lambda e: e.iota(iota_f[:], [[1, 128]], base=0, channel_multiplier=0,
                            allow_small_or_imprecise_dtypes=True), w=["iota_f"])
    P.dve(lambda e: e.tensor_scalar(ident[:], iota_f[:], iota_p[:, 0:1], None, ALU.is_equal),
          r=["iota_p", "iota_f"], w=["ident"])
    P.dve(lambda e: e.tensor_copy(identb[:], ident[:]), r=["ident"], w=["identb"])
    m_st = P.sb("m_st", [64, 64], F32)
    m_in = P.sb("m_in", [64, 64], F32)
    m_lo = P.sb("m_lo", [64, 64], F32)
    P.dve(lambda e: e.tensor_scalar(m_st[:], iota_f[0:64, 0:64], iota_p[0:64, 0:1], None, ALU.is_gt),
          r=["iota_p", "iota_f"], w=["m_st"])
    P.dve(lambda e: e.tensor_scalar(m_in[:], iota_f[0:64, 0:64], iota_p[0:64, 0:1], None, ALU.is_ge),
          r=["iota_p", "iota_f"], w=["m_in"])
    P.dve(lambda e: e.tensor_scalar(m_lo[:], iota_f[0:64, 0:64], iota_p[0:64, 0:1], None, ALU.is_lt),
          r=["iota_p", "iota_f"], w=["m_lo"])
    ones_c = P.sb("ones_c", [128, 128], F32)
    ones_h = P.sb("ones_h", [64, 64], F32)
    ones_g = P.sb("ones_g", [64, 64], F32)
    P.pool(lambda e: e.memset(ones_c[:], 1.0 / 512.0), w=["ones_c"])
    P.pool(lambda e: e.memset(ones_h[:], 1.0), w=["ones_h"])
    P.pool(lambda e: e.memset(ones_g[:], 1.0 / 64.0), w=["ones_g"])
    rmask = P.sb("rmask", [64, 2, 64], F32)
    P.pool(lambda e: e.memset(rmask[:], 1.0), w=["rmask"])
    P.pool(lambda e: e.memset(rmask[:, :, 0:1], 0.0), w=["rmask"])

    big = P.sb("big", [128, 2048], F32)
    stA = P.sb("stA", [64, 128], F32)
    stB = P.sb("stB", [88, 64], F32)
    PA = P.sb("PA", [128, 64], F32)
    PB = P.sb("PB", [64, 88], F32)
    P.pool(lambda e: e.memset(stA[:], 0.0), w=["stA"])
    P.pool(lambda e: e.memset(stB[:], 0.0), w=["stB"])
    rowsA = [(cond_b_d, 0, 48), (conv_b_d, 48, 4), (cg_d, 52, 4), (cb_d, 56, 4)]
    for (src, r0, n) in rowsA:
        P.dma("sp", (lambda e, src=src, r0=r0, n=n: e.dma_start(
            out=stA[r0:r0 + n, :], in_=src.rearrange("(c p) -> c p", p=128))), w=["stA"], lane="stA")
    P.dma("sp", lambda e: e.dma_start(out=stA[60:61, :], in_=mu_d[1664:1792].rearrange("(c p) -> c p", p=128)),
          w=["stA"], lane="stA")
    rowsB = [(mu_d[0:1536], 0, 24), (w0_d, 24, 8), (a0_d, 32, 8), (kk_d, 40, 8), (ka_d, 48, 8),
             (lg_d, 56, 8), (lb_d, 64, 8), (mu_d[1536:1664], 80, 2)]
    for (src, r0, n) in rowsB:
        P.dma("sp", (lambda e, src=src, r0=r0, n=n: e.dma_start(
            out=stB[r0:r0 + n, :], in_=src.rearrange("(c p) -> c p", p=64))), w=["stB"], lane="stB")
    P.dma("sp", lambda e: e.dma_start(out=stB[72:80, :], in_=rk_d), w=["stB"], lane="stB")
    bk, bu = bank()
    P.pe(lambda e: e.transpose(bk[:, 0:64], stA[:, :], ident[0:64, 0:64]), r=["stA", "ident"], w=[bu])
    P.dve(lambda e: e.tensor_copy(PA[:], bk[:, 0:64]), r=[bu], w=["PA"])
    bk2, bu2 = bank()
    P.pe(lambda e: e.transpose(bk2[0:64, 0:88], stB[:, :], ident[0:88, 0:88]), r=["stB", "ident"], w=[bu2])
    P.dve(lambda e: e.tensor_copy(PB[:], bk2[0:64, 0:88]), r=[bu2], w=["PB"])
    OMB = P.sb("OMB", [64, 88], F32)
    P.dve(lambda e: e.tensor_scalar(OMB[:], PB[:], -1.0, 1.0, ALU.mult, ALU.add), r=["PB"], w=["OMB"])
    OMA = P.sb("OMA", [128, 64], F32)
    P.dve(lambda e: e.tensor_scalar(OMA[:], PA[:], -1.0, 1.0, ALU.mult, ALU.add), r=["PA"], w=["OMA"])
    CW = P.sb("CW", [128, 4, 31], F32)
    P.dma("sp", lambda e: e.dma_start(out=big[0:31, 0:512], in_=conv_w_d), w=["big"], lane="big")
    for c in range(4):
        bk, bu = bank()
        P.pe(lambda e, bk=bk, c=c: e.transpose(bk[:, 0:31], big[0:31, c * 128:(c + 1) * 128], ident[0:31, 0:31]),
             r=["big", "ident"], w=[bu])
        P.dve(lambda e, bk=bk, c=c: e.tensor_copy(CW[:, c, :], bk[:, 0:31]), r=[bu], w=["CW"])

    siluT = P.sb("siluT", [128, 8, 4], F32)
    MOD = P.sb("MOD", [128, 48, 4], F32)
    P.dma("sp", lambda e: e.dma_start(out=big[0:4, 0:D], in_=c_d), w=["big"], lane="big")
    bk, bu = bank()
    for kc in range(8):
        P.pe(lambda e, bk=bk, kc=kc: e.transpose(bk[:, kc * 4:(kc + 1) * 4], big[0:4, kc * 128:(kc + 1) * 128],
                                                 ident[0:4, 0:4]), r=["big", "ident"], w=[bu])
    P.act(lambda e, bk=bk: e.activation(siluT[:].rearrange("p a b -> p (a b)"), bk[:, 0:32], AF.Silu),
          r=[bu], w=["siluT"])
    bkm, bum = bank()
    for kc in range(8):
        for q in range(3):
            P.dma("sp", (lambda e, kc=kc, q=q: e.dma_start(
                out=big[:, :], in_=cond_w_d[kc * 128:(kc + 1) * 128, q * 2048:(q + 1) * 2048])), w=["big"], lane="big")
            for mm_ in range(16):
                m = q * 16 + mm_
                P.pe(lambda e, kc=kc, m=m, mm_=mm_: e.matmul(bkm[:, m * 4:(m + 1) * 4], big[:, mm_ * 128:(mm_ + 1) * 128],
                                                    siluT[:, kc, :], start=(kc == 0 and m == 0),
                                                    stop=(kc == 7 and m == 47), skip_group_check=True),
                     r=["big", "siluT"], w=[bum])
    P.dve(lambda e: e.tensor_tensor(MOD[:], bkm[:, 0:192].rearrange("p (m b) -> p m b", b=4),
                                    PA[:, 0:48].unsqueeze(2).broadcast_to([128, 48, 4]), ALU.add),
          r=[bum, "PA"], w=["MOD"])
    for lo in (8, 32):
        P.dve(lambda e, lo=lo: e.tensor_scalar_add(MOD[:, lo:lo + 8, :], MOD[:, lo:lo + 8, :], 1.0),
              r=["MOD"], w=["MOD"])
    GROW = P.sb("GROW", [4, 1, D], F32)
    for gi, lo in enumerate((16,)):
        for half in range(2):
            bk, bu = bank()
            for c in range(4):
                P.pe(lambda e, bk=bk, c=c, lo=lo, half=half: e.transpose(
                    bk[0:4, c * 128:(c + 1) * 128], MOD[:, lo + half * 4 + c, :], ident[:, :]),
                    r=["MOD", "ident"], w=[bu])
            P.dve(lambda e, bk=bk, gi=gi, half=half: e.tensor_copy(GROW[:, gi, half * 512:(half + 1) * 512],
                                                                   bk[0:4, :]), r=[bu], w=["GROW"])
    SEL = P.sb("SEL", [4, 4, 128], F32)
    P.dve(lambda e: e.tensor_copy(SEL[:], ident[0:4, 0:4].unsqueeze(2).broadcast_to([4, 4, 128])),
          r=["ident"], w=["SEL"])

    LG1 = P.sb("LG1", [128, D], F32)
    LB1 = P.sb("LB1", [128, D], F32)
    P.dma("sp", lambda e: e.dma_start(out=LG1[:], in_=ln1g_d.partition_broadcast(128)), w=["LG1"], lane="LG1")
    P.dma("sp", lambda e: e.dma_start(out=LB1[:], in_=ln1b_d.partition_broadcast(128)), w=["LB1"], lane="LB1")
    st6 = P.sb("st6", [128, 2, 6], F32)
    mv = P.sb("mv", [128, 2], F32)
    rstd = P.sb("rstd", [128, 1], F32)
    P.push()
    w_in_bf = P.sb("w_in_bf", [128, 8, 2816], BF16)
    for kc in range(8):
        for q in range(2):
            P.dma("sp", lambda e, kc=kc, q=q: e.dma_start(out=big[:, 0:1408], in_=w_in_d[kc * 128:(kc + 1) * 128, q * 1408:(q + 1) * 1408]),
                  w=["big"], lane="big")
            P.act(lambda e, kc=kc, q=q: e.activation(w_in_bf[:, kc, q * 1408:(q + 1) * 1408], big[:, 0:1408], AF.Copy), r=["big"], w=["w_in_bf"])
    w_oc_bf = P.sb("w_oc_bf", [128, 4, D], BF16)
    w_or_bf = P.sb("w_or_bf", [64, 8, D], BF16)
    for c in range(4):
        P.dma("sp", lambda e, c=c: e.dma_start(out=big[:, 0:D], in_=w_out_d[c * 128:(c + 1) * 128, :]),
              w=["big"], lane="big")
        P.dve(lambda e, c=c: e.tensor_copy(w_oc_bf[:, c, :], big[:, 0:D]), r=["big"], w=["w_oc_bf"])
    for h in range(8):
        P.dma("sp", lambda e, h=h: e.dma_start(out=big[0:64, 0:D], in_=w_out_d[512 + h * 64:512 + (h + 1) * 64, :]),
              w=["big"], lane="big")
        P.dve(lambda e, h=h: e.tensor_copy(w_or_bf[:, h, :], big[0:64, 0:D]), r=["big"], w=["w_or_bf"])
    w2_bf = P.sb("w2_bf", [64, 512], BF16)
    a2_bf = P.sb("a2_bf", [64, 512], BF16)
    g2_bf = P.sb("g2_bf", [128, 512], BF16)
    for (src, dst, np_, nm) in ((w2_d, w2_bf, 64, "w2_bf"), (a2_d, a2_bf, 64, "a2_bf"), (g2_d, g2_bf, 128, "g2_bf")):
        P.dma("sp", lambda e, src=src, np_=np_: e.dma_start(out=big[0:np_, 0:512], in_=src), w=["big"], lane="big")
        P.dve(lambda e, dst=dst, np_=np_: e.tensor_copy(dst[:], big[0:np_, 0:512]), r=["big"], w=[nm])

    x1_d = out_d

    xb = P.sb("xb", [128, D], F32)
    xn = P.sb("xn", [128, D], F32)
    hT = [P.sb("hT%d" % i, [128, 8, TB + 1], BF16) for i in range(2)]
    ub = [P.sb("ub%d" % i, [128, 4, 30 + TB], F32) for i in range(2)]
    sig = P.sb("sig", [128, TB], F32)
    acc = P.sb("acc", [128, 4, TB], F32)
    sq = P.sb("sq", [128, 4, TB], F32)
    cm = P.sb("cm", [128, TB], F32)
    cr = P.sb("cr", [128, TB], F32)
    ct = P.sb("ct", [128, TB], F32)
    ycT = P.sb("ycT", [128, 4, TB], BF16)
    tmpA = P.sb("tmpA", [128, TB], F32)
    F = [P.sb("F%d" % i, [64, 8, TB], F32) for i in range(11)]

    def role(name, idx):
        P.alias[name] = "F%d" % idx
        return F[idx]
    VBb = P.sb("VBb", [64, 8, TB], BF16)
    TW = P.sb("TW", [64, TB], BF16)
    ADb = P.sb("ADb", [64, TB], BF16)
    SGD = P.sb("SGD", [128, TB], BF16)
    RT = P.sb("RT", [64, 8, TB], BF16)
    KT = P.sb("KT", [64, 8, TB], BF16)
    BT = P.sb("BT", [64, 8, TB], BF16)
    ATl = P.sb("ATl", [64, 8, TB], BF16)
    YRW = P.sb("YRW", [64, 8, TB], BF16)
    Vt = P.sb("Vt", [64, 8, 64], BF16)
    Ktm = P.sb("Ktm", [64, 8, 64], BF16)
    Btm = P.sb("Btm", [64, 8, 64], BF16)
    AabT = P.sb("AabT", [64, 8, 64], BF16)
    ArbT = P.sb("ArbT", [64, 8, 64], BF16)
    AakT = P.sb("AakT", [64, 8, 64], BF16)
    ArkT = P.sb("ArkT", [64, 8, 64], BF16)
    Npw = [P.sb("Npw%d" % i, [64, 8, 64], BF16) for i in range(2)]
    NTpw = [P.sb("NTpw%d" % i, [64, 8, 64], BF16) for i in range(2)]
    PT = [P.sb("PT%d" % i, [64, 8, 64], BF16) for i in range(2)]
    Xb = P.sb("Xb", [64, 8, 64], BF16)
    Ub = P.sb("Ub", [64, 8, 64], BF16)
    S0 = P.sb("S0", [64, 8, 64], F32)
    S0b = P.sb("S0b", [64, 8, 64], BF16)
    Stmp = P.sb("Stmp", [64, 8, 64], F32)
    G1 = P.sb("G1", [128, D], F32)
    identb8 = P.sb("identb8", [64, 8, 64], BF16)
    P.dve(lambda e: e.tensor_copy(identb8[:], ident[0:64, 0:64].unsqueeze(1).broadcast_to([64, 8, 64])),
          r=["ident"], w=["identb8"])

    def pbc(col):
        return PB[:, col:col + 8].unsqueeze(2).broadcast_to([64, 8, TB])

    def ombc(col):
        return OMB[:, col:col + 8].unsqueeze(2).broadcast_to([64, 8, TB])

    def m8(m):
        return m[:, :].unsqueeze(1).broadcast_to([64, 8, 64])

    out_ops = []
    xbufs = [(xb, "xb"), (big[:, 0:D], "bigA")]
    xo = big[:, D:2 * D]

    def head(b, blk, first):
        xbuf, xbn = xbufs[blk % 2]
        xw = [xbn, "big"] if first else [xbn]
        par = blk % 2
        hTc, hTn = hT[par], hT[1 - par]
        hu, hun = "hT%d" % par, "hT%d" % (1 - par)
        ubc, ubn = ub[par], ub[1 - par]
        uu, uun = "ub%d" % par, "ub%d" % (1 - par)
        if blk == 0:
            P.pool(lambda e: e.memset(hTc[:, :, 0:1], 0.0), w=[hu])
            P.pool(lambda e: e.memset(ubc[:, :, 0:30], 0.0), w=[uu])
        t0 = blk * TB
        P.dma("sp", lambda e, b=b, t0=t0: e.dma_start(out=xbuf[:], in_=x_d[b, t0:t0 + TB, :]), w=xw, lane=xbn)
        for hh in range(2):
            P.dve(lambda e, hh=hh: e.bn_stats(st6[:, hh, :], xbuf[:, hh * 512:(hh + 1) * 512]), r=[xbn], w=["st6"])
        P.dve(lambda e: e.bn_aggr(mv[:], st6[:].rearrange("p a b -> p (a b)")), r=["st6"], w=["mv"])
        rsqrt(rstd[:], mv[:, 1:2], LN_EPS_, ["mv"], "rstd")
        P.dve(lambda e: e.tensor_scalar(xn[:], xbuf[:], mv[:, 0:1], rstd[:, 0:1], ALU.subtract, ALU.mult),
              r=[xbn, "mv", "rstd"], w=["xn"])
        for half in range(2):
            bk, bu = bank()
            for j in range(4):
                fc = half * 4 + j
                P.pe(lambda e, bk=bk, j=j, fc=fc: e.transpose(bk[:, j * 128:(j + 1) * 128],
                                                              xn[:, fc * 128:(fc + 1) * 128], ident[:, :]),
                     r=["xn", "ident"], w=[bu])
            for j in range(4):
                fc = half * 4 + j
                P.act(lambda e, bk=bk, j=j, fc=fc, b=b, hTc=hTc: e.activation(
                    hTc[:, fc, 1:TB + 1], bk[:, j * 128:(j + 1) * 128], AF.Identity,
                    bias=MOD[:, fc, b:b + 1], scale=MOD[:, 8 + fc, b:b + 1]), r=[bu, "MOD"], w=[hu])
        P.pool(lambda e, hTc=hTc, hTn=hTn: e.tensor_copy(hTn[:, :, 0:1], hTc[:, :, TB:TB + 1]), r=[hu], w=[hun])


    blocks = [(b, blk) for b in range(nb_run) for blk in range(nblk_run)]
    head(blocks[0][0], blocks[0][1], True)
    for kblk, (b, blk) in enumerate(blocks):
        if blk == 0:
            for half in range(2):
                bk, bu = bank()
                P.pe(lambda e, bk=bk, b=b, half=half: e.matmul(bk[:, :], SEL[:, b, :], GROW[:, 0, half * 512:(half + 1) * 512],
                                                               start=True, stop=True), r=["SEL", "GROW"], w=[bu])
                P.act(lambda e, bk=bk, half=half: e.activation(G1[:, half * 512:(half + 1) * 512], bk[:, :], AF.Copy),
                      r=[bu], w=["G1"])
            P.pool(lambda e: e.memset(S0[:], 0.0), w=["S0"])
            P.pool(lambda e: e.memset(S0b[:], 0.0), w=["S0b"])
        par = blk % 2
        hTc, hTn = hT[par], hT[1 - par]
        hu, hun = "hT%d" % par, "hT%d" % (1 - par)
        ubc, ubn = ub[par], ub[1 - par]
        uu, uun = "ub%d" % par, "ub%d" % (1 - par)
        t0 = blk * TB
        xbuf, xbn = xbufs[par]
        RB = role("RB", 0); KB = role("KB", 1); VB = role("VB", 2); GB = role("GB", 3)
        SGW = role("SGW", 4); AB = role("AB", 5); T1 = role("T1", 9)
        def inproj(col0, M, N0):
            bk, bu = bank()
            for kc in range(8):
                P.pe(lambda e, bk=bk, kc=kc, col0=col0, M=M, N0=N0, hTc=hTc: e.matmul(
                    bk[0:M, 0:TB + 1 - N0], w_in_bf[:, kc, col0:col0 + M], hTc[:, kc, N0:TB + 1],
                    start=(kc == 0), stop=(kc == 7)), r=["w_in_bf", hu], w=[bu])
            return bk, bu

        for c in range(4):
            bkg, bug = inproj(512 + c * 128, 128, 1)
            P.act(lambda e, bkg=bkg: e.activation(sig[:], bkg[:, 0:TB], AF.Sigmoid), r=[bug], w=["sig"])
            bkv, buv = inproj(c * 128, 128, 1)
            P.dve(lambda e, bkv=bkv, c=c, ubc=ubc: e.tensor_tensor(ubc[:, c, 30:30 + TB], bkv[:, 0:TB], sig[:], ALU.mult),
                  r=[buv, "sig"], w=[uu])
        P.pool(lambda e, ubc=ubc, ubn=ubn: e.tensor_copy(ubn[:, :, 0:30], ubc[:, :, TB:TB + 30]), r=[uu], w=[uun])

        def shift_evac(bk, bu, Mp, dst, dname, mucol, omcol):
            P.act(lambda e: e.activation(tmpA[0:Mp, :], bk[0:Mp, 1:TB + 1], AF.Identity, scale=omcol),
                  r=[bu, "OMB", "OMA"], w=["tmpA"])
            P.dve(lambda e: e.scalar_tensor_tensor(dst, bk[0:Mp, 0:TB], mucol, tmpA[0:Mp, :], ALU.mult, ALU.add),
                  r=[bu, "tmpA", "PB", "PA"], w=[dname])

        for wi, (dstT, dn) in enumerate(((RB, "RB"), (KB, "KB"), (VB, "VB"))):
            for h in range(8):
                bk, bu = inproj(1024 + wi * 512 + h * 64, 64, 0)
                shift_evac(bk, bu, 64, dstT[:, h, :], dn, PB[:, wi * 8 + h:wi * 8 + h + 1],
                           OMB[:, wi * 8 + h:wi * 8 + h + 1])
        bk, bu = inproj(2560, 64, 0)
        shift_evac(bk, bu, 64, T1[:, 0, :], "T1", PB[:, 80:81], OMB[:, 80:81])
        P.act(lambda e: e.activation(TW[:], T1[:, 0, :], AF.Tanh), r=["T1"], w=["TW"])
        bk, bu = inproj(2624, 64, 0)
        shift_evac(bk, bu, 64, T1[:, 1, :], "T1", PB[:, 81:82], OMB[:, 81:82])
        P.act(lambda e: e.activation(ADb[:], T1[:, 1, :], AF.Copy), r=["T1"], w=["ADb"])
        bk, bu = inproj(2688, 128, 0)
        shift_evac(bk, bu, 128, sq[:, 0, :], "sq", PA[:, 60:61], OMA[:, 60:61])
        P.act(lambda e: e.activation(SGD[:], sq[:, 0, :], AF.Sigmoid), r=["sq"], w=["SGD"])
        for h in range(8):
            bk, bu = bank()
            P.pe(lambda e, bk=bk, h=h: e.matmul(bk[0:64, 0:TB], w2_bf[:, h * 64:(h + 1) * 64], TW[:], start=True, stop=True),
                 r=["w2_bf", "TW"], w=[bu])
            P.act(lambda e, bk=bk, h=h: e.activation(SGW[:, h, :], bk[0:64, 0:TB], AF.Sigmoid, bias=PB[:, 24 + h:25 + h]),
                  r=[bu, "PB"], w=["SGW"])
            bk, bu = bank()
            P.pe(lambda e, bk=bk, h=h: e.matmul(bk[0:64, 0:TB], a2_bf[:, h * 64:(h + 1) * 64], ADb[:], start=True, stop=True),
                 r=["a2_bf", "ADb"], w=[bu])
            P.act(lambda e, bk=bk, h=h: e.activation(AB[:, h, :], bk[0:64, 0:TB], AF.Sigmoid, bias=PB[:, 32 + h:33 + h]),
                  r=[bu, "PB"], w=["AB"])
            bk, bu = bank()
            P.pe(lambda e, bk=bk, h=h: e.matmul(bk[0:64, 0:TB], g2_bf[:, h * 64:(h + 1) * 64], SGD[:], start=True, stop=True),
                 r=["g2_bf", "SGD"], w=[bu])
            P.act(lambda e, bk=bk, h=h: e.activation(GB[:, h, :], bk[0:64, 0:TB], AF.Copy), r=[bu], w=["GB"])

        def conv_gen(ubc=ubc, uu=uu):
            for c in range(4):
                P.dve(lambda e, c=c: e.tensor_scalar(acc[:, c, :], ubc[:, c, 0:TB], CW[:, c, 0:1], PA[:, 48 + c:49 + c],
                                                     ALU.mult, ALU.add), r=[uu, "CW", "PA"], w=[("acc", c)])
            yield
            for j in range(1, 1 if "C" in SKIP else 31):
                for c in range(4):
                    P.dve(lambda e, c=c, j=j: e.scalar_tensor_tensor(acc[:, c, :], ubc[:, c, j:j + TB], CW[:, c, j:j + 1],
                                                                     acc[:, c, :], ALU.mult, ALU.add),
                          r=[uu, "CW", ("acc", c)], w=[("acc", c)])
                yield
        cgen = conv_gen()

        def pull():
            next(cgen, None)
        CUM = role("CUM", 6); EG = role("EG", 7); IEG = role("IEG", 8); EGX = role("EGX", 10)
        fl = lambda t: t[:].rearrange("p h t -> p (h t)")
        for h in range(8):
            P.dve(lambda e, h=h: e.tensor_tensor_scan(CUM[:, h, :], rmask[:].rearrange("p c t -> p (c t)"), SGW[:, h, :], 0.0,
                                                      ALU.mult, ALU.add), r=["rmask", "SGW"], w=["CUM"])
        P.act(lambda e: e.activation(fl(EG), fl(CUM), AF.Exp, scale=-LWC), r=["CUM"], w=["EG"])
        P.act(lambda e: e.activation(fl(IEG), fl(CUM), AF.Exp, scale=LWC), r=["CUM"], w=["IEG"])
        P.pool(lambda e: e.tensor_tensor(fl(T1), fl(CUM), fl(SGW), ALU.subtract), r=["CUM", "SGW"], w=["T1"])
        P.act(lambda e: e.activation(fl(EGX), fl(T1), AF.Exp, scale=-LWC), r=["T1"], w=["EGX"])
        KKR = role("KKR", 4); SQ2 = role("SQ2", 6); KKN = role("KKN", 9)
        P.dve(lambda e: e.tensor_tensor(KKR[:], KB[:], pbc(40), ALU.mult), r=["KB", "PB"], w=["KKR"])
        P.dve(lambda e: e.tensor_tensor(SQ2[:], KKR[:], KKR[:], ALU.mult), r=["KKR"], w=["SQ2"])
        for half in range(2):
            bk, bu = bank()
            P.pe(lambda e, bk=bk, half=half: e.matmul(bk[0:64, :], ones_h[:], fl(SQ2)[:, half * 512:(half + 1) * 512],
                                                      start=True, stop=True), r=["ones_h", "SQ2"], w=[bu])
            rsqrt(fl(KKN)[:, half * 512:(half + 1) * 512], bk[0:64, :], 0.0, [bu], "KKN", floor=1e-12)
        P.dve(lambda e: e.tensor_tensor(KKN[:], KKN[:], KKR[:], ALU.mult), r=["KKN", "KKR"], w=["KKN"])
        T1 = role("T1", 4); KF = role("KF", 6)
        P.pool(lambda e: e.tensor_tensor(T1[:], AB[:], pbc(48), ALU.mult), r=["AB", "PB"], w=["T1"])
        P.pool(lambda e: e.tensor_tensor(T1[:], T1[:], ombc(48), ALU.add), r=["T1", "OMB"], w=["T1"])
        P.dve(lambda e: e.tensor_tensor(KF[:], KB[:], T1[:], ALU.mult), r=["KB", "T1"], w=["KF"])
        BBt = role("BBt", 1)
        P.dve(lambda e: e.tensor_tensor(BBt[:], KKN[:], AB[:], ALU.mult), r=["KKN", "AB"], w=["BBt"])
        P.dve(lambda e: e.tensor_tensor(RT[:], RB[:], EG[:], ALU.mult), r=["RB", "EG"], w=["RT"])
        P.pool(lambda e: e.tensor_tensor(KT[:], KF[:], IEG[:], ALU.mult), r=["KF", "IEG"], w=["KT"])
        P.dve(lambda e: e.tensor_tensor(BT[:], BBt[:], IEG[:], ALU.mult), r=["BBt", "IEG"], w=["BT"])
        P.dve(lambda e: e.scalar_tensor_tensor(ATl[:], KKN[:], -1.0, EGX[:], ALU.mult, ALU.mult),
               r=["KKN", "EGX"], w=["ATl"])
        P.act(lambda e: e.activation(fl(VBb), fl(VB), AF.Copy), r=["VB"], w=["VBb"])
        SQ2 = role("SQ2", 8); BON = role("BON", 4); YB = role("YB", 5)
        P.dve(lambda e: e.tensor_tensor(SQ2[:], RB[:], KF[:], ALU.mult), r=["RB", "KF"], w=["SQ2"])
        P.dve(lambda e: e.tensor_tensor(SQ2[:], SQ2[:], pbc(72), ALU.mult), r=["SQ2", "PB"], w=["SQ2"])
        for half in range(2):
            bk, bu = bank()
            P.pe(lambda e, bk=bk, half=half: e.matmul(bk[0:64, :], ones_h[:], fl(SQ2)[:, half * 512:(half + 1) * 512],
                                                      start=True, stop=True), r=["ones_h", "SQ2"], w=[bu])
            P.dve(lambda e, bk=bk, half=half: e.tensor_tensor(fl(BON)[:, half * 512:(half + 1) * 512], bk[0:64, :],
                                                              fl(VB)[:, half * 512:(half + 1) * 512], ALU.mult),
                  r=[bu, "VB"], w=["BON"])

        for cc in range(0 if "R" in SKIP else TB // CH):
            c0 = cc * CH
            cs = slice(c0, c0 + CH)

            def bfview(bk):
                return bk[0:64, 0:256].bitcast(BF16).rearrange("p (h t) -> p h t", h=8)

            for (srcT, sn, dstT, dn) in ((VBb, "VBb", Vt, "Vt"), (KT, "KT", Ktm, "Ktm"), (BT, "BT", Btm, "Btm")):
                bk, bu = bank()
                for h in range(8):
                    P.pe(lambda e, bk=bk, h=h, srcT=srcT: e.transpose(bfview(bk)[:, h, :], srcT[:, h, cs], identb[0:64, 0:64]),
                         r=[sn, "identb"], w=[bu])
                P.act(lambda e, bk=bk, dstT=dstT: e.activation(dstT[:], bfview(bk), AF.Copy), r=[bu], w=[dn])

            def amat(lhsT_, ln, rhs_, rn, mask, mn, dst, dn, eng):
                bk, bu = bank()
                for h in range(8):
                    P.pe(lambda e, bk=bk, h=h: e.matmul(bk[0:64, h * 64:(h + 1) * 64], lhsT_[:, h, cs], rhs_[:, h, cs],
                                                        start=True, stop=True), r=[ln, rn], w=[bu])
                eng(lambda e, bk=bk: e.tensor_tensor(dst[:], bk[0:64, :].rearrange("p (h t) -> p h t", h=8), m8(mask), ALU.mult),
                    r=[bu, mn], w=[dn])
                pull()

            amat(BT, "BT", ATl, "ATl", m_st, "m_st", NTpw[0], "NT0", P.dve)
            amat(ATl, "ATl", BT, "BT", m_lo, "m_lo", Npw[0], "N0", P.dve)
            amat(BT, "BT", RT, "RT", m_in, "m_in", ArbT, "ArbT", P.dve)
            amat(KT, "KT", ATl, "ATl", m_st, "m_st", AakT, "AakT", P.dve)
            amat(KT, "KT", RT, "RT", m_in, "m_in", ArkT, "ArkT", P.dve)

            def mm8(lhs, ln, rhs, rn, dst, dn, add=None, an=None):
                bk, bu = bank()
                for h in range(8):
                    P.pe(lambda e, bk=bk, h=h: e.matmul(bk[0:64, h * 64:(h + 1) * 64], lhs[:, h, :], rhs[:, h, :],
                                                        start=True, stop=True), r=[ln, rn], w=[bu])
                v = bk[0:64, :].rearrange("p (h t) -> p h t", h=8)
                if add is None:
                    P.act(lambda e: e.activation(dst[:], v, AF.Copy), r=[bu], w=[dn])
                else:
                    P.dve(lambda e: e.tensor_tensor(dst[:], v, add[:], ALU.add), r=[bu, an], w=[dn])
                pull()

            P.dve(lambda e: e.tensor_tensor(PT[0][:], NTpw[0][:], identb8[:], ALU.add), r=["NT0", "identb8"], w=["PT0"])
            cur = 0
            for i in range(5):
                a_, b_ = i % 2, (i + 1) % 2
                mm8(NTpw[a_], "NT%d" % a_, Npw[a_], "N%d" % a_, Npw[b_], "N%d" % b_)
                if i < 4:
                    mm8(Npw[a_], "N%d" % a_, NTpw[a_], "NT%d" % a_, NTpw[b_], "NT%d" % b_)
                mm8(Npw[b_], "N%d" % b_, PT[cur], "PT%d" % cur, PT[1 - cur], "PT%d" % (1 - cur),
                    add=PT[cur], an="PT%d" % cur)
                cur = 1 - cur
            PTf, PTn = PT[cur], "PT%d" % cur
            bk, bu = bank()
            for h in range(8):
                P.pe(lambda e, bk=bk, h=h: e.matmul(bk[0:64, h * 64:(h + 1) * 64], ATl[:, h, cs], S0b[:, h, :],
                                                    start=True, stop=False), r=["ATl", "S0b"], w=[bu])
                P.pe(lambda e, bk=bk, h=h: e.matmul(bk[0:64, h * 64:(h + 1) * 64], AakT[:, h, :], Vt[:, h, :],
                                                    start=False, stop=True), r=["AakT", "Vt"], w=[bu])
            P.act(lambda e, bk=bk: e.activation(Xb[:], bk[0:64, :].rearrange("p (h t) -> p h t", h=8), AF.Copy),
                  r=[bu], w=["Xb"])
            mm8(PTf, PTn, Xb, "Xb", Ub, "Ub")
            bk, bu = bank()
            for h in range(8):
                o_ = bk[0:64, h * 64:(h + 1) * 64]
                P.pe(lambda e, o_=o_, h=h: e.matmul(o_, S0b[:, h, :], RT[:, h, cs], start=True, stop=False),
                     r=["S0b", "RT"], w=[bu])
                P.pe(lambda e, o_=o_, h=h: e.matmul(o_, Ub[:, h, :], ArbT[:, h, :], start=False, stop=False),
                     r=["Ub", "ArbT"], w=[bu])
                P.pe(lambda e, o_=o_, h=h: e.matmul(o_, Vt[:, h, :], ArkT[:, h, :], start=False, stop=True),
                     r=["Vt", "ArkT"], w=[bu])
            P.act(lambda e, bk=bk: e.activation(YB[:, :, cs], bk[0:64, :].rearrange("p (h t) -> p h t", h=8), AF.Copy),
                  r=[bu], w=["YB"])
            bk, bu = bank()
            for h in range(8):
                o_ = bk[0:64, h * 64:(h + 1) * 64]
                P.pe(lambda e, o_=o_, h=h: e.matmul(o_, Btm[:, h, :], Ub[:, h, :], start=True, stop=False),
                     r=["Btm", "Ub"], w=[bu])
                P.pe(lambda e, o_=o_, h=h: e.matmul(o_, Ktm[:, h, :], Vt[:, h, :], start=False, stop=True),
                     r=["Ktm", "Vt"], w=[bu])
            P.dve(lambda e, bk=bk: e.tensor_tensor(Stmp[:], bk[0:64, :].rearrange("p (h t) -> p h t", h=8), S0[:], ALU.add),
                  r=[bu, "S0"], w=["Stmp"])
            P.dve(lambda e: e.tensor_tensor(S0[:], Stmp[:], EG[:, :, c0 + CH - 1:c0 + CH].broadcast_to([64, 8, 64]), ALU.mult),
                  r=["Stmp", "EG"], w=["S0"])
            P.act(lambda e: e.activation(S0b[:], S0[:], AF.Copy), r=["S0"], w=["S0b"])

        T1 = role("T1", 9); KF = role("KF", 10)
        P.pool(lambda e: e.tensor_tensor(SQ2[:], YB[:], YB[:], ALU.mult), r=["YB"], w=["SQ2"])
        for half in range(2):
            hs = slice(half * 512, (half + 1) * 512)
            bk1, bu1 = bank()
            P.pe(lambda e, bk1=bk1, hs=hs: e.matmul(bk1[0:64, :], ones_g[:], fl(YB)[:, hs], start=True, stop=True),
                 r=["ones_g", "YB"], w=[bu1])
            bk2_, bu2_ = bank()
            P.pe(lambda e, bk2_=bk2_, hs=hs: e.matmul(bk2_[0:64, :], ones_g[:], fl(SQ2)[:, hs], start=True, stop=True),
                 r=["ones_g", "SQ2"], w=[bu2_])
            P.dve(lambda e, bk1=bk1, hs=hs: e.tensor_copy(fl(T1)[:, hs], bk1[0:64, :]), r=[bu1], w=["T1"])
            P.dve(lambda e, hs=hs: e.tensor_tensor(fl(KF)[:, hs], fl(T1)[:, hs], fl(T1)[:, hs], ALU.mult), r=["T1"], w=["KF"])
            P.dve(lambda e, bk2_=bk2_, hs=hs: e.tensor_tensor(fl(KF)[:, hs], bk2_[0:64, :], fl(KF)[:, hs], ALU.subtract),
                  r=[bu2_, "KF"], w=["KF"])
        rsqrt(fl(KF), fl(KF), GN_EPS_, ["KF"], "KF")
        P.dve(lambda e: e.tensor_tensor(YB[:], YB[:], T1[:], ALU.subtract), r=["YB", "T1"], w=["YB"])
        P.dve(lambda e: e.tensor_tensor(YB[:], YB[:], KF[:], ALU.mult), r=["YB", "KF"], w=["YB"])
        P.dve(lambda e: e.tensor_tensor(YB[:], YB[:], pbc(56), ALU.mult), r=["YB", "PB"], w=["YB"])
        P.dve(lambda e: e.tensor_tensor(YB[:], YB[:], pbc(64), ALU.add), r=["YB", "PB"], w=["YB"])
        P.dve(lambda e: e.tensor_tensor(YB[:], YB[:], BON[:], ALU.add), r=["YB", "BON"], w=["YB"])
        P.dve(lambda e: e.tensor_tensor(YRW[:], YB[:], GB[:], ALU.mult), r=["YB", "GB"], w=["YRW"])

        if kblk + 1 < len(blocks):
            head(blocks[kblk + 1][0], blocks[kblk + 1][1], False)
        for _ in cgen:
            pass
        for c in range(4):
            P.act(lambda e, c=c: e.activation(sq[:, c, :], acc[:, c, :], AF.Square), r=[("acc", c)], w=["sq"])
        bkm_, bum_ = bank()
        for c in range(4):
            P.pe(lambda e, c=c, bkm_=bkm_: e.matmul(bkm_[:, 0:TB], ones_c[:], acc[:, c, :], start=(c == 0), stop=(c == 3)),
                 r=["ones_c", ("acc", c)], w=[bum_])
        bks_, bus_ = bank()
        for c in range(4):
            P.pe(lambda e, c=c, bks_=bks_: e.matmul(bks_[:, 0:TB], ones_c[:], sq[:, c, :], start=(c == 0), stop=(c == 3)),
                 r=["ones_c", "sq"], w=[bus_])
        P.dve(lambda e, bkm_=bkm_: e.tensor_copy(cm[:], bkm_[:, 0:TB]), r=[bum_], w=["cm"])
        P.dve(lambda e: e.tensor_tensor(ct[:], cm[:], cm[:], ALU.mult), r=["cm"], w=["ct"])
        P.dve(lambda e, bks_=bks_: e.tensor_tensor(cr[:], bks_[:, 0:TB], ct[:], ALU.subtract), r=[bus_, "ct"], w=["cr"])
        rsqrt(cr[:], cr[:], LN_EPS_, ["cr"], "cr")
        for c in range(4):
            P.pool(lambda e, c=c: e.tensor_tensor(sq[:, c, :], acc[:, c, :], cm[:], ALU.subtract),
                   r=[("acc", c), "cm", "sq"], w=["sq"])
            P.pool(lambda e, c=c: e.tensor_tensor(sq[:, c, :], sq[:, c, :], cr[:], ALU.mult), r=["sq", "cr"], w=["sq"])
            P.act(lambda e, c=c: e.activation(ycT[:, c, :], sq[:, c, :], AF.Silu, bias=PA[:, 56 + c:57 + c],
                                              scale=PA[:, 52 + c:53 + c]), r=["sq", "PA"], w=["ycT"])

        for half in range(2):
            hs = slice(half * 512, (half + 1) * 512)
            bk, bu = bank()
            for c in range(4):
                P.pe(lambda e, bk=bk, c=c, hs=hs: e.matmul(bk[:, :], ycT[:, c, :], w_oc_bf[:, c, hs], start=(c == 0), stop=False),
                     r=["ycT", "w_oc_bf"], w=[bu])
            for h in range(8):
                P.pe(lambda e, bk=bk, h=h, hs=hs: e.matmul(bk[:, :], YRW[:, h, :], w_or_bf[:, h, hs], start=False, stop=(h == 7)),
                     r=["YRW", "w_or_bf"], w=[bu])
            P.dve(lambda e, bk=bk, hs=hs: e.tensor_tensor(xo[:, hs], bk[:, :], G1[:, hs], ALU.mult), r=[bu, "G1"], w=["bigB"])
        P.dve(lambda e: e.scalar_tensor_tensor(xo, xbuf[:], ALPHA_, xo, ALU.mult, ALU.add), r=[xbn, "bigB"], w=["bigB"])
        for hh in range(2):
            P.dve(lambda e, hh=hh: e.bn_stats(st6[:, hh, :], xo[:, hh * 512:(hh + 1) * 512]), r=["bigB"], w=["st6"])
        P.dve(lambda e: e.bn_aggr(mv[:], st6[:].rearrange("p a b -> p (a b)")), r=["st6"], w=["mv"])
        rsqrt(rstd[:], mv[:, 1:2], LN_EPS_, ["mv"], "rstd")
        P.dve(lambda e: e.tensor_scalar(xo, xo, mv[:, 0:1], rstd[:, 0:1], ALU.subtract, ALU.mult),
              r=["bigB", "mv", "rstd"], w=["bigB"])
        P.dve(lambda e: e.tensor_tensor(xo, xo, LG1[:], ALU.mult), r=["bigB", "LG1"], w=["bigB"])
        P.dve(lambda e: e.tensor_tensor(xo, xo, LB1[:], ALU.add), r=["bigB", "LB1"], w=["bigB"])
        o = P.dma("sp", lambda e, b=b, t0=t0: e.dma_start(out=x1_d[b, t0:t0 + TB, :], in_=xo), r=["bigB"], w=[("x1d", b, blk)], lane="xn_out")
        out_ops.append(o)


    if stage == "A":
        P.finish(final_ops=out_ops[-1:])
        return

    P.pop()
    P.push()
    NT = 256
    NST = SEQ // NT
    NBUF = 3
    ut_scr = nc.dram_tensor("ut_scr", [128, 128, 8, 128], BF16).ap()
    v_scr = nc.dram_tensor("v_scr", [128, 128, D], BF16).ap()

    ust = [P.sb("ust%d" % i, [128, D], F32) for i in range(2)]
    vst = [P.sb("vst%d" % i, [128, D], F32) for i in range(2)]
    utb = [P.sb("utb%d" % i, [128, 8, 128], BF16) for i in range(2)]
    vbb = [P.sb("vbb%d" % i, [128, D], BF16) for i in range(2)]
    for c in range(128):
        i = c % 2
        P.dma("sp", lambda e, c=c, i=i: e.dma_start(out=ust[i][:], in_=pu_d[c * 128:(c + 1) * 128, :]), w=["ust%d" % i], lane="ust%d" % i)
        P.dma("sp", lambda e, c=c, i=i: e.dma_start(out=vst[i][:], in_=pv_d[c * 128:(c + 1) * 128, :]), w=["vst%d" % i], lane="vst%d" % i)
        for half in range(2):
            bk, bu = bank()
            for q in range(4):
                dc = half * 4 + q
                P.pe(lambda e, bk=bk, q=q, dc=dc, i=i: e.transpose(bk[:, q * 128:(q + 1) * 128], ust[i][:, dc * 128:(dc + 1) * 128], ident[:, :]),
                     r=["ust%d" % i, "ident"], w=[bu])
            if half == 0:
                P.act(lambda e, bk=bk, i=i: e.activation(utb[i][:, 0:4, :], bk[:, :].rearrange("p (q e) -> p q e", q=4), AF.Copy),
                      r=[bu], w=["utb%d" % i])
            else:
                P.dve(lambda e, bk=bk, i=i: e.tensor_copy(utb[i][:, 4:8, :], bk[:, :].rearrange("p (q e) -> p q e", q=4)),
                      r=[bu], w=["utb%d" % i])
        P.pool(lambda e, i=i: e.tensor_copy(vbb[i][:], vst[i][:]), r=["vst%d" % i], w=["vbb%d" % i])
        P.dma("sp", lambda e, c=c, i=i: e.dma_start(out=ut_scr[c], in_=utb[i][:]), r=["utb%d" % i], w=[("UTd", c)], lane="utb%d" % i)
        P.dma("sp", lambda e, c=c, i=i: e.dma_start(out=v_scr[c], in_=vbb[i][:]), r=["vbb%d" % i], w=[("Vd", c)], lane="vbb%d" % i)
    P.pop()
    P.push()

    wq_bf = P.sb("wq_bf", [128, 8, 2048], BF16)
    for kc in range(8):
        P.dma("sp", lambda e, kc=kc: e.dma_start(out=big[:, :], in_=wq_d[kc * 128:(kc + 1) * 128, :]), w=["big", "bigA", "bigB"], lane="big")
        P.act(lambda e, kc=kc: e.activation(wq_bf[:, kc, :], big[:, :], AF.Copy), r=["big"], w=["wq_bf"])
    K12 = P.sb("K12", [128, 16, 128], BF16)
    for s_, kd in enumerate((k1_d, k2_d)):
        for h in range(8):
            P.dma("sp", lambda e, kd=kd, h=h: e.dma_start(out=big[:, 0:128], in_=kd[h]), w=["big"], lane="big")
            bk, bu = bank()
            P.pe(lambda e, bk=bk: e.transpose(bk[:, 0:128], big[:, 0:128], ident[:, :]), r=["big", "ident"], w=[bu])
            P.dve(lambda e, bk=bk, s_=s_, h=h: e.tensor_copy(K12[:, s_ * 8 + h, :], bk[:, 0:128]), r=[bu], w=["K12"])
    P.dma("sp", lambda e: e.dma_start(out=LG1[:], in_=ln2g_d.partition_broadcast(128)), w=["LG1"], lane="LG1")
    P.dma("sp", lambda e: e.dma_start(out=LB1[:], in_=ln2b_d.partition_broadcast(128)), w=["LB1"], lane="LB1")
    for half in range(2):
        bk, bu = bank()
        for c in range(4):
            P.pe(lambda e, bk=bk, c=c, half=half: e.transpose(bk[0:4, c * 128:(c + 1) * 128], MOD[:, 40 + half * 4 + c, :], ident[:, :]),
                 r=["MOD", "ident"], w=[bu])
        P.dve(lambda e, bk=bk, half=half: e.tensor_copy(GROW[:, 0, half * 512:(half + 1) * 512], bk[0:4, :]), r=[bu], w=["GROW"])

    G2 = P.sb("G2", [128, D], F32)
    xs = P.sb("xs", [128, 2, D], F32)
    xn2 = P.sb("xn2", [128, D], F32)
    h2T = [P.sb("h2T%d" % i, [128, 8, NT], BF16) for i in range(2)]
    qT = P.sb("qT", [128, 16, NT], BF16)
    SC = P.sb("SC", [128, 16, 128], F32)
    SCm = P.sb("SCm", [128, 256], F32)
    TV = P.sb("TV", [128, 16, 16], F32)
    TI = P.sb("TI", [128, 16, 16], U32)
    TIf = P.sb("TIf", [128, 16, 16], F32)
    CAND = SC[:].rearrange("p a b -> p (a b)").rearrange("p (h c) -> p h c", h=8)
    SV = P.sb("SV", [128, 8, 16], F32)
    CI = P.sb("CI", [128, 8, 16], U32)
    CIf = P.sb("CIf", [128, 8, 16], F32)
    JS = P.sb("JS", [128, 8, 16], F32)
    IS = P.sb("IS", [128, 8, 16], F32)
    EQ = P.sb("EQ", [128, 8, 16, 16], BF16)
    ASEL = P.sb("ASEL", [128, 8, 16], F32)
    BSEL = P.sb("BSEL", [128, 8, 16], F32)
    GATE = P.sb("GATE", [128, 8, 16], F32)
    ssum = P.sb("ssum", [128, 8], F32)
    ATt = [P.sb("ATt%d" % i, [128, 128], F32) for i in range(2)]
    BTt = [P.sb("BTt%d" % i, [128, 128], F32) for i in range(2)]
    GTt = [P.sb("GTt%d" % i, [128, 128], F32) for i in range(2)]
    iota16 = P.sb("iota16", [128, 16], F32)
    iota3 = P.sb("iota3", [128, 4, 128], BF16)
    thr16 = P.sb("thr16", [128, 16], F32)
    P.pool(lambda e: e.iota(thr16[:], [[16, 16]], base=16, channel_multiplier=0, allow_small_or_imprecise_dtypes=True), w=["thr16"])
    P.pool(lambda e: e.iota(iota16[:], [[1, 16]], base=0, channel_multiplier=0, allow_small_or_imprecise_dtypes=True), w=["iota16"])
    P.pool(lambda e: e.iota(iota3[:], [[0, 4], [1, 128]], base=0, channel_multiplier=0, allow_small_or_imprecise_dtypes=True), w=["iota3"])
    OA = [P.sb("OA%d" % i, [128, 4, 128], BF16) for i in range(2)]
    OB = [P.sb("OB%d" % i, [128, 4, 128], BF16) for i in range(2)]
    GG = P.sb("GG", [128, 128, NT], BF16)
    UTb = [P.sb("UTb%d" % i, [128, 8, 128], BF16) for i in range(NBUF)]
    Vb = [P.sb("Vb%d" % i, [128, D], BF16) for i in range(NBUF)]
    gz = [P.sb("gz%d" % i, [128, NT], F32) for i in range(2)]
    actT = [P.sb("actT%d" % i, [128, NT], BF16) for i in range(2)]

    nrot[0] = 4
    nst_run = nblk_run * TB // NT
    total_g = nb_run * nst_run * 128

    def issue_load(g):
        c = g % 128
        i = g % NBUF
        P.dma("sp", lambda e: e.dma_start(out=UTb[i][:], in_=ut_scr[c]), r=[("UTd", c)], w=["UTb%d" % i], lane="UTb%d" % i)
        P.dma("act", lambda e: e.dma_start(out=Vb[i][:], in_=v_scr[c]), r=[("Vd", c)], w=["Vb%d" % i], lane="Vb%d" % i)

    gctr = 0
    for g in range(min(NBUF, total_g)):
        issue_load(g)

    ybanks = [(banks[4 + k], ("B", 4 + k)) for k in range(4)]
    sts = [(b, st) for b in range(nb_run) for st in range(nst_run)]

    def prep1(b, st, hb):
        h2 = h2T[hb]
        hn = "h2T%d" % hb
        t0 = st * NT
        for j in range(2):
            tl = t0 // TB + j
            P.dma("sp", lambda e, j=j, tl=tl, b=b: e.dma_start(out=xs[:, j, :], in_=x1_d[b, tl * TB:(tl + 1) * TB, :]),
                  r=[("x1d", b, tl)], w=[("xs", j)], lane=("xs", j))
            for hh in range(2):
                P.dve(lambda e, hh=hh, j=j: e.bn_stats(st6[:, hh, :], xs[:, j, hh * 512:(hh + 1) * 512]), r=[("xs", j)], w=["st6"])
            P.dve(lambda e: e.bn_aggr(mv[:], st6[:].rearrange("p a b -> p (a b)")), r=["st6"], w=["mv"])
            rsqrt(rstd[:], mv[:, 1:2], LN_EPS_, ["mv"], "rstd")
            P.dve(lambda e, j=j: e.tensor_scalar(xn2[:], xs[:, j, :], mv[:, 0:1], rstd[:, 0:1], ALU.subtract, ALU.mult),
                  r=[("xs", j), "mv", "rstd"], w=["xn2"])
            yield
            for half in range(2):
                bk, bu = bank()
                for q in range(4):
                    fc = half * 4 + q
                    P.pe(lambda e, bk=bk, q=q, fc=fc: e.transpose(bk[:, q * 128:(q + 1) * 128], xn2[:, fc * 128:(fc + 1) * 128], ident[:, :]),
                         r=["xn2", "ident"], w=[bu])
                for q in range(4):
                    fc = half * 4 + q
                    P.act(lambda e, bk=bk, q=q, fc=fc, b=b, j=j: e.activation(
                        h2[:, fc, j * 128:(j + 1) * 128], bk[:, q * 128:(q + 1) * 128], AF.Identity,
                        bias=MOD[:, 24 + fc, b:b + 1], scale=MOD[:, 32 + fc, b:b + 1]), r=[bu, "MOD"], w=[hn])
                yield
        for m in range(16):
            bk, bu = bank()
            for kc in range(8):
                P.pe(lambda e, bk=bk, kc=kc, m=m: e.matmul(bk[:, 0:NT], wq_bf[:, kc, m * 128:(m + 1) * 128], h2[:, kc, :],
                                                           start=(kc == 0), stop=(kc == 7)), r=["wq_bf", hn], w=[bu])
            if m % 2 == 0:
                P.act(lambda e, bk=bk, m=m: e.activation(qT[:, m, :], bk[:, 0:NT], AF.Copy), r=[bu], w=["qT"])
            else:
                P.dve(lambda e, bk=bk, m=m: e.tensor_copy(qT[:, m, :], bk[:, 0:NT]), r=[bu], w=["qT"])
            yield
        for j in range(2):
            js = slice(j * 128, (j + 1) * 128)
            for grp in range(4):
                bk, bu = bank()
                for q in range(4):
                    hs_ = grp * 4 + q
                    h, s_ = hs_ // 2, hs_ % 2
                    P.pe(lambda e, bk=bk, q=q, hs_=hs_, h=h, s_=s_: e.matmul(bk[:, q * 128:(q + 1) * 128], qT[:, hs_, js], K12[:, s_ * 8 + h, :],
                                                                           start=True, stop=True), r=["qT", "K12"], w=[bu])
                P.act(lambda e, bk=bk, grp=grp: e.activation(SC[:, grp * 4:(grp + 1) * 4, :], bk[:, :].rearrange("p (q n) -> p q n", q=4), AF.Copy),
                      r=[bu], w=["SC"])
                yield
            for hs_ in range(16):
                P.dve(lambda e, hs_=hs_: e.max(TV[:, hs_, 0:8], SC[:, hs_, :]), r=["SC"], w=["TV"])
                P.dve(lambda e, hs_=hs_: e.max_index(TI[:, hs_, 0:8], TV[:, hs_, 0:8], SC[:, hs_, :]), r=["SC", "TV"], w=["TI"])
                P.dve(lambda e, hs_=hs_: e.match_replace(SCm[:, 0:128], TV[:, hs_, 0:8], SC[:, hs_, :], -1e30), r=["SC", "TV"], w=["SCm"])
                yield
                P.dve(lambda e, hs_=hs_: e.max(TV[:, hs_, 8:16], SCm[:, 0:128]), r=["SCm"], w=["TV"])
                P.dve(lambda e, hs_=hs_: e.max_index(TI[:, hs_, 8:16], TV[:, hs_, 8:16], SCm[:, 0:128]), r=["SCm", "TV"], w=["TI"])
                yield
            P.dve(lambda e: e.tensor_copy(TIf[:], TI[:]), r=["TI"], w=["TIf"])
            TV4 = TV[:].rearrange("p (h s) k -> p h s k", s=2)
            TI4 = TIf[:].rearrange("p (h s) k -> p h s k", s=2)
            P.dve(lambda e: e.tensor_tensor(CAND.rearrange("p h (i j) -> p h i j", i=16),
                                            TV4[:, :, 0, :].unsqueeze(3).broadcast_to([128, 8, 16, 16]),
                                            TV4[:, :, 1, :].unsqueeze(2).broadcast_to([128, 8, 16, 16]), ALU.add), r=["TV"], w=["SC"])
            yield
            for h in range(8):
                P.dve(lambda e, h=h: e.max(SV[:, h, 0:8], CAND[:, h, :]), r=["SC"], w=["SV"])
                P.dve(lambda e, h=h: e.max_index(CI[:, h, 0:8], SV[:, h, 0:8], CAND[:, h, :]), r=["SC", "SV"], w=["CI"])
                P.dve(lambda e, h=h: e.match_replace(SCm[:, :], SV[:, h, 0:8], CAND[:, h, :], -1e30), r=["SC", "SV"], w=["SCm"])
                yield
                P.dve(lambda e, h=h: e.max(SV[:, h, 8:16], SCm[:, :]), r=["SCm"], w=["SV"])
                P.dve(lambda e, h=h: e.max_index(CI[:, h, 8:16], SV[:, h, 8:16], SCm[:, :]), r=["SCm", "SV"], w=["CI"])
                yield
            P.dve(lambda e: e.tensor_tensor(GATE[:], SV[:], SV[:, :, 0:1].broadcast_to([128, 8, 16]), ALU.subtract), r=["SV"], w=["GATE"])
            P.act(lambda e: e.activation(GATE[:], GATE[:], AF.Exp), r=["GATE"], w=["GATE"])
            P.dve(lambda e: e.tensor_reduce(ssum[:], GATE[:], AX.X, ALU.add), r=["GATE"], w=["ssum"])
            P.dve(lambda e: e.reciprocal(ssum[:], ssum[:]), r=["ssum"], w=["ssum"])
            P.dve(lambda e: e.tensor_tensor(GATE[:], GATE[:], ssum[:].unsqueeze(2).broadcast_to([128, 8, 16]), ALU.mult), r=["GATE", "ssum"], w=["GATE"])
            yield
            P.dve(lambda e: e.tensor_copy(CIf[:], CI[:]), r=["CI"], w=["CIf"])
            P.dve(lambda e: e.tensor_tensor(EQ[:], CIf[:].unsqueeze(3).broadcast_to([128, 8, 16, 16]),
                                            thr16[:, :].unsqueeze(1).unsqueeze(1).broadcast_to([128, 8, 16, 16]), ALU.is_ge), r=["CIf", "thr16"], w=["EQ"])
            P.dve(lambda e: e.tensor_reduce(IS[:], EQ[:], AX.X, ALU.add), r=["EQ"], w=["IS"])
            P.dve(lambda e: e.scalar_tensor_tensor(JS[:], IS[:], -16.0, CIf[:], ALU.mult, ALU.add), r=["IS", "CIf"], w=["JS"])
            yield
            io4 = iota16[:, :].unsqueeze(1).unsqueeze(1).broadcast_to([128, 8, 16, 16])
            for (selT, sn, s_, dstT, dn) in ((IS, "IS", 0, ASEL, "ASEL"), (JS, "JS", 1, BSEL, "BSEL")):
                P.dve(lambda e, selT=selT: e.tensor_tensor(EQ[:], selT[:].unsqueeze(3).broadcast_to([128, 8, 16, 16]), io4, ALU.is_equal),
                      r=[sn, "iota16"], w=["EQ"])
                P.pool(lambda e, s_=s_: e.tensor_tensor(EQ[:], EQ[:], TI4[:, :, s_, :].unsqueeze(2).broadcast_to([128, 8, 16, 16]), ALU.mult),
                       r=["EQ", "TIf"], w=["EQ"])
                P.dve(lambda e, dstT=dstT: e.tensor_reduce(dstT[:], EQ[:], AX.X, ALU.add), r=["EQ"], w=[dn])
                yield
            for (srcT, sn, dstT, dn) in ((ASEL, "ASEL", ATt[j], "ATt%d" % j), (BSEL, "BSEL", BTt[j], "BTt%d" % j), (GATE, "GATE", GTt[j], "GTt%d" % j)):
                bk, bu = bank()
                P.pe(lambda e, bk=bk, srcT=srcT: e.transpose(bk[:, 0:128], srcT[:].rearrange("p h k -> p (h k)"), ident[:, :]),
                     r=[sn, "ident"], w=[bu])
                P.act(lambda e, bk=bk, dstT=dstT: e.activation(dstT[:], bk[:, 0:128], AF.Copy), r=[bu], w=[dn])
            yield

    def ggbuild():
        for j in range(2):
            for tg in range(0 if "G" in SKIP else 32):
                i = tg % 2
                ts_ = slice(tg * 4, tg * 4 + 4)
                for q in range(4):
                    tt_ = tg * 4 + q
                    P.dve(lambda e, q=q, tt_=tt_, i=i, j=j: e.tensor_scalar(OA[i][:, q, :], iota3[:, 0, :], ATt[j][:, tt_:tt_ + 1], GTt[j][:, tt_:tt_ + 1],
                                                                          ALU.is_equal, ALU.mult),
                          r=["iota3", "ATt%d" % j, "GTt%d" % j], w=[("OA", i, q)])
                P.dve(lambda e, ts_=ts_, i=i, j=j: e.tensor_tensor(OB[i][:], iota3[:], BTt[j][:, ts_].unsqueeze(2).broadcast_to([128, 4, 128]), ALU.is_equal),
                      r=["iota3", "BTt%d" % j], w=["OB%d" % i])
                bk, bu = bank()
                for q in range(4):
                    P.pe(lambda e, bk=bk, q=q, i=i: e.matmul(bk[:, q * 128:(q + 1) * 128], OB[i][:, q, :], OA[i][:, q, :],
                                                            start=True, stop=True), r=[("OA", i, q), "OB%d" % i], w=[bu])
                tb_ = j * 128 + tg * 4
                P.act(lambda e, bk=bk, tb_=tb_: e.activation(GG[:, :, tb_:tb_ + 4].rearrange("p i t -> p t i"),
                                                             bk[:, :].rearrange("p (t i) -> p t i", t=4), AF.Copy), r=[bu], w=["GG"])

    def zstage(c, hb):
        nonlocal gctr
        g = gctr
        gctr += 1
        i = g % NBUF
        pz = c % 2
        bk, bu = bank()
        for dc in range(8):
            P.pe(lambda e, bk=bk, dc=dc, i=i: e.matmul(bk[:, 0:NT], UTb[i][:, dc, :], h2T[hb][:, dc, :], start=(dc == 0), stop=(dc == 7)),
                 r=["UTb%d" % i, "h2T%d" % hb], w=[bu])
        P.act(lambda e, bk=bk, pz=pz: e.activation(gz[pz][:], bk[:, 0:NT], AF.Gelu), r=[bu], w=["gz%d" % pz])
        eng = P.dve if c % 2 == 0 else P.pool
        eng(lambda e, pz=pz, c=c: e.tensor_tensor(actT[pz][:], gz[pz][:], GG[:, c, :], ALU.mult), r=["gz%d" % pz, "GG"], w=["actT%d" % pz])
        return i, pz, g

    def ystage(c, i, pz, g):
        for j in range(2):
            for half in range(2):
                yb, yu = ybanks[j * 2 + half]
                P.pe(lambda e, yb=yb, j=j, half=half, pz=pz, i=i, c=c: e.matmul(
                    yb[:, :], actT[pz][:, j * 128:(j + 1) * 128], Vb[i][:, half * 512:(half + 1) * 512],
                    start=(c == 0), stop=(c == 127)), r=["actT%d" % pz, "Vb%d" % i], w=[yu])
        if g + NBUF < total_g:
            issue_load(g + NBUF)

    def final(b, st):
        t0 = st * NT
        if st == 0:
            for half in range(2):
                bk, bu = bank()
                P.pe(lambda e, bk=bk, b=b, half=half: e.matmul(bk[:, :], SEL[:, b, :], GROW[:, 0, half * 512:(half + 1) * 512],
                                                               start=True, stop=True), r=["SEL", "GROW"], w=[bu])
                P.act(lambda e, bk=bk, half=half: e.activation(G2[:, half * 512:(half + 1) * 512], bk[:, :], AF.Copy), r=[bu], w=["G2"])
        xre = big[:, 0:D]
        for j in range(2):
            tl = t0 // TB + j
            P.dma("sp", lambda e, tl=tl, b=b: e.dma_start(out=xre, in_=x1_d[b, tl * TB:(tl + 1) * TB, :]),
                  r=[("x1d", b, tl)], w=["big"], lane="big")
            for half in range(2):
                yb, yu = ybanks[j * 2 + half]
                hs = slice(half * 512, (half + 1) * 512)
                P.dve(lambda e, yb=yb, hs=hs: e.tensor_tensor(xn2[:, hs], yb[:, :], G2[:, hs], ALU.mult), r=[yu, "G2"], w=["xn2"])
            P.dve(lambda e: e.scalar_tensor_tensor(xn2[:], xre, ALPHA_, xn2[:], ALU.mult, ALU.add), r=["big", "xn2"], w=["xn2"])
            for hh in range(2):
                P.dve(lambda e, hh=hh: e.bn_stats(st6[:, hh, :], xn2[:, hh * 512:(hh + 1) * 512]), r=["xn2"], w=["st6"])
            P.dve(lambda e: e.bn_aggr(mv[:], st6[:].rearrange("p a b -> p (a b)")), r=["st6"], w=["mv"])
            rsqrt(rstd[:], mv[:, 1:2], LN_EPS_, ["mv"], "rstd")
            P.dve(lambda e: e.tensor_scalar(xn2[:], xn2[:], mv[:, 0:1], rstd[:, 0:1], ALU.subtract, ALU.mult), r=["xn2", "mv", "rstd"], w=["xn2"])
            P.dve(lambda e: e.tensor_tensor(xn2[:], xn2[:], LG1[:], ALU.mult), r=["xn2", "LG1"], w=["xn2"])
            P.dve(lambda e: e.tensor_tensor(xn2[:], xn2[:], LB1[:], ALU.add), r=["xn2", "LB1"], w=["xn2"])
            o = P.dma("sp", lambda e, b=b, tl=tl: e.dma_start(out=out_d[b, tl * TB:(tl + 1) * TB, :], in_=xn2[:]),
                      r=["xn2"], w=[("x1d", b, tl)], lane="xn2_out")
            out_ops.append(o)

    for _ in prep1(sts[0][0], sts[0][1], 0):
        pass
    nch = 0 if "E" in SKIP else 128
    for k, (b, st) in enumerate(sts):
        hb = k % 2
        ggbuild()
        gen = prep1(sts[k + 1][0], sts[k + 1][1], 1 - hb) if k + 1 < len(sts) else iter(())
        pend = None
        for c in range(nch):
            cur = zstage(c, hb)
            if pend is not None:
                ystage(c - 1, *pend)
            pend = cur
            if c >= 2:
                next(gen, None)
                next(gen, None)
        if pend is not None:
            ystage(nch - 1, *pend)
        for _ in gen:
            pass
        final(b, st)
    P.finish(final_ops=out_ops[-1:])
    return nc, P


_NAMES = ["x", "c", "cond_w", "cond_b", "w_in", "mu_shift", "conv_w", "conv_b", "conv_ln_g", "conv_ln_b",
          "rw_w0", "rw_w2", "rw_a0", "rw_a2", "rw_g2", "rw_kk", "rw_ka", "rw_rk", "rw_lnx_g", "rw_lnx_b",
          "w_out", "ln1_g", "ln1_b", "peer_wq", "peer_k1", "peer_k2", "peer_u", "peer_v", "ln2_g", "ln2_b"]


def kernel(**inputs):
    from concourse.bass_utils import run_bass_kernel_spmd
    nc, P = build()
    in_maps = []
    for i in range(8):
        m = {}
        for k in _NAMES:
            v = np.ascontiguousarray(np.asarray(inputs[k], dtype=np.float32))
            if k in ("x", "c"):
                v = np.ascontiguousarray(v[i * NB:(i + 1) * NB])
            m[k] = v
        in_maps.append(m)
    res = run_bass_kernel_spmd(nc, in_maps, core_ids=list(range(8)))
    return np.concatenate([np.asarray(r["out"]) for r in res.results], axis=0).astype(np.float32)
```

```python
import contextlib
import numpy as np
import concourse.bass as bass
import concourse.mybir as mybir

F32 = mybir.dt.float32
BF16 = mybir.dt.bfloat16
U32 = mybir.dt.uint32
I32 = mybir.dt.int32
AF = mybir.ActivationFunctionType
ALU = mybir.AluOpType
AX = mybir.AxisListType


class Op:
    __slots__ = ("eng", "dma", "lane", "lane_val", "sig", "sigval")

    def __init__(self, eng, dma):
        self.eng = eng
        self.dma = dma
        self.lane = None
        self.lane_val = 0
        self.sig = False
        self.sigval = 0


class Prog:
    ENGS = ("pe", "act", "dve", "pool", "sp")

    def __init__(self, nc, flags=None):
        self.nc = nc
        self.dry = flags is None
        self.flags = flags
        self.stack = [contextlib.ExitStack()]
        self.lastw = {}
        self.readers = {}
        self.lanes = {}
        self.alias = {}
        self.allops = []
        self.cnt = {e: 0 for e in self.ENGS}
        self.waited = {e: {} for e in self.ENGS}
        self.engobj = {"pe": nc.tensor, "act": nc.scalar, "dve": nc.vector, "pool": nc.gpsimd, "sp": nc.sync}
        if not self.dry:
            self.K = 12
            self.esem = {e: [self.sem("E%s%d" % (e, i)) for i in range(self.K)] for e in self.ENGS if e != "sp"}

    def push(self):
        self.stack.append(contextlib.ExitStack())

    def pop(self):
        self.stack.pop().close()

    def sb(self, name, shape, dt):
        return self.stack[-1].enter_context(self.nc.sbuf_tensor(name, list(shape), dt))

    def ps(self, name, shape, dt=F32):
        return self.stack[-1].enter_context(self.nc.psum_tensor(name, list(shape), dt))

    def sem(self, name):
        return self.stack[0].enter_context(self.nc.semaphore(name))

    def op(self, eng, fn, r=(), w=(), dma=False, lane=None):
        o = Op(eng, dma)
        al = self.alias
        r = [al.get(u, u) for u in r]
        w = [al.get(u, u) for u in w]
        deps = set()
        for u in r:
            lw = self.lastw.get(u)
            if lw is not None:
                deps.add(lw)
        for u in w:
            lw = self.lastw.get(u)
            if lw is not None:
                deps.add(lw)
            for rd in self.readers.get(u, {}).values():
                deps.add(rd)
        for u in w:
            self.lastw[u] = o
            self.readers[u] = {}
        for u in r:
            if u not in w:
                self.readers.setdefault(u, {})[(eng, len(self.allops)) if dma else eng] = o
        idx = len(self.allops)
        self.allops.append(o)
        if self.dry:
            for d in deps:
                if not (d.eng == eng and eng == "pe" and not d.dma):
                    d.sig = True
            if dma:
                self.lanes.setdefault(lane, 0)
            return o
        e = self.engobj[eng]
        wd = self.waited[eng]
        for d in deps:
            if d.dma:
                s, v = d.lane, d.lane_val
            else:
                if d.eng == eng and eng == "pe":
                    continue
                s, v = d.sigval
            k = id(s)
            if wd.get(k, 0) >= v:
                continue
            wd[k] = v
            e.wait_ge(s, v)
        ins = fn(e)
        if dma:
            if lane not in self.lanes:
                self.lanes[lane] = [self.sem("L%d" % len(self.lanes)), 0]
            L = self.lanes[lane]
            L[1] += 16
            o.lane, o.lane_val = L[0], L[1]
            ins.then_inc(L[0], 16)
        elif self.flags[idx]:
            n = self.cnt[eng]
            self.cnt[eng] += 1
            sm = self.esem[eng][n % self.K]
            o.sigval = (sm, n // self.K + 1)
            ins.then_inc(sm, 1)
        return o

    def pe(self, fn, r=(), w=()):
        return self.op("pe", fn, r, w)

    def act(self, fn, r=(), w=()):
        return self.op("act", fn, r, w)

    def dve(self, fn, r=(), w=()):
        return self.op("dve", fn, r, w)

    def pool(self, fn, r=(), w=()):
        return self.op("pool", fn, r, w)

    def dma(self, q, fn, r=(), w=(), lane=None):
        return self.op(q, fn, r, w, dma=True, lane=lane)

    def finish(self, final_ops=()):
        if not self.dry:
            for o in final_ops:
                self.nc.sync.wait_ge(o.lane, o.lane_val)
        while self.stack:
            self.stack.pop().close()

D = 1024
SEQ = 2048
NB = 4
TB = 128
NBLK = SEQ // TB
CH = 64
ALPHA_ = (2.0) ** 0.25
LN_EPS_ = 1e-5
GN_EPS_ = 64e-5
LWC = 0.6065306597126334


def build(nb_run=NB, nblk_run=NBLK, stage="AB"):
    nc1 = bass.Bass("TRN2", target_bir_lowering=False)
    P1 = Prog(nc1)
    body(nc1, P1, nb_run, nblk_run, stage)
    flags = [o.sig for o in P1.allops]
    nc = bass.Bass("TRN2", target_bir_lowering=False)
    P = Prog(nc, flags)
    body(nc, P, nb_run, nblk_run, stage)
    return nc, P


def body(nc, P, nb_run, nblk_run, stage):
    SKIP = ""
    MODE = "S"
    dt = nc.dram_tensor
    x_d = dt("x", [NB, SEQ, D], F32, kind="ExternalInput").ap()
    c_d = dt("c", [NB, D], F32, kind="ExternalInput").ap()
    cond_w_d = dt("cond_w", [D, 6 * D], F32, kind="ExternalInput").ap()
    cond_b_d = dt("cond_b", [6 * D], F32, kind="ExternalInput").ap()
    w_in_d = dt("w_in", [D, 2816], F32, kind="ExternalInput").ap()
    mu_d = dt("mu_shift", [1792], F32, kind="ExternalInput").ap()
    conv_w_d = dt("conv_w", [31, 512], F32, kind="ExternalInput").ap()
    conv_b_d = dt("conv_b", [512], F32, kind="ExternalInput").ap()
    cg_d = dt("conv_ln_g", [512], F32, kind="ExternalInput").ap()
    cb_d = dt("conv_ln_b", [512], F32, kind="ExternalInput").ap()
    w0_d = dt("rw_w0", [512], F32, kind="ExternalInput").ap()
    w2_d = dt("rw_w2", [64, 512], F32, kind="ExternalInput").ap()
    a0_d = dt("rw_a0", [512], F32, kind="ExternalInput").ap()
    a2_d = dt("rw_a2", [64, 512], F32, kind="ExternalInput").ap()
    g2_d = dt("rw_g2", [128, 512], F32, kind="ExternalInput").ap()
    kk_d = dt("rw_kk", [512], F32, kind="ExternalInput").ap()
    ka_d = dt("rw_ka", [512], F32, kind="ExternalInput").ap()
    rk_d = dt("rw_rk", [8, 64], F32, kind="ExternalInput").ap()
    lg_d = dt("rw_lnx_g", [512], F32, kind="ExternalInput").ap()
    lb_d = dt("rw_lnx_b", [512], F32, kind="ExternalInput").ap()
    w_out_d = dt("w_out", [D, D], F32, kind="ExternalInput").ap()
    ln1g_d = dt("ln1_g", [D], F32, kind="ExternalInput").ap()
    ln1b_d = dt("ln1_b", [D], F32, kind="ExternalInput").ap()
    wq_d = dt("peer_wq", [D, 2048], F32, kind="ExternalInput").ap()
    k1_d = dt("peer_k1", [8, 128, 128], F32, kind="ExternalInput").ap()
    k2_d = dt("peer_k2", [8, 128, 128], F32, kind="ExternalInput").ap()
    pu_d = dt("peer_u", [16384, D], F32, kind="ExternalInput").ap()
    pv_d = dt("peer_v", [16384, D], F32, kind="ExternalInput").ap()
    ln2g_d = dt("ln2_g", [D], F32, kind="ExternalInput").ap()
    ln2b_d = dt("ln2_b", [D], F32, kind="ExternalInput").ap()
    out_d = dt("out", [NB, SEQ, D], F32, kind="ExternalOutput").ap()

    banks = [P.ps("bank%d" % i, [128, 512], F32) for i in range(8)]
    bctr = [0]

    nrot = [8]

    def bank():
        i = bctr[0] % nrot[0]
        bctr[0] += 1
        return banks[i], ("B", i)

    def rsqrt(dst, src, eps, r, w, floor=None):
        P.act(lambda e: e.activation(dst, src, AF.Sqrt, bias=epsb[0:dst.shape[0], ekey[eps]:ekey[eps] + 1]), r=list(r) + ["epsb"], w=[w])
        if floor is not None:
            P.dve(lambda e: e.tensor_scalar_max(dst, dst, floor), r=[w], w=[w])
        P.dve(lambda e: e.reciprocal(dst, dst), r=[w], w=[w])

    epsb = P.sb("epsb", [128, 4], F32)
    ekey = {LN_EPS_: 0, GN_EPS_: 1, 0.0: 2}
    P.pool(lambda e: e.memset(epsb[:, 0:1], LN_EPS_), w=["epsb"])
    P.pool(lambda e: e.memset(epsb[:, 1:2], GN_EPS_), w=["epsb"])
    P.pool(lambda e: e.memset(epsb[:, 2:3], 0.0), w=["epsb"])
    ident = P.sb("ident", [128, 128], F32)
    identb = P.sb("identb", [128, 128], BF16)
    iota_p = P.sb("iota_p", [128, 1], F32)
    iota_f = P.sb("iota_f", [128, 128], F32)
    P.pool(lambda e: e.iota(iota_p[:], [[0, 1]], base=0, channel_multiplier=1,
                            allow_small_or_imprecise_dtypes=True), w=["iota_p"])
    P.pool(lambda e: e.iota(iota_f[:], [[1, 128]], base=0, channel_multiplier=0,
                            allow_small_or_imprecise_dtypes=True), w=["iota_f"])
    P.dve(lambda e: e.tensor_scalar(ident[:], iota_f[:], iota_p[:, 0:1], None, ALU.is_equal),
          r=["iota_p", "iota_f"], w=["ident"])
    P.dve(lambda e: e.tensor_copy(identb[:], ident[:]), r=["ident"], w=["identb"])
    m_st = P.sb("m_st", [64, 64], F32)
    m_in = P.sb("m_in", [64, 64], F32)
    m_lo = P.sb("m_lo", [64, 64], F32)
    P.dve(lambda e: e.tensor_scalar(m_st[:], iota_f[0:64, 0:64], iota_p[0:64, 0:1], None, ALU.is_gt),
          r=["iota_p", "iota_f"], w=["m_st"])
    P.dve(lambda e: e.tensor_scalar(m_in[:], iota_f[0:64, 0:64], iota_p[0:64, 0:1], None, ALU.is_ge),
          r=["iota_p", "iota_f"], w=["m_in"])
    P.dve(lambda e: e.tensor_scalar(m_lo[:], iota_f[0:64, 0:64], iota_p[0:64, 0:1], None, ALU.is_lt),
          r=["iota_p", "iota_f"], w=["m_lo"])
    ones_c = P.sb("ones_c", [128, 128], F32)
    ones_h = P.sb("ones_h", [64, 64], F32)
    ones_g = P.sb("ones_g", [64, 64], F32)
    P.pool(lambda e: e.memset(ones_c[:], 1.0 / 512.0), w=["ones_c"])
    P.pool(lambda e: e.memset(ones_h[:], 1.0), w=["ones_h"])
    P.pool(lambda e: e.memset(ones_g[:], 1.0 / 64.0), w=["ones_g"])
    rmask = P.sb("rmask", [64, 2, 64], F32)
    P.pool(lambda e: e.memset(rmask[:], 1.0), w=["rmask"])
    P.pool(lambda e: e.memset(rmask[:, :, 0:1], 0.0), w=["rmask"])

    big = P.sb("big", [128, 2048], F32)
    stA = P.sb("stA", [64, 128], F32)
    stB = P.sb("stB", [88, 64], F32)
    PA = P.sb("PA", [128, 64], F32)
    PB = P.sb("PB", [64, 88], F32)
    P.pool(lambda e: e.memset(stA[:], 0.0), w=["stA"])
    P.pool(lambda e: e.memset(stB[:], 0.0), w=["stB"])
    rowsA = [(cond_b_d, 0, 48), (conv_b_d, 48, 4), (cg_d, 52, 4), (cb_d, 56, 4)]
    for (src, r0, n) in rowsA:
        P.dma("sp", (lambda e, src=src, r0=r0, n=n: e.dma_start(
            out=stA[r0:r0 + n, :], in_=src.rearrange("(c p) -> c p", p=128))), w=["stA"], lane="stA")
    P.dma("sp", lambda e: e.dma_start(out=stA[60:61, :], in_=mu_d[1664:1792].rearrange("(c p) -> c p", p=128)),
          w=["stA"], lane="stA")
    rowsB = [(mu_d[0:1536], 0, 24), (w0_d, 24, 8), (a0_d, 32, 8), (kk_d, 40, 8), (ka_d, 48, 8),
             (lg_d, 56, 8), (lb_d, 64, 8), (mu_d[1536:1664], 80, 2)]
    for (src, r0, n) in rowsB:
        P.dma("sp", (lambda e, src=src, r0=r0, n=n: e.dma_start(
            out=stB[r0:r0 + n, :], in_=src.rearrange("(c p) -> c p", p=64))), w=["stB"], lane="stB")
    P.dma("sp", lambda e: e.dma_start(out=stB[72:80, :], in_=rk_d), w=["stB"], lane="stB")
    bk, bu = bank()
    P.pe(lambda e: e.transpose(bk[:, 0:64], stA[:, :], ident[0:64, 0:64]), r=["stA", "ident"], w=[bu])
    P.dve(lambda e: e.tensor_copy(PA[:], bk[:, 0:64]), r=[bu], w=["PA"])
    bk2, bu2 = bank()
    P.pe(lambda e: e.transpose(bk2[0:64, 0:88], stB[:, :], ident[0:88, 0:88]), r=["stB", "ident"], w=[bu2])
    P.dve(lambda e: e.tensor_copy(PB[:], bk2[0:64, 0:88]), r=[bu2], w=["PB"])
    OMB = P.sb("OMB", [64, 88], F32)
    P.dve(lambda e: e.tensor_scalar(OMB[:], PB[:], -1.0, 1.0, ALU.mult, ALU.add), r=["PB"], w=["OMB"])
    OMA = P.sb("OMA", [128, 64], F32)
    P.dve(lambda e: e.tensor_scalar(OMA[:], PA[:], -1.0, 1.0, ALU.mult, ALU.add), r=["PA"], w=["OMA"])
    CW = P.sb("CW", [128, 4, 31], F32)
    P.dma("sp", lambda e: e.dma_start(out=big[0:31, 0:512], in_=conv_w_d), w=["big"], lane="big")
    for c in range(4):
        bk, bu = bank()
        P.pe(lambda e, bk=bk, c=c: e.transpose(bk[:, 0:31], big[0:31, c * 128:(c + 1) * 128], ident[0:31, 0:31]),
             r=["big", "ident"], w=[bu])
        P.dve(lambda e, bk=bk, c=c: e.tensor_copy(CW[:, c, :], bk[:, 0:31]), r=[bu], w=["CW"])

    siluT = P.sb("siluT", [128, 8, 4], F32)
    MOD = P.sb("MOD", [128, 48, 4], F32)
    P.dma("sp", lambda e: e.dma_start(out=big[0:4, 0:D], in_=c_d), w=["big"], lane="big")
    bk, bu = bank()
    for kc in range(8):
        P.pe(lambda e, bk=bk, kc=kc: e.transpose(bk[:, kc * 4:(kc + 1) * 4], big[0:4, kc * 128:(kc + 1) * 128],
                                                 ident[0:4, 0:4]), r=["big", "ident"], w=[bu])
    P.act(lambda e, bk=bk: e.activation(siluT[:].rearrange("p a b -> p (a b)"), bk[:, 0:32], AF.Silu),
          r=[bu], w=["siluT"])
    bkm, bum = bank()
    for kc in range(8):
        for q in range(3):
            P.dma("sp", (lambda e, kc=kc, q=q: e.dma_start(
                out=big[:, :], in_=cond_w_d[kc * 128:(kc + 1) * 128, q * 2048:(q + 1) * 2048])), w=["big"], lane="big")
            for mm_ in range(16):
                m = q * 16 + mm_
                P.pe(lambda e, kc=kc, m=m, mm_=mm_: e.matmul(bkm[:, m * 4:(m + 1) * 4], big[:, mm_ * 128:(mm_ + 1) * 128],
                                                    siluT[:, kc, :], start=(kc == 0 and m == 0),
                                                    stop=(kc == 7 and m == 47), skip_group_check=True),
                     r=["big", "siluT"], w=[bum])
    P.dve(lambda e: e.tensor_tensor(MOD[:], bkm[:, 0:192].rearrange("p (m b) -> p m b", b=4),
                                    PA[:, 0:48].unsqueeze(2).broadcast_to([128, 48, 4]), ALU.add),
          r=[bum, "PA"], w=["MOD"])
    for lo in (8, 32):
        P.dve(lambda e, lo=lo: e.tensor_scalar_add(MOD[:, lo:lo + 8, :], MOD[:, lo:lo + 8, :], 1.0),
              r=["MOD"], w=["MOD"])
    GROW = P.sb("GROW", [4, 1, D], F32)
    for gi, lo in enumerate((16,)):
        for half in range(2):
            bk, bu = bank()
            for c in range(4):
                P.pe(lambda e, bk=bk, c=c, lo=lo, half=half: e.transpose(
                    bk[0:4, c * 128:(c + 1) * 128], MOD[:, lo + half * 4 + c, :], ident[:, :]),
                    r=["MOD", "ident"], w=[bu])
            P.dve(lambda e, bk=bk, gi=gi, half=half: e.tensor_copy(GROW[:, gi, half * 512:(half + 1) * 512],
                                                                   bk[0:4, :]), r=[bu], w=["GROW"])
    SEL = P.sb("SEL", [4, 4, 128], F32)
    P.dve(lambda e: e.tensor_copy(SEL[:], ident[0:4, 0:4].unsqueeze(2).broadcast_to([4, 4, 128])),
          r=["ident"], w=["SEL"])

    LG1 = P.sb("LG1", [128, D], F32)
    LB1 = P.sb("LB1", [128, D], F32)
    P.dma("sp", lambda e: e.dma_start(out=LG1[:], in_=ln1g_d.partition_broadcast(128)), w=["LG1"], lane="LG1")
    P.dma("sp", lambda e: e.dma_start(out=LB1[:], in_=ln1b_d.partition_broadcast(128)), w=["LB1"], lane="LB1")
    st6 = P.sb("st6", [128, 2, 6], F32)
    mv = P.sb("mv", [128, 2], F32)
    rstd = P.sb("rstd", [128, 1], F32)
    P.push()
    w_in_bf = P.sb("w_in_bf", [128, 8, 2816], BF16)
    for kc in range(8):
        for q in range(2):
            P.dma("sp", lambda e, kc=kc, q=q: e.dma_start(out=big[:, 0:1408], in_=w_in_d[kc * 128:(kc + 1) * 128, q * 1408:(q + 1) * 1408]),
                  w=["big"], lane="big")
            P.act(lambda e, kc=kc, q=q: e.activation(w_in_bf[:, kc, q * 1408:(q + 1) * 1408], big[:, 0:1408], AF.Copy), r=["big"], w=["w_in_bf"])
    w_oc_bf = P.sb("w_oc_bf", [128, 4, D], BF16)
    w_or_bf = P.sb("w_or_bf", [64, 8, D], BF16)
    for c in range(4):
        P.dma("sp", lambda e, c=c: e.dma_start(out=big[:, 0:D], in_=w_out_d[c * 128:(c + 1) * 128, :]),
              w=["big"], lane="big")
        P.dve(lambda e, c=c: e.tensor_copy(w_oc_bf[:, c, :], big[:, 0:D]), r=["big"], w=["w_oc_bf"])
    for h in range(8):
        P.dma("sp", lambda e, h=h: e.dma_start(out=big[0:64, 0:D], in_=w_out_d[512 + h * 64:512 + (h + 1) * 64, :]),
              w=["big"], lane="big")
        P.dve(lambda e, h=h: e.tensor_copy(w_or_bf[:, h, :], big[0:64, 0:D]), r=["big"], w=["w_or_bf"])
    w2_bf = P.sb("w2_bf", [64, 512], BF16)
    a2_bf = P.sb("a2_bf", [64, 512], BF16)
    g2_bf = P.sb("g2_bf", [128, 512], BF16)
    for (src, dst, np_, nm) in ((w2_d, w2_bf, 64, "w2_bf"), (a2_d, a2_bf, 64, "a2_bf"), (g2_d, g2_bf, 128, "g2_bf")):
        P.dma("sp", lambda e, src=src, np_=np_: e.dma_start(out=big[0:np_, 0:512], in_=src), w=["big"], lane="big")
        P.dve(lambda e, dst=dst, np_=np_: e.tensor_copy(dst[:], big[0:np_, 0:512]), r=["big"], w=[nm])

    x1_d = out_d

    xb = P.sb("xb", [128, D], F32)
    xn = P.sb("xn", [128, D], F32)
    hT = [P.sb("hT%d" % i, [128, 8, TB + 1], BF16) for i in range(2)]
    ub = [P.sb("ub%d" % i, [128, 4, 30 + TB], F32) for i in range(2)]
    sig = P.sb("sig", [128, TB], F32)
    acc = P.sb("acc", [128, 4, TB], F32)
    sq = P.sb("sq", [128, 4, TB], F32)
    cm = P.sb("cm", [128, TB], F32)
    cr = P.sb("cr", [128, TB], F32)
    ct = P.sb("ct", [128, TB], F32)
    ycT = P.sb("ycT", [128, 4, TB], BF16)
    tmpA = P.sb("tmpA", [128, TB], F32)
    F = [P.sb("F%d" % i, [64, 8, TB], F32) for i in range(11)]

    def role(name, idx):
        P.alias[name] = "F%d" % idx
        return F[idx]
    VBb = P.sb("VBb", [64, 8, TB], BF16)
    TW = P.sb("TW", [64, TB], BF16)
    ADb = P.sb("ADb", [64, TB], BF16)
    SGD = P.sb("SGD", [128, TB], BF16)
    RT = P.sb("RT", [64, 8, TB], BF16)
    KT = P.sb("KT", [64, 8, TB], BF16)
    BT = P.sb("BT", [64, 8, TB], BF16)
    ATl = P.sb("ATl", [64, 8, TB], BF16)
    YRW = P.sb("YRW", [64, 8, TB], BF16)
    Vt = P.sb("Vt", [64, 8, 64], BF16)
    Ktm = P.sb("Ktm", [64, 8, 64], BF16)
    Btm = P.sb("Btm", [64, 8, 64], BF16)
    AabT = P.sb("AabT", [64, 8, 64], BF16)
    ArbT = P.sb("ArbT", [64, 8, 64], BF16)
    AakT = P.sb("AakT", [64, 8, 64], BF16)
    ArkT = P.sb("ArkT", [64, 8, 64], BF16)
    Npw = [P.sb("Npw%d" % i, [64, 8, 64], BF16) for i in range(2)]
    NTpw = [P.sb("NTpw%d" % i, [64, 8, 64], BF16) for i in range(2)]
    PT = [P.sb("PT%d" % i, [64, 8, 64], BF16) for i in range(2)]
    Xb = P.sb("Xb", [64, 8, 64], BF16)
    Ub = P.sb("Ub", [64, 8, 64], BF16)
    S0 = P.sb("S0", [64, 8, 64], F32)
    S0b = P.sb("S0b", [64, 8, 64], BF16)
    Stmp = P.sb("Stmp", [64, 8, 64], F32)
    G1 = P.sb("G1", [128, D], F32)
    identb8 = P.sb("identb8", [64, 8, 64], BF16)
    P.dve(lambda e: e.tensor_copy(identb8[:], ident[0:64, 0:64].unsqueeze(1).broadcast_to([64, 8, 64])),
          r=["ident"], w=["identb8"])

    def pbc(col):
        return PB[:, col:col + 8].unsqueeze(2).broadcast_to([64, 8, TB])

    def ombc(col):
        return OMB[:, col:col + 8].unsqueeze(2).broadcast_to([64, 8, TB])

    def m8(m):
        return m[:, :].unsqueeze(1).broadcast_to([64, 8, 64])

    out_ops = []
    xbufs = [(xb, "xb"), (big[:, 0:D], "bigA")]
    xo = big[:, D:2 * D]

    def head(b, blk, first):
        xbuf, xbn = xbufs[blk % 2]
        xw = [xbn, "big"] if first else [xbn]
        par = blk % 2
        hTc, hTn = hT[par], hT[1 - par]
        hu, hun = "hT%d" % par, "hT%d" % (1 - par)
        ubc, ubn = ub[par], ub[1 - par]
        uu, uun = "ub%d" % par, "ub%d" % (1 - par)
        if blk == 0:
            P.pool(lambda e: e.memset(hTc[:, :, 0:1], 0.0), w=[hu])
            P.pool(lambda e: e.memset(ubc[:, :, 0:30], 0.0), w=[uu])
        t0 = blk * TB
        P.dma("sp", lambda e, b=b, t0=t0: e.dma_start(out=xbuf[:], in_=x_d[b, t0:t0 + TB, :]), w=xw, lane=xbn)
        for hh in range(2):
            P.dve(lambda e, hh=hh: e.bn_stats(st6[:, hh, :], xbuf[:, hh * 512:(hh + 1) * 512]), r=[xbn], w=["st6"])
        P.dve(lambda e: e.bn_aggr(mv[:], st6[:].rearrange("p a b -> p (a b)")), r=["st6"], w=["mv"])
        rsqrt(rstd[:], mv[:, 1:2], LN_EPS_, ["mv"], "rstd")
        P.dve(lambda e: e.tensor_scalar(xn[:], xbuf[:], mv[:, 0:1], rstd[:, 0:1], ALU.subtract, ALU.mult),
              r=[xbn, "mv", "rstd"], w=["xn"])
        for half in range(2):
            bk, bu = bank()
            for j in range(4):
                fc = half * 4 + j
                P.pe(lambda e, bk=bk, j=j, fc=fc: e.transpose(bk[:, j * 128:(j + 1) * 128],
                                                              xn[:, fc * 128:(fc + 1) * 128], ident[:, :]),
                     r=["xn", "ident"], w=[bu])
            for j in range(4):
                fc = half * 4 + j
                P.act(lambda e, bk=bk, j=j, fc=fc, b=b, hTc=hTc: e.activation(
                    hTc[:, fc, 1:TB + 1], bk[:, j * 128:(j + 1) * 128], AF.Identity,
                    bias=MOD[:, fc, b:b + 1], scale=MOD[:, 8 + fc, b:b + 1]), r=[bu, "MOD"], w=[hu])
        P.pool(lambda e, hTc=hTc, hTn=hTn: e.tensor_copy(hTn[:, :, 0:1], hTc[:, :, TB:TB + 1]), r=[hu], w=[hun])


    blocks = [(b, blk) for b in range(nb_run) for blk in range(nblk_run)]
    head(blocks[0][0], blocks[0][1], True)
    for kblk, (b, blk) in enumerate(blocks):
        if blk == 0:
            for half in range(2):
                bk, bu = bank()
                P.pe(lambda e, bk=bk, b=b, half=half: e.matmul(bk[:, :], SEL[:, b, :], GROW[:, 0, half * 512:(half + 1) * 512],
                                                               start=True, stop=True), r=["SEL", "GROW"], w=[bu])
                P.act(lambda e, bk=bk, half=half: e.activation(G1[:, half * 512:(half + 1) * 512], bk[:, :], AF.Copy),
                      r=[bu], w=["G1"])
            P.pool(lambda e: e.memset(S0[:], 0.0), w=["S0"])
            P.pool(lambda e: e.memset(S0b[:], 0.0), w=["S0b"])
        par = blk % 2
        hTc, hTn = hT[par], hT[1 - par]
        hu, hun = "hT%d" % par, "hT%d" % (1 - par)
        ubc, ubn = ub[par], ub[1 - par]
        uu, uun = "ub%d" % par, "ub%d" % (1 - par)
        t0 = blk * TB
        xbuf, xbn = xbufs[par]
        RB = role("RB", 0); KB = role("KB", 1); VB = role("VB", 2); GB = role("GB", 3)
        SGW = role("SGW", 4); AB = role("AB", 5); T1 = role("T1", 9)
        def inproj(col0, M, N0):
            bk, bu = bank()
            for kc in range(8):
                P.pe(lambda e, bk=bk, kc=kc, col0=col0, M=M, N0=N0, hTc=hTc: e.matmul(
                    bk[0:M, 0:TB + 1 - N0], w_in_bf[:, kc, col0:col0 + M], hTc[:, kc, N0:TB + 1],
                    start=(kc == 0), stop=(kc == 7)), r=["w_in_bf", hu], w=[bu])
            return bk, bu

        for c in range(4):
            bkg, bug = inproj(512 + c * 128, 128, 1)
            P.act(lambda e, bkg=bkg: e.activation(sig[:], bkg[:, 0:TB], AF.Sigmoid), r=[bug], w=["sig"])
            bkv, buv = inproj(c * 128, 128, 1)
            P.dve(lambda e, bkv=bkv, c=c, ubc=ubc: e.tensor_tensor(ubc[:, c, 30:30 + TB], bkv[:, 0:TB], sig[:], ALU.mult),
                  r=[buv, "sig"], w=[uu])
        P.pool(lambda e, ubc=ubc, ubn=ubn: e.tensor_copy(ubn[:, :, 0:30], ubc[:, :, TB:TB + 30]), r=[uu], w=[uun])

        def shift_evac(bk, bu, Mp, dst, dname, mucol, omcol):
            P.act(lambda e: e.activation(tmpA[0:Mp, :], bk[0:Mp, 1:TB + 1], AF.Identity, scale=omcol),
                  r=[bu, "OMB", "OMA"], w=["tmpA"])
            P.dve(lambda e: e.scalar_tensor_tensor(dst, bk[0:Mp, 0:TB], mucol, tmpA[0:Mp, :], ALU.mult, ALU.add),
                  r=[bu, "tmpA", "PB", "PA"], w=[dname])

        for wi, (dstT, dn) in enumerate(((RB, "RB"), (KB, "KB"), (VB, "VB"))):
            for h in range(8):
                bk, bu = inproj(1024 + wi * 512 + h * 64, 64, 0)
                shift_evac(bk, bu, 64, dstT[:, h, :], dn, PB[:, wi * 8 + h:wi * 8 + h + 1],
                           OMB[:, wi * 8 + h:wi * 8 + h + 1])
        bk, bu = inproj(2560, 64, 0)
        shift_evac(bk, bu, 64, T1[:, 0, :], "T1", PB[:, 80:81], OMB[:, 80:81])
        P.act(lambda e: e.activation(TW[:], T1[:, 0, :], AF.Tanh), r=["T1"], w=["TW"])
        bk, bu = inproj(2624, 64, 0)
        shift_evac(bk, bu, 64, T1[:, 1, :], "T1", PB[:, 81:82], OMB[:, 81:82])
        P.act(lambda e: e.activation(ADb[:], T1[:, 1, :], AF.Copy), r=["T1"], w=["ADb"])
        bk, bu = inproj(2688, 128, 0)
        shift_evac(bk, bu, 128, sq[:, 0, :], "sq", PA[:, 60:61], OMA[:, 60:61])
        P.act(lambda e: e.activation(SGD[:], sq[:, 0, :], AF.Sigmoid), r=["sq"], w=["SGD"])
        for h in range(8):
            bk, bu = bank()
            P.pe(lambda e, bk=bk, h=h: e.matmul(bk[0:64, 0:TB], w2_bf[:, h * 64:(h + 1) * 64], TW[:], start=True, stop=True),
                 r=["w2_bf", "TW"], w=[bu])
            P.act(lambda e, bk=bk, h=h: e.activation(SGW[:, h, :], bk[0:64, 0:TB], AF.Sigmoid, bias=PB[:, 24 + h:25 + h]),
                  r=[bu, "PB"], w=["SGW"])
            bk, bu = bank()
            P.pe(lambda e, bk=bk, h=h: e.matmul(bk[0:64, 0:TB], a2_bf[:, h * 64:(h + 1) * 64], ADb[:], start=True, stop=True),
                 r=["a2_bf", "ADb"], w=[bu])
            P.act(lambda e, bk=bk, h=h: e.activation(AB[:, h, :], bk[0:64, 0:TB], AF.Sigmoid, bias=PB[:, 32 + h:33 + h]),
                  r=[bu, "PB"], w=["AB"])
            bk, bu = bank()
            P.pe(lambda e, bk=bk, h=h: e.matmul(bk[0:64, 0:TB], g2_bf[:, h * 64:(h + 1) * 64], SGD[:], start=True, stop=True),
                 r=["g2_bf", "SGD"], w=[bu])
            P.act(lambda e, bk=bk, h=h: e.activation(GB[:, h, :], bk[0:64, 0:TB], AF.Copy), r=[bu], w=["GB"])

        def conv_gen(ubc=ubc, uu=uu):
            for c in range(4):
                P.dve(lambda e, c=c: e.tensor_scalar(acc[:, c, :], ubc[:, c, 0:TB], CW[:, c, 0:1], PA[:, 48 + c:49 + c],
                                                     ALU.mult, ALU.add), r=[uu, "CW", "PA"], w=[("acc", c)])
            yield
            for j in range(1, 1 if "C" in SKIP else 31):
                for c in range(4):
                    P.dve(lambda e, c=c, j=j: e.scalar_tensor_tensor(acc[:, c, :], ubc[:, c, j:j + TB], CW[:, c, j:j + 1],
                                                                     acc[:, c, :], ALU.mult, ALU.add),
                          r=[uu, "CW", ("acc", c)], w=[("acc", c)])
                yield
        cgen = conv_gen()

        def pull():
            next(cgen, None)
        CUM = role("CUM", 6); EG = role("EG", 7); IEG = role("IEG", 8); EGX = role("EGX", 10)
        fl = lambda t: t[:].rearrange("p h t -> p (h t)")
        for h in range(8):
            P.dve(lambda e, h=h: e.tensor_tensor_scan(CUM[:, h, :], rmask[:].rearrange("p c t -> p (c t)"), SGW[:, h, :], 0.0,
                                                      ALU.mult, ALU.add), r=["rmask", "SGW"], w=["CUM"])
        P.act(lambda e: e.activation(fl(EG), fl(CUM), AF.Exp, scale=-LWC), r=["CUM"], w=["EG"])
        P.act(lambda e: e.activation(fl(IEG), fl(CUM), AF.Exp, scale=LWC), r=["CUM"], w=["IEG"])
        P.pool(lambda e: e.tensor_tensor(fl(T1), fl(CUM), fl(SGW), ALU.subtract), r=["CUM", "SGW"], w=["T1"])
        P.act(lambda e: e.activation(fl(EGX), fl(T1), AF.Exp, scale=-LWC), r=["T1"], w=["EGX"])
        KKR = role("KKR", 4); SQ2 = role("SQ2", 6); KKN = role("KKN", 9)
        P.dve(lambda e: e.tensor_tensor(KKR[:], KB[:], pbc(40), ALU.mult), r=["KB", "PB"], w=["KKR"])
        P.dve(lambda e: e.tensor_tensor(SQ2[:], KKR[:], KKR[:], ALU.mult), r=["KKR"], w=["SQ2"])
        for half in range(2):
            bk, bu = bank()
            P.pe(lambda e, bk=bk, half=half: e.matmul(bk[0:64, :], ones_h[:], fl(SQ2)[:, half * 512:(half + 1) * 512],
                                                      start=True, stop=True), r=["ones_h", "SQ2"], w=[bu])
            rsqrt(fl(KKN)[:, half * 512:(half + 1) * 512], bk[0:64, :], 0.0, [bu], "KKN", floor=1e-12)
        P.dve(lambda e: e.tensor_tensor(KKN[:], KKN[:], KKR[:], ALU.mult), r=["KKN", "KKR"], w=["KKN"])
        T1 = role("T1", 4); KF = role("KF", 6)
        P.pool(lambda e: e.tensor_tensor(T1[:], AB[:], pbc(48), ALU.mult), r=["AB", "PB"], w=["T1"])
        P.pool(lambda e: e.tensor_tensor(T1[:], T1[:], ombc(48), ALU.add), r=["T1", "OMB"], w=["T1"])
        P.dve(lambda e: e.tensor_tensor(KF[:], KB[:], T1[:], ALU.mult), r=["KB", "T1"], w=["KF"])
        BBt = role("BBt", 1)
        P.dve(lambda e: e.tensor_tensor(BBt[:], KKN[:], AB[:], ALU.mult), r=["KKN", "AB"], w=["BBt"])
        P.dve(lambda e: e.tensor_tensor(RT[:], RB[:], EG[:], ALU.mult), r=["RB", "EG"], w=["RT"])
        P.pool(lambda e: e.tensor_tensor(KT[:], KF[:], IEG[:], ALU.mult), r=["KF", "IEG"], w=["KT"])
        P.dve(lambda e: e.tensor_tensor(BT[:], BBt[:], IEG[:], ALU.mult), r=["BBt", "IEG"], w=["BT"])
        P.dve(lambda e: e.scalar_tensor_tensor(ATl[:], KKN[:], -1.0, EGX[:], ALU.mult, ALU.mult),
               r=["KKN", "EGX"], w=["ATl"])
        P.act(lambda e: e.activation(fl(VBb), fl(VB), AF.Copy), r=["VB"], w=["VBb"])
        SQ2 = role("SQ2", 8); BON = role("BON", 4); YB = role("YB", 5)
        P.dve(lambda e: e.tensor_tensor(SQ2[:], RB[:], KF[:], ALU.mult), r=["RB", "KF"], w=["SQ2"])
        P.dve(lambda e: e.tensor_tensor(SQ2[:], SQ2[:], pbc(72), ALU.mult), r=["SQ2", "PB"], w=["SQ2"])
        for half in range(2):
            bk, bu = bank()
            P.pe(lambda e, bk=bk, half=half: e.matmul(bk[0:64, :], ones_h[:], fl(SQ2)[:, half * 512:(half + 1) * 512],
                                                      start=True, stop=True), r=["ones_h", "SQ2"], w=[bu])
            P.dve(lambda e, bk=bk, half=half: e.tensor_tensor(fl(BON)[:, half * 512:(half + 1) * 512], bk[0:64, :],
                                                              fl(VB)[:, half * 512:(half + 1) * 512], ALU.mult),
                  r=[bu, "VB"], w=["BON"])

        for cc in range(0 if "R" in SKIP else TB // CH):
            c0 = cc * CH
            cs = slice(c0, c0 + CH)

            def bfview(bk):
                return bk[0:64, 0:256].bitcast(BF16).rearrange("p (h t) -> p h t", h=8)

            for (srcT, sn, dstT, dn) in ((VBb, "VBb", Vt, "Vt"), (KT, "KT", Ktm, "Ktm"), (BT, "BT", Btm, "Btm")):
                bk, bu = bank()
                for h in range(8):
                    P.pe(lambda e, bk=bk, h=h, srcT=srcT: e.transpose(bfview(bk)[:, h, :], srcT[:, h, cs], identb[0:64, 0:64]),
                         r=[sn, "identb"], w=[bu])
                P.act(lambda e, bk=bk, dstT=dstT: e.activation(dstT[:], bfview(bk), AF.Copy), r=[bu], w=[dn])

            def amat(lhsT_, ln, rhs_, rn, mask, mn, dst, dn, eng):
                bk, bu = bank()
                for h in range(8):
                    P.pe(lambda e, bk=bk, h=h: e.matmul(bk[0:64, h * 64:(h + 1) * 64], lhsT_[:, h, cs], rhs_[:, h, cs],
                                                        start=True, stop=True), r=[ln, rn], w=[bu])
                eng(lambda e, bk=bk: e.tensor_tensor(dst[:], bk[0:64, :].rearrange("p (h t) -> p h t", h=8), m8(mask), ALU.mult),
                    r=[bu, mn], w=[dn])
                pull()

            amat(BT, "BT", ATl, "ATl", m_st, "m_st", NTpw[0], "NT0", P.dve)
            amat(ATl, "ATl", BT, "BT", m_lo, "m_lo", Npw[0], "N0", P.dve)
            amat(BT, "BT", RT, "RT", m_in, "m_in", ArbT, "ArbT", P.dve)
            amat(KT, "KT", ATl, "ATl", m_st, "m_st", AakT, "AakT", P.dve)
            amat(KT, "KT", RT, "RT", m_in, "m_in", ArkT, "ArkT", P.dve)

            def mm8(lhs, ln, rhs, rn, dst, dn, add=None, an=None):
                bk, bu = bank()
                for h in range(8):
                    P.pe(lambda e, bk=bk, h=h: e.matmul(bk[0:64, h * 64:(h + 1) * 64], lhs[:, h, :], rhs[:, h, :],
                                                        start=True, stop=True), r=[ln, rn], w=[bu])
                v = bk[0:64, :].rearrange("p (h t) -> p h t", h=8)
                if add is None:
                    P.act(lambda e: e.activation(dst[:], v, AF.Copy), r=[bu], w=[dn])
                else:
                    P.dve(lambda e: e.tensor_tensor(dst[:], v, add[:], ALU.add), r=[bu, an], w=[dn])
                pull()

            P.dve(lambda e: e.tensor_tensor(PT[0][:], NTpw[0][:], identb8[:], ALU.add), r=["NT0", "identb8"], w=["PT0"])
            cur = 0
            for i in range(5):
                a_, b_ = i % 2, (i + 1) % 2
                mm8(NTpw[a_], "NT%d" % a_, Npw[a_], "N%d" % a_, Npw[b_], "N%d" % b_)
                if i < 4:
                    mm8(Npw[a_], "N%d" % a_, NTpw[a_], "NT%d" % a_, NTpw[b_], "NT%d" % b_)
                mm8(Npw[b_], "N%d" % b_, PT[cur], "PT%d" % cur, PT[1 - cur], "PT%d" % (1 - cur),
                    add=PT[cur], an="PT%d" % cur)
                cur = 1 - cur
            PTf, PTn = PT[cur], "PT%d" % cur
            bk, bu = bank()
            for h in range(8):
                P.pe(lambda e, bk=bk, h=h: e.matmul(bk[0:64, h * 64:(h + 1) * 64], ATl[:, h, cs], S0b[:, h, :],
                                                    start=True, stop=False), r=["ATl", "S0b"], w=[bu])
                P.pe(lambda e, bk=bk, h=h: e.matmul(bk[0:64, h * 64:(h + 1) * 64], AakT[:, h, :], Vt[:, h, :],
                                                    start=False, stop=True), r=["AakT", "Vt"], w=[bu])
            P.act(lambda e, bk=bk: e.activation(Xb[:], bk[0:64, :].rearrange("p (h t) -> p h t", h=8), AF.Copy),
                  r=[bu], w=["Xb"])
            mm8(PTf, PTn, Xb, "Xb", Ub, "Ub")
            bk, bu = bank()
            for h in range(8):
                o_ = bk[0:64, h * 64:(h + 1) * 64]
                P.pe(lambda e, o_=o_, h=h: e.matmul(o_, S0b[:, h, :], RT[:, h, cs], start=True, stop=False),
                     r=["S0b", "RT"], w=[bu])
                P.pe(lambda e, o_=o_, h=h: e.matmul(o_, Ub[:, h, :], ArbT[:, h, :], start=False, stop=False),
                     r=["Ub", "ArbT"], w=[bu])
                P.pe(lambda e, o_=o_, h=h: e.matmul(o_, Vt[:, h, :], ArkT[:, h, :], start=False, stop=True),
                     r=["Vt", "ArkT"], w=[bu])
            P.act(lambda e, bk=bk: e.activation(YB[:, :, cs], bk[0:64, :].rearrange("p (h t) -> p h t", h=8), AF.Copy),
                  r=[bu], w=["YB"])
            bk, bu = bank()
            for h in range(8):
                o_ = bk[0:64, h * 64:(h + 1) * 64]
                P.pe(lambda e, o_=o_, h=h: e.matmul(o_, Btm[:, h, :], Ub[:, h, :], start=True, stop=False),
                     r=["Btm", "Ub"], w=[bu])
                P.pe(lambda e, o_=o_, h=h: e.matmul(o_, Ktm[:, h, :], Vt[:, h, :], start=False, stop=True),
                     r=["Ktm", "Vt"], w=[bu])
            P.dve(lambda e, bk=bk: e.tensor_tensor(Stmp[:], bk[0:64, :].rearrange("p (h t) -> p h t", h=8), S0[:], ALU.add),
                  r=[bu, "S0"], w=["Stmp"])
            P.dve(lambda e: e.tensor_tensor(S0[:], Stmp[:], EG[:, :, c0 + CH - 1:c0 + CH].broadcast_to([64, 8, 64]), ALU.mult),
                  r=["Stmp", "EG"], w=["S0"])
            P.act(lambda e: e.activation(S0b[:], S0[:], AF.Copy), r=["S0"], w=["S0b"])

        T1 = role("T1", 9); KF = role("KF", 10)
        P.pool(lambda e: e.tensor_tensor(SQ2[:], YB[:], YB[:], ALU.mult), r=["YB"], w=["SQ2"])
        for half in range(2):
            hs = slice(half * 512, (half + 1) * 512)
            bk1, bu1 = bank()
            P.pe(lambda e, bk1=bk1, hs=hs: e.matmul(bk1[0:64, :], ones_g[:], fl(YB)[:, hs], start=True, stop=True),
                 r=["ones_g", "YB"], w=[bu1])
            bk2_, bu2_ = bank()
            P.pe(lambda e, bk2_=bk2_, hs=hs: e.matmul(bk2_[0:64, :], ones_g[:], fl(SQ2)[:, hs], start=True, stop=True),
                 r=["ones_g", "SQ2"], w=[bu2_])
            P.dve(lambda e, bk1=bk1, hs=hs: e.tensor_copy(fl(T1)[:, hs], bk1[0:64, :]), r=[bu1], w=["T1"])
            P.dve(lambda e, hs=hs: e.tensor_tensor(fl(KF)[:, hs], fl(T1)[:, hs], fl(T1)[:, hs], ALU.mult), r=["T1"], w=["KF"])
            P.dve(lambda e, bk2_=bk2_, hs=hs: e.tensor_tensor(fl(KF)[:, hs], bk2_[0:64, :], fl(KF)[:, hs], ALU.subtract),
                  r=[bu2_, "KF"], w=["KF"])
        rsqrt(fl(KF), fl(KF), GN_EPS_, ["KF"], "KF")
        P.dve(lambda e: e.tensor_tensor(YB[:], YB[:], T1[:], ALU.subtract), r=["YB", "T1"], w=["YB"])
        P.dve(lambda e: e.tensor_tensor(YB[:], YB[:], KF[:], ALU.mult), r=["YB", "KF"], w=["YB"])
        P.dve(lambda e: e.tensor_tensor(YB[:], YB[:], pbc(56), ALU.mult), r=["YB", "PB"], w=["YB"])
        P.dve(lambda e: e.tensor_tensor(YB[:], YB[:], pbc(64), ALU.add), r=["YB", "PB"], w=["YB"])
        P.dve(lambda e: e.tensor_tensor(YB[:], YB[:], BON[:], ALU.add), r=["YB", "BON"], w=["YB"])
        P.dve(lambda e: e.tensor_tensor(YRW[:], YB[:], GB[:], ALU.mult), r=["YB", "GB"], w=["YRW"])

        if kblk + 1 < len(blocks):
            head(blocks[kblk + 1][0], blocks[kblk + 1][1], False)
        for _ in cgen:
            pass
        for c in range(4):
            P.act(lambda e, c=c: e.activation(sq[:, c, :], acc[:, c, :], AF.Square), r=[("acc", c)], w=["sq"])
        bkm_, bum_ = bank()
        for c in range(4):
            P.pe(lambda e, c=c, bkm_=bkm_: e.matmul(bkm_[:, 0:TB], ones_c[:], acc[:, c, :], start=(c == 0), stop=(c == 3)),
                 r=["ones_c", ("acc", c)], w=[bum_])
        bks_, bus_ = bank()
        for c in range(4):
            P.pe(lambda e, c=c, bks_=bks_: e.matmul(bks_[:, 0:TB], ones_c[:], sq[:, c, :], start=(c == 0), stop=(c == 3)),
                 r=["ones_c", "sq"], w=[bus_])
        P.dve(lambda e, bkm_=bkm_: e.tensor_copy(cm[:], bkm_[:, 0:TB]), r=[bum_], w=["cm"])
        P.dve(lambda e: e.tensor_tensor(ct[:], cm[:], cm[:], ALU.mult), r=["cm"], w=["ct"])
        P.dve(lambda e, bks_=bks_: e.tensor_tensor(cr[:], bks_[:, 0:TB], ct[:], ALU.subtract), r=[bus_, "ct"], w=["cr"])
        rsqrt(cr[:], cr[:], LN_EPS_, ["cr"], "cr")
        for c in range(4):
            P.pool(lambda e, c=c: e.tensor_tensor(sq[:, c, :], acc[:, c, :], cm[:], ALU.subtract),
                   r=[("acc", c), "cm", "sq"], w=["sq"])
            P.pool(lambda e, c=c: e.tensor_tensor(sq[:, c, :], sq[:, c, :], cr[:], ALU.mult), r=["sq", "cr"], w=["sq"])
            P.act(lambda e, c=c: e.activation(ycT[:, c, :], sq[:, c, :], AF.Silu, bias=PA[:, 56 + c:57 + c],
                                              scale=PA[:, 52 + c:53 + c]), r=["sq", "PA"], w=["ycT"])

        for half in range(2):
            hs = slice(half * 512, (half + 1) * 512)
            bk, bu = bank()
            for c in range(4):
                P.pe(lambda e, bk=bk, c=c, hs=hs: e.matmul(bk[:, :], ycT[:, c, :], w_oc_bf[:, c, hs], start=(c == 0), stop=False),
                     r=["ycT", "w_oc_bf"], w=[bu])
            for h in range(8):
                P.pe(lambda e, bk=bk, h=h, hs=hs: e.matmul(bk[:, :], YRW[:, h, :], w_or_bf[:, h, hs], start=False, stop=(h == 7)),
                     r=["YRW", "w_or_bf"], w=[bu])
            P.dve(lambda e, bk=bk, hs=hs: e.tensor_tensor(xo[:, hs], bk[:, :], G1[:, hs], ALU.mult), r=[bu, "G1"], w=["bigB"])
        P.dve(lambda e: e.scalar_tensor_tensor(xo, xbuf[:], ALPHA_, xo, ALU.mult, ALU.add), r=[xbn, "bigB"], w=["bigB"])
        for hh in range(2):
            P.dve(lambda e, hh=hh: e.bn_stats(st6[:, hh, :], xo[:, hh * 512:(hh + 1) * 512]), r=["bigB"], w=["st6"])
        P.dve(lambda e: e.bn_aggr(mv[:], st6[:].rearrange("p a b -> p (a b)")), r=["st6"], w=["mv"])
        rsqrt(rstd[:], mv[:, 1:2], LN_EPS_, ["mv"], "rstd")
        P.dve(lambda e: e.tensor_scalar(xo, xo, mv[:, 0:1], rstd[:, 0:1], ALU.subtract, ALU.mult),
              r=["bigB", "mv", "rstd"], w=["bigB"])
        P.dve(lambda e: e.tensor_tensor(xo, xo, LG1[:], ALU.mult), r=["bigB", "LG1"], w=["bigB"])
        P.dve(lambda e: e.tensor_tensor(xo, xo, LB1[:], ALU.add), r=["bigB", "LB1"], w=["bigB"])
        o = P.dma("sp", lambda e, b=b, t0=t0: e.dma_start(out=x1_d[b, t0:t0 + TB, :], in_=xo), r=["bigB"], w=[("x1d", b, blk)], lane="xn_out")
        out_ops.append(o)


    if stage == "A":
        P.finish(final_ops=out_ops[-1:])
        return

    P.pop()
    P.push()
    NT = 256
    NST = SEQ // NT
    NBUF = 3
    ut_scr = nc.dram_tensor("ut_scr", [128, 128, 8, 128], BF16).ap()
    v_scr = nc.dram_tensor("v_scr", [128, 128, D], BF16).ap()

    ust = [P.sb("ust%d" % i, [128, D], F32) for i in range(2)]
    vst = [P.sb("vst%d" % i, [128, D], F32) for i in range(2)]
    utb = [P.sb("utb%d" % i, [128, 8, 128], BF16) for i in range(2)]
    vbb = [P.sb("vbb%d" % i, [128, D], BF16) for i in range(2)]
    for c in range(128):
        i = c % 2
        P.dma("sp", lambda e, c=c, i=i: e.dma_start(out=ust[i][:], in_=pu_d[c * 128:(c + 1) * 128, :]), w=["ust%d" % i], lane="ust%d" % i)
        P.dma("sp", lambda e, c=c, i=i: e.dma_start(out=vst[i][:], in_=pv_d[c * 128:(c + 1) * 128, :]), w=["vst%d" % i], lane="vst%d" % i)
        for half in range(2):
            bk, bu = bank()
            for q in range(4):
                dc = half * 4 + q
                P.pe(lambda e, bk=bk, q=q, dc=dc, i=i: e.transpose(bk[:, q * 128:(q + 1) * 128], ust[i][:, dc * 128:(dc + 1) * 128], ident[:, :]),
                     r=["ust%d" % i, "ident"], w=[bu])
            if half == 0:
                P.act(lambda e, bk=bk, i=i: e.activation(utb[i][:, 0:4, :], bk[:, :].rearrange("p (q e) -> p q e", q=4), AF.Copy),
                      r=[bu], w=["utb%d" % i])
            else:
                P.dve(lambda e, bk=bk, i=i: e.tensor_copy(utb[i][:, 4:8, :], bk[:, :].rearrange("p (q e) -> p q e", q=4)),
                      r=[bu], w=["utb%d" % i])
        P.pool(lambda e, i=i: e.tensor_copy(vbb[i][:], vst[i][:]), r=["vst%d" % i], w=["vbb%d" % i])
        P.dma("sp", lambda e, c=c, i=i: e.dma_start(out=ut_scr[c], in_=utb[i][:]), r=["utb%d" % i], w=[("UTd", c)], lane="utb%d" % i)
        P.dma("sp", lambda e, c=c, i=i: e.dma_start(out=v_scr[c], in_=vbb[i][:]), r=["vbb%d" % i], w=[("Vd", c)], lane="vbb%d" % i)
    P.pop()
    P.push()

    wq_bf = P.sb("wq_bf", [128, 8, 2048], BF16)
    for kc in range(8):
        P.dma("sp", lambda e, kc=kc: e.dma_start(out=big[:, :], in_=wq_d[kc * 128:(kc + 1) * 128, :]), w=["big", "bigA", "bigB"], lane="big")
        P.act(lambda e, kc=kc: e.activation(wq_bf[:, kc, :], big[:, :], AF.Copy), r=["big"], w=["wq_bf"])
    K12 = P.sb("K12", [128, 16, 128], BF16)
    for s_, kd in enumerate((k1_d, k2_d)):
        for h in range(8):
            P.dma("sp", lambda e, kd=kd, h=h: e.dma_start(out=big[:, 0:128], in_=kd[h]), w=["big"], lane="big")
            bk, bu = bank()
            P.pe(lambda e, bk=bk: e.transpose(bk[:, 0:128], big[:, 0:128], ident[:, :]), r=["big", "ident"], w=[bu])
            P.dve(lambda e, bk=bk, s_=s_, h=h: e.tensor_copy(K12[:, s_ * 8 + h, :], bk[:, 0:128]), r=[bu], w=["K12"])
    P.dma("sp", lambda e: e.dma_start(out=LG1[:], in_=ln2g_d.partition_broadcast(128)), w=["LG1"], lane="LG1")
    P.dma("sp", lambda e: e.dma_start(out=LB1[:], in_=ln2b_d.partition_broadcast(128)), w=["LB1"], lane="LB1")
    for half in range(2):
        bk, bu = bank()
        for c in range(4):
            P.pe(lambda e, bk=bk, c=c, half=half: e.transpose(bk[0:4, c * 128:(c + 1) * 128], MOD[:, 40 + half * 4 + c, :], ident[:, :]),
                 r=["MOD", "ident"], w=[bu])
        P.dve(lambda e, bk=bk, half=half: e.tensor_copy(GROW[:, 0, half * 512:(half + 1) * 512], bk[0:4, :]), r=[bu], w=["GROW"])

    G2 = P.sb("G2", [128, D], F32)
    xs = P.sb("xs", [128, 2, D], F32)
    xn2 = P.sb("xn2", [128, D], F32)
    h2T = [P.sb("h2T%d" % i, [128, 8, NT], BF16) for i in range(2)]
    qT = P.sb("qT", [128, 16, NT], BF16)
    SC = P.sb("SC", [128, 16, 128], F32)
    SCm = P.sb("SCm", [128, 256], F32)
    TV = P.sb("TV", [128, 16, 16], F32)
    TI = P.sb("TI", [128, 16, 16], U32)
    TIf = P.sb("TIf", [128, 16, 16], F32)
    CAND = SC[:].rearrange("p a b -> p (a b)").rearrange("p (h c) -> p h c", h=8)
    SV = P.sb("SV", [128, 8, 16], F32)
    CI = P.sb("CI", [128, 8, 16], U32)
    CIf = P.sb("CIf", [128, 8, 16], F32)
    JS = P.sb("JS", [128, 8, 16], F32)
    IS = P.sb("IS", [128, 8, 16], F32)
    EQ = P.sb("EQ", [128, 8, 16, 16], BF16)
    ASEL = P.sb("ASEL", [128, 8, 16], F32)
    BSEL = P.sb("BSEL", [128, 8, 16], F32)
    GATE = P.sb("GATE", [128, 8, 16], F32)
    ssum = P.sb("ssum", [128, 8], F32)
    ATt = [P.sb("ATt%d" % i, [128, 128], F32) for i in range(2)]
    BTt = [P.sb("BTt%d" % i, [128, 128], F32) for i in range(2)]
    GTt = [P.sb("GTt%d" % i, [128, 128], F32) for i in range(2)]
    iota16 = P.sb("iota16", [128, 16], F32)
    iota3 = P.sb("iota3", [128, 4, 128], BF16)
    thr16 = P.sb("thr16", [128, 16], F32)
    P.pool(lambda e: e.iota(thr16[:], [[16, 16]], base=16, channel_multiplier=0, allow_small_or_imprecise_dtypes=True), w=["thr16"])
    P.pool(lambda e: e.iota(iota16[:], [[1, 16]], base=0, channel_multiplier=0, allow_small_or_imprecise_dtypes=True), w=["iota16"])
    P.pool(lambda e: e.iota(iota3[:], [[0, 4], [1, 128]], base=0, channel_multiplier=0, allow_small_or_imprecise_dtypes=True), w=["iota3"])
    OA = [P.sb("OA%d" % i, [128, 4, 128], BF16) for i in range(2)]
    OB = [P.sb("OB%d" % i, [128, 4, 128], BF16) for i in range(2)]
    GG = P.sb("GG", [128, 128, NT], BF16)
    UTb = [P.sb("UTb%d" % i, [128, 8, 128], BF16) for i in range(NBUF)]
    Vb = [P.sb("Vb%d" % i, [128, D], BF16) for i in range(NBUF)]
    gz = [P.sb("gz%d" % i, [128, NT], F32) for i in range(2)]
    actT = [P.sb("actT%d" % i, [128, NT], BF16) for i in range(2)]

    nrot[0] = 4
    nst_run = nblk_run * TB // NT
    total_g = nb_run * nst_run * 128

    def issue_load(g):
        c = g % 128
        i = g % NBUF
        P.dma("sp", lambda e: e.dma_start(out=UTb[i][:], in_=ut_scr[c]), r=[("UTd", c)], w=["UTb%d" % i], lane="UTb%d" % i)
        P.dma("act", lambda e: e.dma_start(out=Vb[i][:], in_=v_scr[c]), r=[("Vd", c)], w=["Vb%d" % i], lane="Vb%d" % i)

    gctr = 0
    for g in range(min(NBUF, total_g)):
        issue_load(g)

    ybanks = [(banks[4 + k], ("B", 4 + k)) for k in range(4)]
    sts = [(b, st) for b in range(nb_run) for st in range(nst_run)]

    def prep1(b, st, hb):
        h2 = h2T[hb]
        hn = "h2T%d" % hb
        t0 = st * NT
        for j in range(2):
            tl = t0 // TB + j
            P.dma("sp", lambda e, j=j, tl=tl, b=b: e.dma_start(out=xs[:, j, :], in_=x1_d[b, tl * TB:(tl + 1) * TB, :]),
                  r=[("x1d", b, tl)], w=[("xs", j)], lane=("xs", j))
            for hh in range(2):
                P.dve(lambda e, hh=hh, j=j: e.bn_stats(st6[:, hh, :], xs[:, j, hh * 512:(hh + 1) * 512]), r=[("xs", j)], w=["st6"])
            P.dve(lambda e: e.bn_aggr(mv[:], st6[:].rearrange("p a b -> p (a b)")), r=["st6"], w=["mv"])
            rsqrt(rstd[:], mv[:, 1:2], LN_EPS_, ["mv"], "rstd")
            P.dve(lambda e, j=j: e.tensor_scalar(xn2[:], xs[:, j, :], mv[:, 0:1], rstd[:, 0:1], ALU.subtract, ALU.mult),
                  r=[("xs", j), "mv", "rstd"], w=["xn2"])
            yield
            for half in range(2):
                bk, bu = bank()
                for q in range(4):
                    fc = half * 4 + q
                    P.pe(lambda e, bk=bk, q=q, fc=fc: e.transpose(bk[:, q * 128:(q + 1) * 128], xn2[:, fc * 128:(fc + 1) * 128], ident[:, :]),
                         r=["xn2", "ident"], w=[bu])
                for q in range(4):
                    fc = half * 4 + q
                    P.act(lambda e, bk=bk, q=q, fc=fc, b=b, j=j: e.activation(
                        h2[:, fc, j * 128:(j + 1) * 128], bk[:, q * 128:(q + 1) * 128], AF.Identity,
                        bias=MOD[:, 24 + fc, b:b + 1], scale=MOD[:, 32 + fc, b:b + 1]), r=[bu, "MOD"], w=[hn])
                yield
        for m in range(16):
            bk, bu = bank()
            for kc in range(8):
                P.pe(lambda e, bk=bk, kc=kc, m=m: e.matmul(bk[:, 0:NT], wq_bf[:, kc, m * 128:(m + 1) * 128], h2[:, kc, :],
                                                           start=(kc == 0), stop=(kc == 7)), r=["wq_bf", hn], w=[bu])
            if m % 2 == 0:
                P.act(lambda e, bk=bk, m=m: e.activation(qT[:, m, :], bk[:, 0:NT], AF.Copy), r=[bu], w=["qT"])
            else:
                P.dve(lambda e, bk=bk, m=m: e.tensor_copy(qT[:, m, :], bk[:, 0:NT]), r=[bu], w=["qT"])
            yield
        for j in range(2):
            js = slice(j * 128, (j + 1) * 128)
            for grp in range(4):
                bk, bu = bank()
                for q in range(4):
                    hs_ = grp * 4 + q
                    h, s_ = hs_ // 2, hs_ % 2
                    P.pe(lambda e, bk=bk, q=q, hs_=hs_, h=h, s_=s_: e.matmul(bk[:, q * 128:(q + 1) * 128], qT[:, hs_, js], K12[:, s_ * 8 + h, :],
                                                                           start=True, stop=True), r=["qT", "K12"], w=[bu])
                P.act(lambda e, bk=bk, grp=grp: e.activation(SC[:, grp * 4:(grp + 1) * 4, :], bk[:, :].rearrange("p (q n) -> p q n", q=4), AF.Copy),
                      r=[bu], w=["SC"])
                yield
            for hs_ in range(16):
                P.dve(lambda e, hs_=hs_: e.max(TV[:, hs_, 0:8], SC[:, hs_, :]), r=["SC"], w=["TV"])
                P.dve(lambda e, hs_=hs_: e.max_index(TI[:, hs_, 0:8], TV[:, hs_, 0:8], SC[:, hs_, :]), r=["SC", "TV"], w=["TI"])
                P.dve(lambda e, hs_=hs_: e.match_replace(SCm[:, 0:128], TV[:, hs_, 0:8], SC[:, hs_, :], -1e30), r=["SC", "TV"], w=["SCm"])
                yield
                P.dve(lambda e, hs_=hs_: e.max(TV[:, hs_, 8:16], SCm[:, 0:128]), r=["SCm"], w=["TV"])
                P.dve(lambda e, hs_=hs_: e.max_index(TI[:, hs_, 8:16], TV[:, hs_, 8:16], SCm[:, 0:128]), r=["SCm", "TV"], w=["TI"])
                yield
            P.dve(lambda e: e.tensor_copy(TIf[:], TI[:]), r=["TI"], w=["TIf"])
            TV4 = TV[:].rearrange("p (h s) k -> p h s k", s=2)
            TI4 = TIf[:].rearrange("p (h s) k -> p h s k", s=2)
            P.dve(lambda e: e.tensor_tensor(CAND.rearrange("p h (i j) -> p h i j", i=16),
                                            TV4[:, :, 0, :].unsqueeze(3).broadcast_to([128, 8, 16, 16]),
                                            TV4[:, :, 1, :].unsqueeze(2).broadcast_to([128, 8, 16, 16]), ALU.add), r=["TV"], w=["SC"])
            yield
            for h in range(8):
                P.dve(lambda e, h=h: e.max(SV[:, h, 0:8], CAND[:, h, :]), r=["SC"], w=["SV"])
                P.dve(lambda e, h=h: e.max_index(CI[:, h, 0:8], SV[:, h, 0:8], CAND[:, h, :]), r=["SC", "SV"], w=["CI"])
                P.dve(lambda e, h=h: e.match_replace(SCm[:, :], SV[:, h, 0:8], CAND[:, h, :], -1e30), r=["SC", "SV"], w=["SCm"])
                yield
                P.dve(lambda e, h=h: e.max(SV[:, h, 8:16], SCm[:, :]), r=["SCm"], w=["SV"])
                P.dve(lambda e, h=h: e.max_index(CI[:, h, 8:16], SV[:, h, 8:16], SCm[:, :]), r=["SCm", "SV"], w=["CI"])
                yield
            P.dve(lambda e: e.tensor_tensor(GATE[:], SV[:], SV[:, :, 0:1].broadcast_to([128, 8, 16]), ALU.subtract), r=["SV"], w=["GATE"])
            P.act(lambda e: e.activation(GATE[:], GATE[:], AF.Exp), r=["GATE"], w=["GATE"])
            P.dve(lambda e: e.tensor_reduce(ssum[:], GATE[:], AX.X, ALU.add), r=["GATE"], w=["ssum"])
            P.dve(lambda e: e.reciprocal(ssum[:], ssum[:]), r=["ssum"], w=["ssum"])
            P.dve(lambda e: e.tensor_tensor(GATE[:], GATE[:], ssum[:].unsqueeze(2).broadcast_to([128, 8, 16]), ALU.mult), r=["GATE", "ssum"], w=["GATE"])
            yield
            P.dve(lambda e: e.tensor_copy(CIf[:], CI[:]), r=["CI"], w=["CIf"])
            P.dve(lambda e: e.tensor_tensor(EQ[:], CIf[:].unsqueeze(3).broadcast_to([128, 8, 16, 16]),
                                            thr16[:, :].unsqueeze(1).unsqueeze(1).broadcast_to([128, 8, 16, 16]), ALU.is_ge), r=["CIf", "thr16"], w=["EQ"])
            P.dve(lambda e: e.tensor_reduce(IS[:], EQ[:], AX.X, ALU.add), r=["EQ"], w=["IS"])
            P.dve(lambda e: e.scalar_tensor_tensor(JS[:], IS[:], -16.0, CIf[:], ALU.mult, ALU.add), r=["IS", "CIf"], w=["JS"])
            yield
            io4 = iota16[:, :].unsqueeze(1).unsqueeze(1).broadcast_to([128, 8, 16, 16])
            for (selT, sn, s_, dstT, dn) in ((IS, "IS", 0, ASEL, "ASEL"), (JS, "JS", 1, BSEL, "BSEL")):
                P.dve(lambda e, selT=selT: e.tensor_tensor(EQ[:], selT[:].unsqueeze(3).broadcast_to([128, 8, 16, 16]), io4, ALU.is_equal),
                      r=[sn, "iota16"], w=["EQ"])
                P.pool(lambda e, s_=s_: e.tensor_tensor(EQ[:], EQ[:], TI4[:, :, s_, :].unsqueeze(2).broadcast_to([128, 8, 16, 16]), ALU.mult),
                       r=["EQ", "TIf"], w=["EQ"])
                P.dve(lambda e, dstT=dstT: e.tensor_reduce(dstT[:], EQ[:], AX.X, ALU.add), r=["EQ"], w=[dn])
                yield
            for (srcT, sn, dstT, dn) in ((ASEL, "ASEL", ATt[j], "ATt%d" % j), (BSEL, "BSEL", BTt[j], "BTt%d" % j), (GATE, "GATE", GTt[j], "GTt%d" % j)):
                bk, bu = bank()
                P.pe(lambda e, bk=bk, srcT=srcT: e.transpose(bk[:, 0:128], srcT[:].rearrange("p h k -> p (h k)"), ident[:, :]),
                     r=[sn, "ident"], w=[bu])
                P.act(lambda e, bk=bk, dstT=dstT: e.activation(dstT[:], bk[:, 0:128], AF.Copy), r=[bu], w=[dn])
            yield

    def ggbuild():
        for j in range(2):
            for tg in range(0 if "G" in SKIP else 32):
                i = tg % 2
                ts_ = slice(tg * 4, tg * 4 + 4)
                for q in range(4):
                    tt_ = tg * 4 + q
                    P.dve(lambda e, q=q, tt_=tt_, i=i, j=j: e.tensor_scalar(OA[i][:, q, :], iota3[:, 0, :], ATt[j][:, tt_:tt_ + 1], GTt[j][:, tt_:tt_ + 1],
                                                                          ALU.is_equal, ALU.mult),
                          r=["iota3", "ATt%d" % j, "GTt%d" % j], w=[("OA", i, q)])
                P.dve(lambda e, ts_=ts_, i=i, j=j: e.tensor_tensor(OB[i][:], iota3[:], BTt[j][:, ts_].unsqueeze(2).broadcast_to([128, 4, 128]), ALU.is_equal),
                      r=["iota3", "BTt%d" % j], w=["OB%d" % i])
                bk, bu = bank()
                for q in range(4):
                    P.pe(lambda e, bk=bk, q=q, i=i: e.matmul(bk[:, q * 128:(q + 1) * 128], OB[i][:, q, :], OA[i][:, q, :],
                                                            start=True, stop=True), r=[("OA", i, q), "OB%d" % i], w=[bu])
                tb_ = j * 128 + tg * 4
                P.act(lambda e, bk=bk, tb_=tb_: e.activation(GG[:, :, tb_:tb_ + 4].rearrange("p i t -> p t i"),
                                                             bk[:, :].rearrange("p (t i) -> p t i", t=4), AF.Copy), r=[bu], w=["GG"])

    def zstage(c, hb):
        nonlocal gctr
        g = gctr
        gctr += 1
        i = g % NBUF
        pz = c % 2
        bk, bu = bank()
        for dc in range(8):
            P.pe(lambda e, bk=bk, dc=dc, i=i: e.matmul(bk[:, 0:NT], UTb[i][:, dc, :], h2T[hb][:, dc, :], start=(dc == 0), stop=(dc == 7)),
                 r=["UTb%d" % i, "h2T%d" % hb], w=[bu])
        P.act(lambda e, bk=bk, pz=pz: e.activation(gz[pz][:], bk[:, 0:NT], AF.Gelu), r=[bu], w=["gz%d" % pz])
        eng = P.dve if (c % 2 == 0 or "D" in MODE) else P.pool
        eng(lambda e, pz=pz, c=c: e.tensor_tensor(actT[pz][:], gz[pz][:], GG[:, c, :], ALU.mult), r=["gz%d" % pz, "GG"], w=["actT%d" % pz])
        return i, pz, g

    def ystage(c, i, pz, g):
        for j in range(2):
            for half in range(2):
                yb, yu = ybanks[j * 2 + half]
                P.pe(lambda e, yb=yb, j=j, half=half, pz=pz, i=i, c=c: e.matmul(
                    yb[:, :], actT[pz][:, j * 128:(j + 1) * 128], Vb[i][:, half * 512:(half + 1) * 512],
                    start=(c == 0), stop=(c == 127)), r=["actT%d" % pz, "Vb%d" % i], w=[yu])
        if g + NBUF < total_g:
            issue_load(g + NBUF)

    def final(b, st):
        t0 = st * NT
        if st == 0:
            for half in range(2):
                bk, bu = bank()
                P.pe(lambda e, bk=bk, b=b, half=half: e.matmul(bk[:, :], SEL[:, b, :], GROW[:, 0, half * 512:(half + 1) * 512],
                                                               start=True, stop=True), r=["SEL", "GROW"], w=[bu])
                P.act(lambda e, bk=bk, half=half: e.activation(G2[:, half * 512:(half + 1) * 512], bk[:, :], AF.Copy), r=[bu], w=["G2"])
        xre = big[:, 0:D]
        for j in range(2):
            tl = t0 // TB + j
            P.dma("sp", lambda e, tl=tl, b=b: e.dma_start(out=xre, in_=x1_d[b, tl * TB:(tl + 1) * TB, :]),
                  r=[("x1d", b, tl)], w=["big"], lane="big")
            for half in range(2):
                yb, yu = ybanks[j * 2 + half]
                hs = slice(half * 512, (half + 1) * 512)
                P.dve(lambda e, yb=yb, hs=hs: e.tensor_tensor(xn2[:, hs], yb[:, :], G2[:, hs], ALU.mult), r=[yu, "G2"], w=["xn2"])
            P.dve(lambda e: e.scalar_tensor_tensor(xn2[:], xre, ALPHA_, xn2[:], ALU.mult, ALU.add), r=["big", "xn2"], w=["xn2"])
            for hh in range(2):
                P.dve(lambda e, hh=hh: e.bn_stats(st6[:, hh, :], xn2[:, hh * 512:(hh + 1) * 512]), r=["xn2"], w=["st6"])
            P.dve(lambda e: e.bn_aggr(mv[:], st6[:].rearrange("p a b -> p (a b)")), r=["st6"], w=["mv"])
            rsqrt(rstd[:], mv[:, 1:2], LN_EPS_, ["mv"], "rstd")
            P.dve(lambda e: e.tensor_scalar(xn2[:], xn2[:], mv[:, 0:1], rstd[:, 0:1], ALU.subtract, ALU.mult), r=["xn2", "mv", "rstd"], w=["xn2"])
            P.dve(lambda e: e.tensor_tensor(xn2[:], xn2[:], LG1[:], ALU.mult), r=["xn2", "LG1"], w=["xn2"])
            P.dve(lambda e: e.tensor_tensor(xn2[:], xn2[:], LB1[:], ALU.add), r=["xn2", "LB1"], w=["xn2"])
            o = P.dma("sp", lambda e, b=b, tl=tl: e.dma_start(out=out_d[b, tl * TB:(tl + 1) * TB, :], in_=xn2[:]),
                      r=["xn2"], w=[("x1d", b, tl)], lane="xn2_out")
            out_ops.append(o)

    for _ in prep1(sts[0][0], sts[0][1], 0):
        pass
    nch = 0 if "E" in SKIP else 128
    for k, (b, st) in enumerate(sts):
        hb = k % 2
        ggbuild()
        gen = prep1(sts[k + 1][0], sts[k + 1][1], 1 - hb) if k + 1 < len(sts) else iter(())
        pend = None
        for c in range(nch):
            cur = zstage(c, hb)
            if pend is not None:
                ystage(c - 1, *pend)
            pend = cur
            if "S" in MODE:
                next(gen, None)
                if c % 6 == 0:
                    next(gen, None)
            elif c >= 2:
                next(gen, None)
                next(gen, None)
        if pend is not None:
            ystage(nch - 1, *pend)
        for _ in gen:
            pass
        final(b, st)
    P.finish(final_ops=out_ops[-1:])
    return nc, P


_NAMES = ["x", "c", "cond_w", "cond_b", "w_in", "mu_shift", "conv_w", "conv_b", "conv_ln_g", "conv_ln_b",
          "rw_w0", "rw_w2", "rw_a0", "rw_a2", "rw_g2", "rw_kk", "rw_ka", "rw_rk", "rw_lnx_g", "rw_lnx_b",
          "w_out", "ln1_g", "ln1_b", "peer_wq", "peer_k1", "peer_k2", "peer_u", "peer_v", "ln2_g", "ln2_b"]


def kernel(**inputs):
    from concourse.bass_utils import run_bass_kernel_spmd
    nc, P = build()
    in_maps = []
    for i in range(8):
        m = {}
        for k in _NAMES:
            v = np.ascontiguousarray(np.asarray(inputs[k], dtype=np.float32))
            if k in ("x", "c"):
                v = np.ascontiguousarray(v[i * NB:(i + 1) * NB])
            m[k] = v
        in_maps.append(m)
    res = run_bass_kernel_spmd(nc, in_maps, core_ids=list(range(8)))
    return np.concatenate([np.asarray(r["out"]) for r in res.results], axis=0).astype(np.float32)
```
